# Optimizing a Trainium2 kernel written in Bass

```python
import math
import jax, jax.numpy as jnp
from jax import lax
import numpy as np

D_MODEL = 2048
BATCH = 1
SEQ = 8192
DEPTH = 2

HEAD_DIM = 128
N_MIX_HEADS = D_MODEL // HEAD_DIM
DIFF_HEADS = N_MIX_HEADS // 4
MOBA_HEADS = (N_MIX_HEADS - DIFF_HEADS) // 2
DSA_HEADS = N_MIX_HEADS - DIFF_HEADS - MOBA_HEADS
MOBA_BLOCK = 256
MOBA_TOPK = 3
MOBA_Q_BLOCK = 64
DIFF_QK_DIM = HEAD_DIM // 2
DIFF_V_DIM = HEAD_DIM
DSA_TOPK = 256
IDX_HEADS = 8
IDX_DIM = 64
Q_BLOCK = 128
MEM_LEN = 256
CROSS_HEADS = 4
D_FF = 5504
CONV_WIDTH = 3
ROPE_THETA = 10000.0
RMS_EPS = 1e-6

MOBA_W = MOBA_HEADS * HEAD_DIM
DIFF_QK_W = DIFF_HEADS * 2 * DIFF_QK_DIM
DIFF_V_W = DIFF_HEADS * DIFF_V_DIM
DSA_W = DSA_HEADS * HEAD_DIM
IDX_Q_W = IDX_HEADS * IDX_DIM
D_MIX = MOBA_W + DIFF_V_W + DSA_W
IN_SPLITS = (MOBA_W, MOBA_W, MOBA_W, DIFF_QK_W, DIFF_QK_W, DIFF_V_W,
             DSA_W, DSA_W, DSA_W, IDX_Q_W, IDX_DIM, IDX_HEADS)
D_IN = sum(IN_SPLITS)
CROSS_W = CROSS_HEADS * HEAD_DIM

kernel_name = 'hymba_style_moba_diff_dsa_hybrid'

F32 = jnp.float32


def rms_norm(x, g):
    xf = x.astype(F32)
    y = xf * lax.rsqrt(jnp.mean(xf * xf, axis=-1, keepdims=True) + RMS_EPS)
    return (y * g.astype(F32)).astype(x.dtype)


def rope(x, pos):
    d = x.shape[-1]
    half = d // 2
    inv = ROPE_THETA ** (-(jnp.arange(half, dtype=F32) * 2.0 / d))
    ang = pos.astype(F32)[..., None] * inv
    cos = jnp.cos(ang)[:, :, None, :]
    sin = jnp.sin(ang)[:, :, None, :]
    xf = x.astype(F32)
    x1, x2 = xf[..., :half], xf[..., half:]
    return jnp.concatenate([x1 * cos - x2 * sin, x2 * cos + x1 * sin], axis=-1).astype(x.dtype)


def _split_columns(proj):
    offs, acc = [], 0
    for w in IN_SPLITS[:-1]:
        acc += w
        offs.append(acc)
    return jnp.split(proj, offs, axis=-1)


def _query_block_sweep(fn, block, *arrays):
    b, s = arrays[0].shape[:2]
    n = s // block
    blocked = tuple(a.reshape(b, n, block, *a.shape[2:]).swapaxes(0, 1) for a in arrays)
    out = lax.map(lambda args: fn(args[0], *args[1:]), (jnp.arange(n, dtype=jnp.int32),) + blocked)
    return out.swapaxes(0, 1).reshape(b, s, *out.shape[3:])


def moba_attention(q, k, v):
    b, s, h, d = q.shape
    nb = -(-s // MOBA_BLOCK)
    pad = nb * MOBA_BLOCK - s
    kp = jnp.pad(k, ((0, 0), (0, pad), (0, 0), (0, 0)))
    vp = jnp.pad(v, ((0, 0), (0, pad), (0, 0), (0, 0)))
    kb = kp.reshape(b, nb, MOBA_BLOCK, h, d)
    cnt = jnp.clip(s - jnp.arange(nb) * MOBA_BLOCK, 1, MOBA_BLOCK).astype(F32)
    kbar = kb.astype(F32).sum(axis=2) / cnt[None, :, None, None]
    kb_t = kb.transpose(0, 3, 1, 2, 4)
    vb_t = vp.reshape(b, nb, MOBA_BLOCK, h, d).transpose(0, 3, 1, 2, 4)
    n_sel = max(1, min(MOBA_TOPK, nb - 1))
    n_s = n_sel * MOBA_BLOCK
    scale = d ** -0.5
    bi = jnp.arange(b)[:, None, None, None]
    hi = jnp.arange(h)[None, None, :, None]

    def step(qi, qblk):
        t = qi * MOBA_Q_BLOCK + jnp.arange(MOBA_Q_BLOCK)
        cur = (qi * MOBA_Q_BLOCK) // MOBA_BLOCK
        qf = qblk.astype(F32)
        gate = jnp.einsum('bqhd,bnhd->bqhn', qf, kbar)
        gate = jnp.where(jnp.arange(nb) < cur, gate, -jnp.inf)
        _, sel = lax.top_k(gate, n_sel)
        sel_ok = sel < cur
        ks = kb_t[bi, hi, sel].astype(F32)
        vs = vb_t[bi, hi, sel].astype(F32)
        s_sel = jnp.einsum('bqhd,bqhnkd->bqhnk', qf, ks) * scale
        s_sel = jnp.where(sel_ok[..., None], s_sel, -jnp.inf).reshape(b, MOBA_Q_BLOCK, h, n_s)
        k_own = lax.dynamic_slice_in_dim(kp, cur * MOBA_BLOCK, MOBA_BLOCK, axis=1).astype(F32)
        v_own = lax.dynamic_slice_in_dim(vp, cur * MOBA_BLOCK, MOBA_BLOCK, axis=1).astype(F32)
        s_own = jnp.einsum('bqhd,bkhd->bqhk', qf, k_own) * scale
        causal = (cur * MOBA_BLOCK + jnp.arange(MOBA_BLOCK))[None, :] <= t[:, None]
        s_own = jnp.where(causal[None, :, None, :], s_own, -jnp.inf)
        p = jax.nn.softmax(jnp.concatenate([s_sel, s_own], axis=-1), axis=-1)
        p_sel = p[..., :n_s].reshape(b, MOBA_Q_BLOCK, h, n_sel, MOBA_BLOCK)
        o = (jnp.einsum('bqhnk,bqhnkd->bqhd', p_sel, vs)
             + jnp.einsum('bqhk,bkhd->bqhd', p[..., n_s:], v_own))
        return o.astype(qblk.dtype)

    return _query_block_sweep(step, MOBA_Q_BLOCK, q)


def diff_attention(q, k, v, lam):
    s = q.shape[1]
    scale = q.shape[-1] ** -0.5
    kf = k.astype(F32)
    vf = v.astype(F32)
    kpos = jnp.arange(s)

    def step(qi, qblk):
        t = qi * Q_BLOCK + jnp.arange(Q_BLOCK)
        sc = jnp.einsum('bqhcd,bshcd->bchqs', qblk.astype(F32), kf) * scale
        causal = kpos[None, :] <= t[:, None]
        sc = jnp.where(causal[None, None, None], sc, -jnp.inf)
        p = jax.nn.softmax(sc, axis=-1)
        a = p[:, 0] - lam * p[:, 1]
        return jnp.einsum('bhqs,bshd->bqhd', a, vf).astype(v.dtype)

    return _query_block_sweep(step, Q_BLOCK, q)


def dsa_attention(q, k, v, q_idx, k_idx, w_idx):
    b, s, h, d = q.shape
    n_keep = min(DSA_TOPK, s // 4)
    scale = d ** -0.5
    idx_scale = q_idx.shape[-1] ** -0.5
    kidx_f = k_idx.astype(F32)
    kpos = jnp.arange(s)
    bi = jnp.arange(b)[:, None, None]

    def step(qi, qblk, qiblk, wblk):
        t = qi * Q_BLOCK + jnp.arange(Q_BLOCK)
        rel = jax.nn.relu(jnp.einsum('bqhd,bsd->bqhs', qiblk.astype(F32), kidx_f) * idx_scale)
        score = jnp.einsum('bqhs,bqh->bqs', rel, wblk.astype(F32))
        admissible = kpos[None, :] <= t[:, None]
        score = jnp.where(admissible[None], score, -jnp.inf)
        _, sel = lax.top_k(score, n_keep)
        sel_ok = sel <= t[None, :, None]
        ks = k[bi, sel].astype(F32)
        vs = v[bi, sel].astype(F32)
        logits = jnp.einsum('bqhd,bqkhd->bqhk', qblk.astype(F32), ks) * scale
        logits = jnp.where(sel_ok[:, :, None, :], logits, -jnp.inf)
        p = jax.nn.softmax(logits, axis=-1)
        return jnp.einsum('bqhk,bqkhd->bqhd', p, vs).astype(qblk.dtype)

    return _query_block_sweep(step, Q_BLOCK, q, q_idx, w_idx)


def setup_inputs(seed: int = 0) -> dict:
    key = jax.random.key(seed)
    ks = jax.random.split(key, 24)

    def nrm(k, shape, scale):
        return jax.random.normal(k, shape, F32) * scale

    def gain(k, shape):
        return 1.0 + 0.02 * jax.random.normal(k, shape, F32)

    L = DEPTH
    x = nrm(ks[0], (BATCH, SEQ, D_MODEL), 1.0)
    mem = nrm(ks[1], (BATCH, MEM_LEN, D_MODEL), 1.0)
    positions = (jax.random.randint(ks[2], (BATCH, 1), 0, 1024, dtype=jnp.int32)
                 + jnp.arange(SEQ, dtype=jnp.int32)[None, :])
    return {
        'x': x,
        'mem': mem,
        'positions': positions,
        'attn_norm': gain(ks[3], (L, D_MODEL)),
        'w_in': nrm(ks[4], (L, D_MODEL, D_IN), D_MODEL ** -0.5),
        'moba_qk_gain': gain(ks[5], (L, 2, HEAD_DIM)),
        'diff_qk_gain': gain(ks[6], (L, 2, DIFF_QK_DIM)),
        'diff_lambda': nrm(ks[7], (L, 4, DIFF_QK_DIM), 0.1),
        'diff_subln': gain(ks[8], (L, DIFF_V_DIM)),
        'dsa_qk_gain': gain(ks[9], (L, 2, HEAD_DIM)),
        'w_out': nrm(ks[10], (L, D_MIX, D_MODEL), D_MIX ** -0.5),
        'cross_norm': gain(ks[11], (L, D_MODEL)),
        'mem_norm': gain(ks[12], (L, D_MODEL)),
        'cross_wq': nrm(ks[13], (L, D_MODEL, CROSS_W), D_MODEL ** -0.5),
        'cross_wkv': nrm(ks[14], (L, D_MODEL, 2 * CROSS_W), D_MODEL ** -0.5),
        'cross_qk_gain': gain(ks[15], (L, 2, HEAD_DIM)),
        'cross_wo': nrm(ks[16], (L, CROSS_W, D_MODEL), CROSS_W ** -0.5),
        'ffn_norm': gain(ks[17], (L, D_MODEL)),
        'ffn_w_up': nrm(ks[18], (L, D_MODEL, 2 * D_FF), D_MODEL ** -0.5),
        'ffn_conv_w': nrm(ks[19], (L, CONV_WIDTH, 2 * D_FF), CONV_WIDTH ** -0.5),
        'ffn_conv_b': nrm(ks[20], (L, 2 * D_FF), 0.02),
        'ffn_w_down': nrm(ks[21], (L, D_FF, D_MODEL), D_FF ** -0.5),
    }


def reference(x, mem, positions, attn_norm, w_in, moba_qk_gain, diff_qk_gain, diff_lambda,
              diff_subln, dsa_qk_gain, w_out, cross_norm, mem_norm, cross_wq, cross_wkv,
              cross_qk_gain, cross_wo, ffn_norm, ffn_w_up, ffn_conv_w, ffn_conv_b, ffn_w_down):
    b, s, _ = x.shape
    for l in range(DEPTH):
        h = rms_norm(x, attn_norm[l])
        (mq, mk, mv, dq, dk, dv, sq, sk, sv, iq, ik, iw) = _split_columns(h @ w_in[l])

        mq = rope(rms_norm(mq.reshape(b, s, MOBA_HEADS, HEAD_DIM), moba_qk_gain[l, 0]), positions)
        mk = rope(rms_norm(mk.reshape(b, s, MOBA_HEADS, HEAD_DIM), moba_qk_gain[l, 1]), positions)
        mv = mv.reshape(b, s, MOBA_HEADS, HEAD_DIM)
        o_moba = moba_attention(mq, mk, mv).reshape(b, s, MOBA_W)

        dq = rope(rms_norm(dq.reshape(b, s, 2 * DIFF_HEADS, DIFF_QK_DIM), diff_qk_gain[l, 0]),
                  positions).reshape(b, s, DIFF_HEADS, 2, DIFF_QK_DIM)
        dk = rope(rms_norm(dk.reshape(b, s, 2 * DIFF_HEADS, DIFF_QK_DIM), diff_qk_gain[l, 1]),
                  positions).reshape(b, s, DIFF_HEADS, 2, DIFF_QK_DIM)
        dv = dv.reshape(b, s, DIFF_HEADS, DIFF_V_DIM)
        lam_init = 0.8 - 0.6 * math.exp(-0.3 * l)
        lf = diff_lambda[l].astype(F32)
        lam = jnp.exp(jnp.sum(lf[0] * lf[1])) - jnp.exp(jnp.sum(lf[2] * lf[3])) + lam_init
        o_diff = diff_attention(dq, dk, dv, lam)
        o_diff = (rms_norm(o_diff, diff_subln[l]) * (1.0 - lam_init)).reshape(b, s, DIFF_V_W)

        sq = rope(rms_norm(sq.reshape(b, s, DSA_HEADS, HEAD_DIM), dsa_qk_gain[l, 0]), positions)
        sk = rope(rms_norm(sk.reshape(b, s, DSA_HEADS, HEAD_DIM), dsa_qk_gain[l, 1]), positions)
        sv = sv.reshape(b, s, DSA_HEADS, HEAD_DIM)
        iq = rope(iq.reshape(b, s, IDX_HEADS, IDX_DIM), positions)
        ik = rope(ik.reshape(b, s, 1, IDX_DIM), positions)[:, :, 0]
        iw = iw * (IDX_HEADS ** -0.5)
        o_dsa = dsa_attention(sq, sk, sv, iq, ik, iw).reshape(b, s, DSA_W)

        mixed = jnp.concatenate([o_moba, o_diff, o_dsa], axis=-1).astype(x.dtype)
        x = x + mixed @ w_out[l]

        h = rms_norm(x, cross_norm[l])
        m = rms_norm(mem, mem_norm[l])
        n_mem = mem.shape[1]
        cq = rms_norm((h @ cross_wq[l]).reshape(b, s, CROSS_HEADS, HEAD_DIM), cross_qk_gain[l, 0])
        ck, cv = jnp.split(m @ cross_wkv[l], 2, axis=-1)
        ck = rms_norm(ck.reshape(b, n_mem, CROSS_HEADS, HEAD_DIM), cross_qk_gain[l, 1])
        cv = cv.reshape(b, n_mem, CROSS_HEADS, HEAD_DIM)
        sc = jnp.einsum('bqhd,bmhd->bhqm', cq.astype(F32), ck.astype(F32)) * (HEAD_DIM ** -0.5)
        p = jax.nn.softmax(sc, axis=-1)
        co = jnp.einsum('bhqm,bmhd->bqhd', p, cv.astype(F32)).reshape(b, s, CROSS_W)
        x = x + co.astype(x.dtype) @ cross_wo[l]

        h = rms_norm(x, ffn_norm[l])
        u = h @ ffn_w_up[l]
        up = jnp.pad(u, ((0, 0), (CONV_WIDTH - 1, 0), (0, 0)))
        cw = ffn_conv_w[l]
        uc = (cw[0] * up[:, :-2] + cw[1] * up[:, 1:-1] + cw[2] * up[:, 2:] + ffn_conv_b[l])
        g, val = jnp.split(uc, 2, axis=-1)
        x = x + (jax.nn.silu(g) * val) @ ffn_w_down[l]
    return x
```

```python
import numpy as np
import ml_dtypes
from contextlib import ExitStack
import concourse.bass as bass
import concourse.mybir as mybir
from concourse.bass_utils import run_bass_kernel_spmd

F32 = mybir.dt.float32
BF16 = mybir.dt.bfloat16
I32 = mybir.dt.int32
ALU = mybir.AluOpType
AF = mybir.ActivationFunctionType
AX = mybir.AxisListType

NCORES = 8
ENGS = ("pe", "act", "dve", "pool", "sp")


class Buf:
    __slots__ = ("name", "w", "rs")

    def __init__(self, name=""):
        self.name = name
        self.w = None
        self.rs = []


class Op:
    __slots__ = ("eng", "fn", "deps", "sig", "sigidx", "dma", "dn")

    def __init__(self, eng, fn, dma):
        self.eng = eng
        self.fn = fn
        self.deps = []
        self.sig = False
        self.sigidx = -1
        self.dma = dma
        self.dn = -1


class Prog:
    NDS = 40
    CAP = 30000

    def __init__(self, nc):
        self.nc = nc
        self.ops = []
        self.ndma = 0

    def op(self, eng, fn, reads=(), writes=(), dma=False):
        o = Op(eng, fn, dma)
        deps = {}
        for b in reads:
            if b.w is not None:
                deps[id(b.w)] = b.w
        for b in writes:
            if b.w is not None:
                deps[id(b.w)] = b.w
            for r in b.rs:
                deps[id(r)] = r
        deps.pop(id(o), None)
        o.deps = list(deps.values())
        for b in writes:
            b.w = o
            b.rs = []
        for b in reads:
            if not b.rs or b.rs[-1] is not o:
                b.rs.append(o)
        if dma:
            o.dn = self.ndma
            self.ndma += 1
        self.ops.append(o)
        return o

    def dma(self, out, in_, reads=(), writes=(), eng="sp", **kw):
        return self.op(eng, lambda e: e.dma_start(out=out, in_=in_, **kw), reads, writes, dma=True)

    def emit(self, final_bufs=()):
        nc = self.nc
        ops = self.ops
        fin = Op("sp", None, False)
        fd = {}
        for b in final_bufs:
            if b.w is not None:
                fd[id(b.w)] = b.w
        fin.deps = list(fd.values())
        ops = ops + [fin]
        for o in ops:
            for p in o.deps:
                if not p.dma:
                    if p.eng == "pe" and o.eng == "pe" and not o.dma:
                        continue
                    p.sig = True
        cnt = {e: 0 for e in ENGS}
        for o in ops:
            if o.sig:
                o.sigidx = cnt[o.eng]
                cnt[o.eng] += 1
        with ExitStack() as st:
            esems = {}
            for e in ENGS:
                n = (cnt[e] + self.CAP - 1) // self.CAP
                esems[e] = [st.enter_context(nc.semaphore(f"s_{e}{k}")) for k in range(n)]
            nds = min(self.NDS, max(1, self.ndma))
            dsems = [st.enter_context(nc.semaphore(f"s_dma{k}")) for k in range(nds)]
            block = st.enter_context(nc.Block())
            per = {e: [o for o in ops if o.eng == e] for e in ENGS}

            def run(eng_name, eobj):
                waited_e = {e: -1 for e in ENGS}
                waited_d = {}
                for o in per[eng_name]:
                    need_e = {}
                    need_d = {}
                    for p in o.deps:
                        if p.dma:
                            k = p.dn % nds
                            v = 16 * (p.dn // nds + 1)
                            if waited_d.get(k, 0) < v:
                                need_d[k] = max(need_d.get(k, 0), v)
                        else:
                            if p.eng == "pe" and eng_name == "pe" and not o.dma:
                                continue
                            if waited_e[p.eng] < p.sigidx:
                                need_e[p.eng] = max(need_e.get(p.eng, -1), p.sigidx)
                    if o.dma:
                        k = o.dn % nds
                        v = 16 * (o.dn // nds)
                        if v > 0 and waited_d.get(k, 0) < v:
                            need_d[k] = max(need_d.get(k, 0), v)
                    for pe_, si in need_e.items():
                        eobj.wait_ge(esems[pe_][si // self.CAP], si % self.CAP + 1)
                        waited_e[pe_] = si
                    for k, v in need_d.items():
                        eobj.wait_ge(dsems[k], v)
                        waited_d[k] = v
                    if o.fn is None:
                        continue
                    ins = o.fn(eobj)
                    if o.dma:
                        ins.then_inc(dsems[o.dn % nds], 16)
                    elif o.sig:
                        ins.then_inc(esems[eng_name][o.sigidx // self.CAP], 1)

            @block.tensor
            def _(e):
                run("pe", e)

            @block.scalar
            def _(e):
                run("act", e)

            @block.vector
            def _(e):
                run("dve", e)

            @block.gpsimd
            def _(e):
                run("pool", e)

            @block.sync
            def _(e):
                run("sp", e)


class TB:
    __slots__ = ("t", "b")

    def __init__(self, t, name=""):
        self.t = t
        self.b = Buf(name)


class Ctx:
    def __init__(self):
        self.nc = bass.Bass("TRN2", target_bir_lowering=False)
        self.P = Prog(self.nc)
        self.st = ExitStack()
        self.finals = []
        self.n = 0

    def dram(self, name, shape, dt, out=False):
        return self.nc.dram_tensor(name, list(shape), dt, kind="ExternalOutput" if out else "ExternalInput").ap()

    def sb(self, shape, dt, name=None):
        self.n += 1
        name = f"sb_{name or 't'}_{self.n}"
        return TB(self.st.enter_context(self.nc.sbuf_tensor(name, list(shape), dt)), name)

    def ps(self, shape, dt=F32, name=None):
        self.n += 1
        name = f"ps_{name or 'p'}_{self.n}"
        return TB(self.st.enter_context(self.nc.psum_tensor(name, list(shape), dt)), name)

    def ring(self, n, shape, dt, name, psum=False):
        return Ring([(self.ps if psum else self.sb)(shape, dt, f"{name}{k}") for k in range(n)])

    def load(self, dst_ap, src_ap, dst_b, eng="sp"):
        return self.P.dma(dst_ap, src_ap, writes=[dst_b], eng=eng)

    def store(self, dst_ap, src_ap, src_b, eng="sp"):
        fb = Buf("out")
        self.finals.append(fb)
        return self.P.dma(dst_ap, src_ap, reads=[src_b], writes=[fb], eng=eng)

    def finish(self):
        self.P.emit(self.finals)
        self.st.close()
        return self.nc


class Ring:
    def __init__(self, items):
        self.items = items
        self.k = 0

    def next(self):
        it = self.items[self.k % len(self.items)]
        self.k += 1
        return it


D_MODEL = 2048
DC = 16
TL = 1024
D_IN = 6728
D_FF = 5504
FC = 43
EPS = 1e-6
PI = float(np.pi)

OFF = {}
_o = 0
for _n, _w in (("mq", 768), ("mk", 768), ("mv", 768), ("dq", 512), ("dk", 512), ("dv", 512),
               ("sq", 768), ("sk", 768), ("sv", 768), ("iq", 512), ("ik", 64), ("iw", 8)):
    OFF[_n] = _o
    _o += _w
FCH = []
for _n, _nch, _kind in (("mq", 6, 0), ("mk", 6, 0), ("dq", 4, 1), ("dk", 4, 1), ("sq", 6, 0), ("sk", 6, 0),
                        ("iq", 4, 2), ("ik", 1, 2)):
    for _j in range(_nch):
        FCH.append((_n, OFF[_n] + 128 * _j, _kind))
NF = len(FCH)
FIDX = {}
for _i, (_n, _c, _k) in enumerate(FCH):
    FIDX.setdefault(_n, _i)


def emit_rope_tables(cx, pos_ap, rc):
    P = cx.P
    pi_ = cx.sb([128, TL], I32, "posi")
    cx.load(pi_.t[:], pos_ap.partition_broadcast(128), pi_.b)
    pf = cx.sb([128, TL], F32, "posf")
    P.op("dve", lambda e: e.tensor_copy(out=pf.t[:], in_=pi_.t[:]), [pi_.b], [pf.b])
    tabs = {}
    a = cx.sb([128, TL], F32, "ra")
    ki = cx.sb([128, TL], I32, "rki")
    kf = cx.sb([128, TL], F32, "rkf")
    for hd, col in ((128, 0), (64, 1)):
        for nm, shift in (("cos", PI / 2), ("sin", 0.0)):
            out = cx.sb([128, TL], F32, f"{nm}{hd}")
            P.op("dve", lambda e, col=col: e.tensor_scalar(out=a.t[:], in0=pf.t[:], scalar1=rc.t[:, col:col + 1],
                                                           scalar2=None, op0=ALU.mult), [pf.b, rc.b], [a.b])
            P.op("dve", lambda e, shift=shift: e.tensor_scalar(out=a.t[:], in0=a.t[:], scalar1=shift - PI,
                                                               scalar2=None, op0=ALU.add), [a.b], [a.b])
            P.op("dve", lambda e: e.tensor_scalar(out=ki.t[:], in0=a.t[:], scalar1=1.0 / (2 * PI), scalar2=None,
                                                  op0=ALU.mult), [a.b], [ki.b])
            P.op("dve", lambda e: e.tensor_copy(out=kf.t[:], in_=ki.t[:]), [ki.b], [kf.b])
            P.op("dve", lambda e: e.scalar_tensor_tensor(out=a.t[:], in0=kf.t[:], scalar=-2 * PI, in1=a.t[:],
                                                         op0=ALU.mult, op1=ALU.add), [kf.b, a.b], [a.b])
            P.op("dve", lambda e: e.tensor_scalar(out=a.t[:], in0=a.t[:], scalar1=PI, scalar2=-PI, op0=ALU.min,
                                                  op1=ALU.max), [a.b], [a.b])
            P.op("act", lambda e, out=out: e.activation(out=out.t[:], in_=a.t[:], func=AF.Sin, scale=-1.0),
                 [a.b], [out.b])
            if nm == "sin":
                P.op("dve", lambda e, out=out, col=col: e.tensor_scalar(out=out.t[:], in0=out.t[:],
                                                                        scalar1=rc.t[:, 2 + col:3 + col], scalar2=None,
                                                                        op0=ALU.mult), [out.b, rc.b], [out.b])
            tabs[(nm, hd)] = out
    return tabs


def build_A():
    cx = Ctx()
    P = cx.P
    xT = cx.dram("xT", [DC, 128, TL], F32)
    pos = cx.dram("pos", [1, TL], I32)
    gnorm = cx.dram("gnorm", [128, DC], F32)
    wF = cx.dram("wF", [NF, 128, DC, 128], F32)
    wV = cx.dram("wV", [8, 128, DC, 256], F32)
    wI = cx.dram("wI", [128, DC, 8], F32)
    gq = cx.dram("gq", [128, NF], F32)
    rcd = cx.dram("rc", [128, 4], F32)
    cmd = cx.dram("cm", [4, 128, 128], F32)
    fT = cx.dram("fT", [NF, 128, TL], BF16, out=True)
    vtok = cx.dram("vtok", [TL, 2048], BF16, out=True)
    iwo = cx.dram("iwo", [TL, 8], F32, out=True)

    gn = cx.sb([128, DC], F32, "gn")
    cx.load(gn.t[:], gnorm, gn.b)
    gqt = cx.sb([128, NF], F32, "gqt")
    cx.load(gqt.t[:], gq, gqt.b)
    rc = cx.sb([128, 4], F32, "rc")
    cx.load(rc.t[:], rcd, rc.b)
    cm = cx.sb([128, 4, 128], F32, "cm")
    for k in range(4):
        cx.load(cm.t[:, k, :], cmd[k], cm.b)
    epsb = cx.sb([128, 1], F32, "epsb")
    P.op("dve", lambda e: e.memset(epsb.t[:], EPS), [], [epsb.b])
    tabs = emit_rope_tables(cx, pos, rc)

    hT = cx.sb([128, DC, TL], BF16, "hT")
    xr = cx.ring(3, [128, TL], F32, "xr")
    sqr = cx.ring(2, [128, TL], F32, "sqr")
    psA = cx.ring(2, [128, 512], F32, "psA", psum=True)
    psM = cx.ring(2, [128, 512], F32, "psM", psum=True)
    psW = cx.ring(2, [128, 512], F32, "psW", psum=True)
    psV = cx.ring(2, [128, 512], F32, "psV", psum=True)
    m0, m1 = psM.next(), psM.next()
    for dc in range(DC):
        x_ = xr.next()
        cx.load(x_.t[:], xT[dc], x_.b)
        s_ = sqr.next()
        P.op("act", lambda e, x_=x_, s_=s_: e.activation(out=s_.t[:], in_=x_.t[:], func=AF.Square), [x_.b], [s_.b])
        for hf, m_ in ((0, m0), (1, m1)):
            P.op("pe", lambda e, m_=m_, s_=s_, hf=hf, dc=dc: e.matmul(m_.t[:], lhsT=cm.t[:, 2, :],
                                                                     rhs=s_.t[:, hf * 512:(hf + 1) * 512],
                                                                     start=(dc == 0), stop=(dc == DC - 1)),
                 [cm.b, s_.b], [m_.b])
    rstd = cx.sb([128, TL], F32, "rstd")
    for hf, m_ in ((0, m0), (1, m1)):
        sl = slice(hf * 512, (hf + 1) * 512)
        P.op("act", lambda e, m_=m_, sl=sl: e.activation(out=rstd.t[:, sl], in_=m_.t[:], func=AF.Sqrt, scale=1.0 / DC,
                                                         bias=epsb.t[:, 0:1]), [m_.b, epsb.b], [rstd.b])
    P.op("dve", lambda e: e.reciprocal(out=rstd.t[:], in_=rstd.t[:]), [rstd.b], [rstd.b])
    for dc in range(DC):
        x_ = xr.next()
        cx.load(x_.t[:], xT[dc], x_.b)
        P.op("dve", lambda e, x_=x_, dc=dc: e.scalar_tensor_tensor(
            out=hT.t[:, dc, :], in0=x_.t[:], scalar=gn.t[:, dc:dc + 1], in1=rstd.t[:], op0=ALU.mult, op1=ALU.mult),
            [x_.b, gn.b, rstd.b], [hT.b])

    wst = cx.ring(2, [128, DC, 128], F32, "wst")
    wbf = cx.ring(2, [128, DC, 128], BF16, "wbf")
    xs_r = cx.ring(2, [128, 512], F32, "xs")
    sq_r = cx.ring(2, [128, 512], F32, "sq")
    rr_r = cx.ring(2, [128, 512], F32, "rr")
    y_r = cx.ring(2, [128, 512], F32, "y")
    t1_r = cx.ring(2, [128, 512], F32, "t1")
    t2_r = cx.ring(2, [128, 512], F32, "t2")
    ob_r = cx.ring(3, [128, 512], BF16, "ob")
    for j, (nm, c0, kind) in enumerate(FCH):
        ws = wst.next()
        cx.load(ws.t[:], wF[j], ws.b, eng="sp" if j % 2 == 0 else "pool")
        wb = wbf.next()
        P.op("dve" if j % 2 else "pool", lambda e, ws=ws, wb=wb: e.tensor_copy(out=wb.t[:], in_=ws.t[:]), [ws.b], [wb.b])
        hd = 128 if kind == 0 else 64
        perm_k = 0 if kind == 0 else 1
        mm_k = 2 if kind == 0 else 3
        cosT, sinT = tabs[("cos", hd)], tabs[("sin", hd)]
        for hf in range(2):
            sl = slice(hf * 512, (hf + 1) * 512)
            pa = psA.next()
            for dc in range(DC):
                P.op("pe", lambda e, pa=pa, wb=wb, dc=dc, sl=sl: e.matmul(pa.t[:], lhsT=wb.t[:, dc, :], rhs=hT.t[:, dc, sl],
                                                                         start=(dc == 0), stop=(dc == DC - 1)),
                     [wb.b, hT.b], [pa.b])
            xs = xs_r.next()
            P.op("act", lambda e, xs=xs, pa=pa: e.copy(out=xs.t[:], in_=pa.t[:]), [pa.b], [xs.b])
            if kind != 2:
                sq = sq_r.next()
                P.op("act", lambda e, sq=sq, xs=xs: e.activation(out=sq.t[:], in_=xs.t[:], func=AF.Square), [xs.b], [sq.b])
                pm = psM.next()
                P.op("pe", lambda e, pm=pm, sq=sq, mm_k=mm_k: e.matmul(pm.t[:], lhsT=cm.t[:, mm_k, :], rhs=sq.t[:],
                                                                      start=True, stop=True), [cm.b, sq.b], [pm.b])
                rr = rr_r.next()
                P.op("act", lambda e, rr=rr, pm=pm: e.activation(out=rr.t[:], in_=pm.t[:], func=AF.Sqrt,
                                                                 bias=epsb.t[:, 0:1]), [pm.b, epsb.b], [rr.b])
                P.op("dve", lambda e, rr=rr: e.reciprocal(out=rr.t[:], in_=rr.t[:]), [rr.b], [rr.b])
                y = y_r.next()
                P.op("dve", lambda e, y=y, xs=xs, rr=rr, j=j: e.scalar_tensor_tensor(
                    out=y.t[:], in0=xs.t[:], scalar=gqt.t[:, j:j + 1], in1=rr.t[:], op0=ALU.mult, op1=ALU.mult),
                    [xs.b, gqt.b, rr.b], [y.b])
            else:
                y = xs
            pw = psW.next()
            P.op("pe", lambda e, pw=pw, y=y, perm_k=perm_k: e.matmul(pw.t[:], lhsT=cm.t[:, perm_k, :], rhs=y.t[:],
                                                                    start=True, stop=True), [cm.b, y.b], [pw.b])
            t1 = t1_r.next()
            P.op("pool", lambda e, t1=t1, y=y, sl=sl, cosT=cosT: e.tensor_tensor(out=t1.t[:], in0=y.t[:], in1=cosT.t[:, sl],
                                                                               op=ALU.mult), [y.b, cosT.b], [t1.b])
            t2 = t2_r.next()
            P.op("dve", lambda e, t2=t2, pw=pw, sl=sl, sinT=sinT: e.tensor_tensor(out=t2.t[:], in0=pw.t[:], in1=sinT.t[:, sl],
                                                                                 op=ALU.mult), [pw.b, sinT.b], [t2.b])
            ob = ob_r.next()
            P.op("dve", lambda e, ob=ob, t1=t1, t2=t2: e.tensor_tensor(out=ob.t[:], in0=t1.t[:], in1=t2.t[:], op=ALU.add),
                 [t1.b, t2.b], [ob.b])
            cx.store(fT[j, :, sl], ob.t[:], ob.b)

    vst = cx.sb([128, DC, 256], F32, "vst")
    vbf = cx.ring(2, [128, DC, 256], BF16, "vbf")
    vo_r = cx.ring(3, [128, 256], BF16, "vo")
    for cb in range(8):
        cx.load(vst.t[:], wV[cb], vst.b)
        vb = vbf.next()
        P.op("dve" if cb % 2 else "pool", lambda e, vb=vb: e.tensor_copy(out=vb.t[:], in_=vst.t[:]), [vst.b], [vb.b])
        for tt in range(8):
            pv = psV.next()
            for dc in range(DC):
                P.op("pe", lambda e, pv=pv, vb=vb, dc=dc, tt=tt: e.matmul(pv.t[:, 0:256], lhsT=hT.t[:, dc, tt * 128:(tt + 1) * 128],
                                                                         rhs=vb.t[:, dc, :], start=(dc == 0), stop=(dc == DC - 1)),
                     [hT.b, vb.b], [pv.b])
            vo = vo_r.next()
            P.op("act", lambda e, vo=vo, pv=pv: e.copy(out=vo.t[:], in_=pv.t[:, 0:256]), [pv.b], [vo.b])
            cx.store(vtok[tt * 128:(tt + 1) * 128, cb * 256:(cb + 1) * 256], vo.t[:], vo.b)
    wis = cx.sb([128, DC, 8], F32, "wis")
    cx.load(wis.t[:], wI, wis.b)
    wib = cx.sb([128, DC, 8], BF16, "wib")
    P.op("dve", lambda e: e.tensor_copy(out=wib.t[:], in_=wis.t[:]), [wis.b], [wib.b])
    io_r = cx.ring(2, [128, 8], F32, "io")
    for tt in range(8):
        pv = psV.next()
        for dc in range(DC):
            P.op("pe", lambda e, pv=pv, dc=dc, tt=tt: e.matmul(pv.t[:, 0:8], lhsT=hT.t[:, dc, tt * 128:(tt + 1) * 128],
                                                              rhs=wib.t[:, dc, :], start=(dc == 0), stop=(dc == DC - 1)),
                 [hT.b, wib.b], [pv.b])
        io = io_r.next()
        P.op("act", lambda e, io=io, pv=pv: e.activation(out=io.t[:], in_=pv.t[:, 0:8], func=AF.Copy, scale=8 ** -0.5),
             [pv.b], [io.b])
        cx.store(iwo[tt * 128:(tt + 1) * 128, :], io.t[:], io.b)
    return cx.finish()


def tok_index(c):
    i = np.arange(8)[:, None]
    r = np.arange(128)[None, :]
    return ((8 * i + c) * 128 + r).reshape(-1)


def fm(a):
    T, F = a.shape
    return np.ascontiguousarray(a.T.reshape(F // 128, 128, T))


def wtile(w):
    return np.ascontiguousarray(w.reshape(DC, 128, w.shape[1]).transpose(1, 0, 2))


def rope_consts():
    rc = np.zeros((128, 4), np.float32)
    p = np.arange(128)
    inv128 = np.float32(10000.0) ** (-(np.arange(64, dtype=np.float32) * np.float32(2.0) / np.float32(128)))
    inv64 = np.float32(10000.0) ** (-(np.arange(32, dtype=np.float32) * np.float32(2.0) / np.float32(64)))
    rc[:, 0] = inv128[p % 64]
    rc[:, 1] = inv64[p % 32]
    rc[:, 2] = np.where(p < 64, -1.0, 1.0)
    rc[:, 3] = np.where((p % 64) < 32, -1.0, 1.0)
    cm = np.zeros((4, 128, 128), np.float32)
    m = np.arange(128)
    cm[0, (m + 64) % 128, m] = 1.0
    cm[1, 64 * (m // 64) + ((m % 64) + 32) % 64, m] = 1.0
    cm[2] = 1.0 / 128
    cm[3, :64, :64] = 1.0 / 64
    cm[3, 64:, 64:] = 1.0 / 64
    return rc, cm


def prep_A_weights(inp, l):
    w_in = inp["w_in"][l]
    wF = np.zeros((NF, 128, DC, 128), np.float32)
    gq = np.ones((128, NF), np.float32)
    gains = {"mq": inp["moba_qk_gain"][l, 0], "mk": inp["moba_qk_gain"][l, 1],
             "dq": np.tile(inp["diff_qk_gain"][l, 0], 2), "dk": np.tile(inp["diff_qk_gain"][l, 1], 2),
             "sq": inp["dsa_qk_gain"][l, 0], "sk": inp["dsa_qk_gain"][l, 1]}
    for j, (nm, c0, kind) in enumerate(FCH):
        ncol = 64 if nm == "ik" else 128
        wF[j, :, :, :ncol] = wtile(w_in[:, c0:c0 + ncol])
        if nm in gains:
            gq[:, j] = gains[nm]
    wv = np.concatenate([w_in[:, OFF["mv"]:OFF["mv"] + 768], w_in[:, OFF["dv"]:OFF["dv"] + 512],
                         w_in[:, OFF["sv"]:OFF["sv"] + 768]], axis=1)
    wV = np.stack([wtile(wv[:, 256 * k:256 * (k + 1)]) for k in range(8)])
    wI = wtile(w_in[:, OFF["iw"]:OFF["iw"] + 8])
    rc, cm = rope_consts()
    return dict(wF=wF, wV=wV, wI=wI, gq=gq, rc=rc, cm=cm,
                gnorm=np.ascontiguousarray(inp["attn_norm"][l].reshape(DC, 128).T))


TLH = 8 * 130
CGRP = ((0, 3), (3, 3), (6, 2))


def build_C(dbg=False):
    cx = Ctx()
    P = cx.P
    x2h = cx.dram("x2h", [DC, 128, TLH], F32)
    gnorm = cx.dram("gnorm", [128, DC], F32)
    wU = cx.dram("wU", [2 * FC, 128, DC, 128], F32)
    cwd = cx.dram("cw", [128, 2 * FC, 4], F32)
    wD = cx.dram("wD", [DC, 3, 128, 16, 128], F32)
    cmd = cx.dram("cm", [4, 128, 128], F32)
    x3T = cx.dram("x3T", [DC, 128, TL], F32, out=True)

    gn = cx.sb([128, DC], F32, "gn")
    cx.load(gn.t[:], gnorm, gn.b)
    cw = cx.sb([128, 2 * FC, 4], F32, "cw")
    cx.load(cw.t[:], cwd, cw.b)
    cm = cx.sb([128, 128], F32, "cm")
    cx.load(cm.t[:], cmd[2], cm.b)
    epsb = cx.sb([128, 1], F32, "epsb")
    P.op("dve", lambda e: e.memset(epsb.t[:], EPS), [], [epsb.b])

    psU = cx.ring(6, [128, 512], F32, "psU", psum=True)
    psD = cx.ring(2, [128, 512], F32, "psD", psum=True)
    hT = cx.sb([128, DC, TLH], BF16, "hT")
    actT = cx.sb([128, FC, TL], BF16, "actT")
    xr = cx.ring(3, [128, TLH], F32, "xr")
    sqr = cx.ring(2, [128, TLH], F32, "sqr")
    ms = [psU.next() for _ in range(3)]
    for dc in range(DC):
        x_ = xr.next()
        cx.load(x_.t[:], x2h[dc], x_.b)
        s_ = sqr.next()
        P.op("act", lambda e, x_=x_, s_=s_: e.activation(out=s_.t[:], in_=x_.t[:], func=AF.Square), [x_.b], [s_.b])
        for (t0, nt), m_ in zip(CGRP, ms):
            P.op("pe", lambda e, m_=m_, s_=s_, t0=t0, nt=nt, dc=dc: e.matmul(
                m_.t[:, 0:nt * 130], lhsT=cm.t[:], rhs=s_.t[:, t0 * 130:(t0 + nt) * 130], start=(dc == 0),
                stop=(dc == DC - 1)), [cm.b, s_.b], [m_.b])
    rstd = cx.sb([128, TLH], F32, "rstd")
    for (t0, nt), m_ in zip(CGRP, ms):
        P.op("act", lambda e, m_=m_, t0=t0, nt=nt: e.activation(out=rstd.t[:, t0 * 130:(t0 + nt) * 130], in_=m_.t[:, 0:nt * 130],
                                                               func=AF.Sqrt, scale=1.0 / DC, bias=epsb.t[:, 0:1]),
             [m_.b, epsb.b], [rstd.b])
    P.op("dve", lambda e: e.reciprocal(out=rstd.t[:], in_=rstd.t[:]), [rstd.b], [rstd.b])
    for dc in range(DC):
        x_ = xr.next()
        cx.load(x_.t[:], x2h[dc], x_.b)
        P.op("dve", lambda e, x_=x_, dc=dc: e.scalar_tensor_tensor(
            out=hT.t[:, dc, :], in0=x_.t[:], scalar=gn.t[:, dc:dc + 1], in1=rstd.t[:], op0=ALU.mult, op1=ALU.mult),
            [x_.b, gn.b, rstd.b], [hT.b])

    wst = cx.ring(2, [128, DC, 128], F32, "wst")
    wbf = cx.ring(2, [128, DC, 128], BF16, "wbf")
    c0_r = cx.ring(2, [128, 3, 128], F32, "c0")
    c1_r = cx.ring(2, [128, 3, 128], F32, "c1")
    gv_r = cx.ring(2, [128, TL], F32, "gv")
    nload = 0
    for fc in range(FC):
        gsil = gv_r.next()
        for which in (0, 1):
            ch = fc + which * FC
            ws = wst.next()
            cx.load(ws.t[:], wU[ch], ws.b, eng="sp" if nload % 2 == 0 else "pool")
            wb = wbf.next()
            P.op("pool" if nload % 2 == 0 else "dve", lambda e, ws=ws, wb=wb: e.tensor_copy(out=wb.t[:], in_=ws.t[:]),
                 [ws.b], [wb.b])
            nload += 1
            for (t0, nt) in CGRP:
                pu = psU.next()
                for dc in range(DC):
                    P.op("pe", lambda e, pu=pu, wb=wb, dc=dc, t0=t0, nt=nt: e.matmul(
                        pu.t[:, 0:nt * 130], lhsT=wb.t[:, dc, :], rhs=hT.t[:, dc, t0 * 130:(t0 + nt) * 130],
                        start=(dc == 0), stop=(dc == DC - 1)), [wb.b, hT.b], [pu.b])
                uv = pu.t[:, 0:nt * 130].rearrange("p (t c) -> p t c", c=130)
                c0 = c0_r.next()
                P.op("act", lambda e, c0=c0, uv=uv, nt=nt, ch=ch: e.activation(
                    out=c0.t[:, 0:nt, :], in_=uv[:, :, 2:130], func=AF.Identity, scale=cw.t[:, ch, 2:3],
                    bias=cw.t[:, ch, 3:4]), [pu.b, cw.b], [c0.b])
                c1 = c1_r.next()
                P.op("dve", lambda e, c0=c0, c1=c1, uv=uv, nt=nt, ch=ch: e.scalar_tensor_tensor(
                    out=c1.t[:, 0:nt, :], in0=uv[:, :, 1:129], scalar=cw.t[:, ch, 1:2], in1=c0.t[:, 0:nt, :],
                    op0=ALU.mult, op1=ALU.add), [pu.b, cw.b, c0.b], [c1.b])
                osl = slice(t0 * 128, (t0 + nt) * 128)
                if which == 0:
                    gview = gsil.t[:, osl].rearrange("p (t c) -> p t c", c=128)
                    P.op("dve", lambda e, c1=c1, uv=uv, nt=nt, ch=ch, gview=gview: e.scalar_tensor_tensor(
                        out=gview, in0=uv[:, :, 0:128], scalar=cw.t[:, ch, 0:1], in1=c1.t[:, 0:nt, :],
                        op0=ALU.mult, op1=ALU.add), [pu.b, cw.b, c1.b], [gsil.b])
                    P.op("act", lambda e, osl=osl, gsil=gsil: e.activation(out=gsil.t[:, osl], in_=gsil.t[:, osl], func=AF.Silu),
                         [gsil.b], [gsil.b])
                else:
                    P.op("dve", lambda e, c1=c1, c0=c0, uv=uv, nt=nt, ch=ch: e.scalar_tensor_tensor(
                        out=c0.t[:, 0:nt, :], in0=uv[:, :, 0:128], scalar=cw.t[:, ch, 0:1], in1=c1.t[:, 0:nt, :],
                        op0=ALU.mult, op1=ALU.add), [pu.b, cw.b, c1.b], [c0.b])
                    aview = actT.t[:, fc, osl].rearrange("p (t c) -> p t c", c=128)
                    gview = gsil.t[:, osl].rearrange("p (t c) -> p t c", c=128)
                    P.op("pool", lambda e, c0=c0, nt=nt, aview=aview, gview=gview: e.tensor_tensor(
                        out=aview, in0=c0.t[:, 0:nt, :], in1=gview, op=ALU.mult), [c0.b, gsil.b], [actT.b])

    if dbg:
        dA = cx.dram('dbgA', [128, FC, TL], BF16, out=True)
        cx.store(dA, actT.t[:], actT.b)
        dH = cx.dram('dbgH', [128, DC, TLH], BF16, out=True)
        cx.store(dH, hT.t[:], hT.b)
    xo_r = cx.ring(2, [128, TL], F32, "xo")
    xres_r = cx.ring(2, [128, 8, 128], F32, "xres")
    for dcc in range(DC):
        xres = xres_r.next()
        cx.load(xres.t[:], x2h[dcc].rearrange("p (t c) -> p t c", c=130)[:, :, 2:130], xres.b)
        pd = [psD.next(), psD.next()]
        for gi in range(3):
            nk = 16 if gi < 2 else FC - 32
            ws = wst.next()
            cx.load(ws.t[:], wD[dcc, gi], ws.b, eng="sp" if nload % 2 == 0 else "pool")
            wb = wbf.next()
            P.op("pool" if nload % 2 == 0 else "dve", lambda e, ws=ws, wb=wb: e.tensor_copy(out=wb.t[:], in_=ws.t[:]),
                 [ws.b], [wb.b])
            nload += 1
            for k in range(nk):
                fc = gi * 16 + k
                for hf in range(2):
                    P.op("pe", lambda e, hf=hf, wb=wb, k=k, fc=fc, pd=pd: e.matmul(
                        pd[hf].t[:], lhsT=wb.t[:, k, :], rhs=actT.t[:, fc, hf * 512:(hf + 1) * 512],
                        start=(fc == 0), stop=(fc == FC - 1)), [wb.b, actT.b], [pd[hf].b])
        xo = xo_r.next()
        for hf in range(2):
            P.op("dve", lambda e, hf=hf, xo=xo, xres=xres, pd=pd: e.tensor_tensor(
                out=xo.t[:, hf * 512:(hf + 1) * 512], in0=pd[hf].t[:],
                in1=xres.t[:, hf * 4:(hf + 1) * 4, :].rearrange("p t c -> p (t c)"), op=ALU.add), [pd[hf].b, xres.b], [xo.b])
        cx.store(x3T[dcc], xo.t[:], xo.b)
    return cx.finish()


def prep_C_weights(inp, l):
    wup = inp["ffn_w_up"][l]
    wU = np.ascontiguousarray(wup.reshape(DC, 128, 2 * FC, 128).transpose(2, 1, 0, 3))
    cwv = np.concatenate([inp["ffn_conv_w"][l], inp["ffn_conv_b"][l][None]], 0)
    cw = np.ascontiguousarray(cwv.reshape(4, 2 * FC, 128).transpose(2, 1, 0))
    wd = inp["ffn_w_down"][l]
    wdp = np.zeros((48 * 128, D_MODEL), np.float32)
    wdp[:D_FF] = wd
    wD = np.ascontiguousarray(wdp.reshape(3, 16, 128, DC, 128).transpose(3, 0, 2, 1, 4))
    rc, cm = rope_consts()
    return dict(wU=wU, cw=cw, wD=wD, cm=cm, gnorm=np.ascontiguousarray(inp["ffn_norm"][l].reshape(DC, 128).T))


def halo_cols(xfull_T, c):
    out = np.zeros((D_MODEL, TLH), np.float32)
    for i in range(8):
        t0 = (8 * i + c) * 128
        lo = max(t0 - 2, 0)
        out[:, i * 130 + (2 - (t0 - lo)):i * 130 + 130] = xfull_T[:, lo:t0 + 128]
    return np.ascontiguousarray(out.reshape(DC, 128, TLH))


def _mm(cx, out, lhsT, rhs, start, stop, reads, writes, skip=False):
    if skip:
        return cx.P.op("pe", lambda e: e.matmul(out, lhsT=lhsT, rhs=rhs, start=start, stop=stop, skip_group_check=True),
                       reads, writes)
    return cx.P.op("pe", lambda e: e.matmul(out, lhsT=lhsT, rhs=rhs, start=start, stop=stop), reads, writes)


def _tr(cx, out, in_, ident, reads, writes):
    return cx.P.op("pe", lambda e: e.transpose(out=out, in_=in_, identity=ident), reads, writes)


def _act(cx, out, in_, func, reads, writes, **kw):
    return cx.P.op("act", lambda e: e.activation(out=out, in_=in_, func=func, **kw), reads, writes)


def _tt(cx, eng, out, in0, in1, op, reads, writes):
    return cx.P.op(eng, lambda e: e.tensor_tensor(out=out, in0=in0, in1=in1, op=op), reads, writes)


def _ts(cx, eng, out, in0, s1, s2, op0, op1, reads, writes, accum_out=None):
    if op1 is None:
        return cx.P.op(eng, lambda e: e.tensor_scalar(out=out, in0=in0, scalar1=s1, scalar2=None, op0=op0), reads, writes)
    if accum_out is not None:
        return cx.P.op(eng, lambda e: e.tensor_scalar(out=out, in0=in0, scalar1=s1, scalar2=s2, op0=op0, op1=op1,
                                                      accum_out=accum_out), reads, writes)
    return cx.P.op(eng, lambda e: e.tensor_scalar(out=out, in0=in0, scalar1=s1, scalar2=s2, op0=op0, op1=op1), reads, writes)


def _stt(cx, out, in0, scalar, in1, op0, op1, reads, writes):
    return cx.P.op("dve", lambda e: e.scalar_tensor_tensor(out=out, in0=in0, scalar=scalar, in1=in1, op0=op0, op1=op1),
                   reads, writes)


def _cp(cx, eng, out, in_, reads, writes):
    if eng == "act":
        return cx.P.op("act", lambda e: e.copy(out=out, in_=in_), reads, writes)
    return cx.P.op(eng, lambda e: e.tensor_copy(out=out, in_=in_), reads, writes)


def _recip(cx, out, in_, reads, writes):
    return cx.P.op("dve", lambda e: e.reciprocal(out=out, in_=in_), reads, writes)


def _memset(cx, eng, out, val, writes):
    return cx.P.op(eng, lambda e: e.memset(out, val), [], writes)


class Scope:
    def __init__(self, cx):
        self.cx = cx

    def __enter__(self):
        self.saved = self.cx.st
        self.cx.st = ExitStack()
        return self

    def __exit__(self, *a):
        cx = self.cx
        cx.st.close()
        cx.st = self.saved
        last = {}
        dmas = []
        for o in cx.P.ops:
            if o.dma:
                dmas.append(o)
            else:
                last[o.eng] = o
        cx.fence = list(last.values()) + dmas[-Prog.NDS:]
        return False


def _fenced_tb(cx, tb):
    f = getattr(cx, "fence", None)
    if f:
        tb.b.rs = list(f)
    return tb


_orig_sb = Ctx.sb
_orig_ps = Ctx.ps
Ctx.sb = lambda self, shape, dt, name=None: _fenced_tb(self, _orig_sb(self, shape, dt, name))
Ctx.ps = lambda self, shape, dt=F32, name=None: _fenced_tb(self, _orig_ps(self, shape, dt, name))


NKT = 64
OFFK = [0]
for _kt in range(NKT):
    OFFK.append(OFFK[-1] + 8 - _kt // 8)
NSLOT = OFFK[-1]
QCH = {"mq": 0, "dq": 6, "sq": 10, "iq": 16}
KCH = {"mk": 0, "dk": 6, "sk": 10, "ik": 16}
NQC = 20
NKC = 17
BIG = 1.0e30
BIGB = 30000.0


def attn_pass(cx, env, KT, krows, QT, qrows, Vt, scale, key_tiles, mode, O, R, maskT=None, biasT=None, selrows=None,
              suffix=True):
    pss, ptr = env["pss"], env["ptr"]
    ones, Mj = env["ones"], env["Mj"]
    nkt = len(key_tiles)
    for n, kt in enumerate(key_tiles):
        imin = (kt // 8) if suffix else 0
        c0 = imin * 128
        groups = []
        a = c0
        while a < TL:
            b = min(TL, (a // 512 + 1) * 512)
            groups.append((a, b))
            a = b
        pt = ptr.next()
        for (a, b) in groups:
            ps = pss.next()
            w = b - a
            last = (mode != "moba")
            _mm(cx, ps.t[:, 0:w], KT.t[krows, kt * 128:(kt + 1) * 128], QT.t[qrows, a:b], True, last,
                [KT.b, QT.b], [ps.b])
            if mode == "moba":
                nb = kt // 2
                _mm(cx, ps.t[:, 0:w], selrows.t[0:32, nb:nb + 1].to_broadcast([32, 128]), biasT.t[:, a:b], False, True,
                    [selrows.b, biasT.b], [ps.b])
            _act(cx, pt.t[:, a:b], ps.t[:, 0:w], AF.Exp, [ps.b], [pt.b], scale=scale)
        if mode in ("causal", "moba"):
            j = kt % 8
            _tt(cx, "pool", pt.t[:, c0:c0 + 128], pt.t[:, c0:c0 + 128], Mj.t[:, j, :], ALU.mult, [pt.b, Mj.b], [pt.b])
        elif mode == "dsa":
            nq = TL - c0
            mv = maskT.t[:, OFFK[kt]:OFFK[kt] + nq // 128, :].rearrange("p s q -> p (s q)")
            _tt(cx, "dve", pt.t[:, c0:TL], pt.t[:, c0:TL], mv, ALU.mult, [pt.b, maskT.b], [pt.b])
        for (a, b) in groups:
            bank = a // 512
            o0 = a - bank * 512
            w = b - a
            _mm(cx, O[bank].t[:, o0:o0 + w], Vt.t[:, n, :], pt.t[:, a:b], n == 0, n == nkt - 1, [Vt.b, pt.b], [O[bank].b],
                skip=True)
            _mm(cx, R[bank].t[:, o0:o0 + w], ones.t[:], pt.t[:, a:b], n == 0, n == nkt - 1, [ones.b, pt.b], [R[bank].b],
                skip=True)


def build_B(dbg=False):
    cx = Ctx()
    P = cx.P
    xT = cx.dram("xT", [DC, 128, TL], F32)
    qT = cx.dram("qT", [NQC, 128, TL], BF16)
    kT = cx.dram("kT", [NKC, 128, 8192], BF16)
    vG = cx.dram("vG", [16, 128, NKT, 128], BF16)
    iwd = cx.dram("iw", [TL, 8], F32)
    pcd = cx.dram("pc", [128, 4], F32)
    wOd = cx.dram("wO", [DC, 128, DC, 128], F32)
    dld = cx.dram("dl", [1, 256], F32)
    sld = cx.dram("sl", [128, 1], F32)
    cmd = cx.dram("cm", [4, 128, 128], F32)
    gcd = cx.dram("gc", [128, 2 * DC + 2], F32)
    memd = cx.dram("memT", [DC, 128, 256], F32)
    wqd = cx.dram("wq", [4, 128, DC, 128], F32)
    wkd = cx.dram("wk", [4, 128, DC, 128], F32)
    wvd = cx.dram("wv", [4, 128, DC, 128], F32)
    wcod = cx.dram("wco", [DC, 128, 4, 128], F32)
    x2T = cx.dram("x2T", [DC, 128, TL], F32, out=True)

    pc = cx.sb([128, 4], F32, "pc")
    cx.load(pc.t[:], pcd, pc.b)
    cm = cx.sb([128, 4, 128], F32, "cm")
    for k in range(4):
        cx.load(cm.t[:, k, :], cmd[k], cm.b)
    epsb = cx.sb([128, 1], F32, "epsb")
    _memset(cx, "dve", epsb.t[:], EPS, [epsb.b])
    dkq = cx.sb([128, 128], F32, "dkq")
    P.op("pool", lambda e: e.iota(dkq.t[:], [[-1, 128]], base=0, channel_multiplier=1,
                                  allow_small_or_imprecise_dtypes=True), [], [dkq.b])
    ident = cx.sb([128, 128], BF16, "ident")
    _ts(cx, "dve", ident.t[:], dkq.t[:], 0.0, None, ALU.is_equal, None, [dkq.b], [ident.b])
    identf = cx.sb([128, 128], F32, "identf")
    _ts(cx, "dve", identf.t[:], dkq.t[:], 0.0, None, ALU.is_equal, None, [dkq.b], [identf.b])
    ones = cx.sb([128, 128], BF16, "ones")
    _memset(cx, "dve", ones.t[:], 1.0, [ones.b])
    thrj = cx.sb([128, 8], F32, "thrj")
    for j in range(8):
        _ts(cx, "dve", thrj.t[:, j:j + 1], pc.t[:, 0:1], 128.0, -128.0 * j, ALU.mult, ALU.add, [pc.b], [thrj.b])
    Mj = cx.sb([128, 8, 128], BF16, "Mj")
    for j in range(8):
        _ts(cx, "dve", Mj.t[:, j, :], dkq.t[:], thrj.t[:, j:j + 1], None, ALU.is_le, None, [dkq.b, thrj.b], [Mj.b])
    iwt = cx.sb([128, 8, 8], F32, "iwt")
    cx.load(iwt.t[:], iwd.rearrange("(i p) h -> p i h", p=128), iwt.b)
    dl = cx.sb([128, 256], F32, "dl")
    cx.load(dl.t[:], dld.partition_broadcast(128), dl.b)
    lam = cx.sb([128, 4], F32, "lam")
    dlp = cx.sb([128, 2, 64], F32, "dlp")
    _tt(cx, "dve", dlp.t[:, 0, :], dl.t[:, 0:64], dl.t[:, 64:128], ALU.mult, [dl.b], [dlp.b])
    _tt(cx, "dve", dlp.t[:, 1, :], dl.t[:, 128:192], dl.t[:, 192:256], ALU.mult, [dl.b], [dlp.b])
    P.op("dve", lambda e: e.reduce_sum(out=lam.t[:, 0:2], in_=dlp.t[:], axis=AX.X), [dlp.b], [lam.b])
    _act(cx, lam.t[:, 0:2], lam.t[:, 0:2], AF.Exp, [lam.b], [lam.b])
    _tt(cx, "dve", lam.t[:, 2:3], lam.t[:, 0:1], lam.t[:, 1:2], ALU.subtract, [lam.b], [lam.b])
    _ts(cx, "dve", lam.t[:, 3:4], lam.t[:, 2:3], pc.t[:, 2:3], -1.0, ALU.add, ALU.mult, [lam.b, pc.b], [lam.b])
    sl = cx.sb([128, 1], F32, "sl")
    cx.load(sl.t[:], sld, sl.b)
    slg = cx.sb([128, 1], F32, "slg")
    _tt(cx, "dve", slg.t[:], sl.t[:], pc.t[:, 3:4], ALU.mult, [sl.b, pc.b], [slg.b])
    gc = cx.sb([128, 2 * DC + 2], F32, "gc")
    cx.load(gc.t[:], gcd, gc.b)

    mixedT = cx.sb([128, DC, TL], BF16, "mixedT")
    outer = Scope(cx)
    outer.__enter__()
    maskT = cx.sb([128, NSLOT, 128], BF16, "maskT")

    with Scope(cx):
        pss = cx.ring(4, [128, 512], F32, "pss", psum=True)
        pst = cx.ring(2, [128, 4, 128], BF16, "pst", psum=True)
        MTj = cx.sb([128, 8, 128], F32, "MTj")
        NEGj = cx.sb([128, 8, 128], F32, "NEGj")
        POSj = cx.sb([128, 8, 128], F32, "POSj")
        for j in range(8):
            _ts(cx, "dve", MTj.t[:, j, :], dkq.t[:], -1.0, thrj.t[:, j:j + 1], ALU.mult, ALU.is_le, [dkq.b, thrj.b], [MTj.b])
        _ts(cx, "dve", NEGj.t[:], MTj.t[:], -1.0, BIG, ALU.add, ALU.mult, [MTj.b], [NEGj.b])
        _ts(cx, "dve", POSj.t[:], NEGj.t[:], -1.0, None, ALU.mult, None, [NEGj.b], [POSj.b])
        kdup = cx.sb([128, 8192], BF16, "kdup")
        cx.load(kdup.t[0:64, :], kT[KCH["ik"], 0:64, :], kdup.b)
        cx.load(kdup.t[64:128, :], kT[KCH["ik"], 0:64, :], kdup.b, eng="pool")
        qiT = cx.sb([128, 4, TL], BF16, "qiT")
        for k in range(4):
            cx.load(qiT.t[:, k, :], qT[QCH["iq"] + k], qiT.b)
        Isc = cx.sb([128, 8192], F32, "Isc")
        msk = cx.sb([128, 8192], BF16, "msk")
        rl_r = cx.ring(3, [128, 512], F32, "rl")
        st_r = cx.ring(2, [128, 8], F32, "bst")
        dg_r = cx.ring(1, [128, 1024], F32, "dg")
        for i in range(8):
            nk = 8 * i + 8
            nkeys = nk * 128
            for cg in range(nk // 4):
                ks = slice(cg * 512, (cg + 1) * 512)
                for h in range(8):
                    rows = slice(64 * (h % 2), 64 * (h % 2) + 64)
                    ps = pss.next()
                    _mm(cx, ps.t[:], qiT.t[rows, h // 2, i * 128:(i + 1) * 128], kdup.t[rows, ks], True, True,
                        [qiT.b, kdup.b], [ps.b])
                    rl = rl_r.next()
                    _act(cx, rl.t[:], ps.t[:], AF.Relu, [ps.b], [rl.b])
                    if h == 0:
                        _ts(cx, "dve", Isc.t[:, ks], rl.t[:], iwt.t[:, i, 0:1], None, ALU.mult, None, [rl.b, iwt.b], [Isc.b])
                    else:
                        _stt(cx, Isc.t[:, ks], rl.t[:], iwt.t[:, i, h:h + 1], Isc.t[:, ks], ALU.mult, ALU.add,
                             [rl.b, iwt.b, Isc.b], [Isc.b])
            dsl = slice(nkeys - 1024, nkeys)
            dg = dg_r.next()
            _tt(cx, "dve", dg.t[:], Isc.t[:, dsl], MTj.t[:].rearrange("p j k -> p (j k)"), ALU.mult, [Isc.b, MTj.b], [dg.b])
            _tt(cx, "dve", Isc.t[:, dsl], dg.t[:], NEGj.t[:].rearrange("p j k -> p (j k)"), ALU.add, [dg.b, NEGj.b], [Isc.b])
            _tt(cx, "dve", dg.t[:], dg.t[:], POSj.t[:].rearrange("p j k -> p (j k)"), ALU.add, [dg.b, POSj.b], [dg.b])
            bs = st_r.next()
            P.op("dve", lambda e, bs=bs, nkeys=nkeys: e.tensor_reduce(out=bs.t[:, 1:2], in_=Isc.t[:, 0:nkeys], axis=AX.X,
                                                                     op=ALU.max), [Isc.b], [bs.b])
            P.op("dve", lambda e, bs=bs, dg=dg: e.tensor_reduce(out=bs.t[:, 0:1], in_=dg.t[:], axis=AX.X, op=ALU.min),
                 [dg.b], [bs.b])
            if i > 0:
                P.op("dve", lambda e, bs=bs, nkeys=nkeys: e.tensor_reduce(out=bs.t[:, 5:6], in_=Isc.t[:, 0:nkeys - 1024],
                                                                         axis=AX.X, op=ALU.min), [Isc.b], [bs.b])
                _tt(cx, "dve", bs.t[:, 0:1], bs.t[:, 0:1], bs.t[:, 5:6], ALU.min, [bs.b], [bs.b])
            _ts(cx, "dve", bs.t[:, 1:2], bs.t[:, 1:2], 1.0, None, ALU.add, None, [bs.b], [bs.b])
            for it in range(24):
                _tt(cx, "dve", bs.t[:, 2:3], bs.t[:, 0:1], bs.t[:, 1:2], ALU.add, [bs.b], [bs.b])
                _ts(cx, "dve", bs.t[:, 2:3], bs.t[:, 2:3], 0.5, None, ALU.mult, None, [bs.b], [bs.b])
                _memset(cx, "dve", bs.t[:, 3:4], 0.0, [bs.b])
                _ts(cx, "dve", msk.t[:, 0:nkeys], Isc.t[:, 0:nkeys], bs.t[:, 2:3], 0.0, ALU.is_ge, ALU.add, [Isc.b, bs.b],
                    [msk.b, bs.b], accum_out=bs.t[:, 3:4])
                _ts(cx, "dve", bs.t[:, 4:5], bs.t[:, 3:4], 256.0, None, ALU.is_ge, None, [bs.b], [bs.b])
                _tt(cx, "dve", bs.t[:, 5:6], bs.t[:, 2:3], bs.t[:, 0:1], ALU.subtract, [bs.b], [bs.b])
                _stt(cx, bs.t[:, 0:1], bs.t[:, 5:6], bs.t[:, 4:5], bs.t[:, 0:1], ALU.mult, ALU.add, [bs.b], [bs.b])
                _tt(cx, "dve", bs.t[:, 5:6], bs.t[:, 1:2], bs.t[:, 2:3], ALU.subtract, [bs.b], [bs.b])
                _stt(cx, bs.t[:, 1:2], bs.t[:, 5:6], bs.t[:, 4:5], bs.t[:, 2:3], ALU.mult, ALU.add, [bs.b], [bs.b])
            _ts(cx, "dve", msk.t[:, 0:nkeys], Isc.t[:, 0:nkeys], bs.t[:, 0:1], None, ALU.is_ge, None, [Isc.b, bs.b], [msk.b])
            for g in range(nk // 4):
                pT = pst.next()
                for jj in range(4):
                    kt = g * 4 + jj
                    _tr(cx, pT.t[:, jj, :], msk.t[:, kt * 128:(kt + 1) * 128], ident.t[:], [msk.b, ident.b], [pT.b])
                for jj in range(4):
                    kt = g * 4 + jj
                    slot = OFFK[kt] + (i - kt // 8)
                    _cp(cx, "act" if jj % 2 else "dve", maskT.t[:, slot, :], pT.t[:, jj, :], [pT.b], [maskT.b])

    with Scope(cx):
        env = dict(pss=cx.ring(4, [128, 512], F32, "pss", psum=True), ptr=cx.ring(3, [128, TL], BF16, "pt"),
                   ones=ones, Mj=Mj)
        O = [cx.ps([128, 512], F32, "O0"), cx.ps([128, 512], F32, "O1")]
        R = [cx.ps([128, 512], F32, "R0"), cx.ps([128, 512], F32, "R1")]
        KT_r = cx.ring(2, [128, 8192], BF16, "KT")
        V_r = cx.ring(2, [128, NKT, 128], BF16, "V")
        QT_r = cx.ring(2, [128, TL], BF16, "QT")
        rinv_r = cx.ring(1, [128, TL], F32, "rinv")
        o1 = cx.sb([128, TL], F32, "o1")
        o2 = cx.sb([128, TL], F32, "o2")
        sqd = rinv_r.items[0]
        nidx = cx.sb([128, 32], F32, "nidx")
        P.op("pool", lambda e: e.iota(nidx.t[:], [[1, 32]], base=0, channel_multiplier=0,
                                      allow_small_or_imprecise_dtypes=True), [], [nidx.b])
        curv = cx.sb([128, 8], F32, "curv")
        for i in range(8):
            _ts(cx, "dve", curv.t[:, i:i + 1], pc.t[:, 1:2], 4.0 * i, None, ALU.add, None, [pc.b], [curv.b])
        selrows = ident
        biasT = cx.sb([32, TL], BF16, "biasT")
        kb = cx.sb([128, 32], F32, "kb")
        qf_r = cx.ring(2, [128, 128], F32, "qf")
        gt = cx.sb([128, 6, 32], F32, "gt")
        m8 = cx.sb([128, 8], F32, "m8")
        all_kt = list(range(NKT))

        def load_head(kc, krow_all, qc, vh):
            KT = KT_r.next()
            cx.load(KT.t[:], kT[kc], KT.b, eng="sp")
            Vt = V_r.next()
            cx.load(Vt.t[:], vG[vh], Vt.b, eng="pool")
            QT = QT_r.next()
            cx.load(QT.t[:], qT[qc], QT.b, eng="sp")
            return KT, Vt, QT

        def finalize(dst_ap, dst_b, to_f32_tb=None):
            ri = rinv_r.next()
            for bk in range(2):
                sl_ = slice(bk * 512, (bk + 1) * 512)
                _recip(cx, ri.t[:, sl_], R[bk].t[:], [R[bk].b], [ri.b])
                if to_f32_tb is None:
                    _tt(cx, "dve", dst_ap[:, sl_], O[bk].t[:], ri.t[:, sl_], ALU.mult, [O[bk].b, ri.b], [dst_b])
                else:
                    _tt(cx, "dve", to_f32_tb.t[:, sl_], O[bk].t[:], ri.t[:, sl_], ALU.mult, [O[bk].b, ri.b], [to_f32_tb.b])

        for h in range(6):
            KT, Vt, QT = load_head(KCH["sk"] + h, None, QCH["sq"] + h, 10 + h)
            attn_pass(cx, env, KT, slice(0, 128), QT, slice(0, 128), Vt, 128 ** -0.5, all_kt, "dsa", O, R, maskT=maskT)
            finalize(mixedT.t[:, 10 + h, :], mixedT.b)
        for h in range(6):
            KT, Vt, QT = load_head(KCH["mk"] + h, None, QCH["mq"] + h, h)
            P.op("dve", lambda e, KT=KT: e.tensor_reduce(out=kb.t[:], in_=KT.t[:].rearrange("p (n k) -> p n k", k=256),
                                                         axis=AX.X, op=ALU.add), [KT.b], [kb.b])
            for i in range(8):
                qf = qf_r.next()
                _cp(cx, "pool", qf.t[:], QT.t[:, i * 128:(i + 1) * 128], [QT.b], [qf.b])
                ps = env["pss"].next()
                _mm(cx, ps.t[:, 0:32], qf.t[:], kb.t[:], True, True, [qf.b, kb.b], [ps.b])
                _ts(cx, "dve", gt.t[:, 0, :], nidx.t[:], curv.t[:, i:i + 1], None, ALU.is_lt, None, [nidx.b, curv.b], [gt.b])
                _ts(cx, "dve", gt.t[:, 1, :], nidx.t[:], curv.t[:, i:i + 1], None, ALU.is_equal, None, [nidx.b, curv.b], [gt.b])
                _tt(cx, "dve", gt.t[:, 2, :], ps.t[:, 0:32], gt.t[:, 0, :], ALU.mult, [ps.b, gt.b], [gt.b])
                _ts(cx, "dve", gt.t[:, 3, :], gt.t[:, 0, :], -1.0, BIG, ALU.add, ALU.mult, [gt.b], [gt.b])
                _tt(cx, "dve", gt.t[:, 2, :], gt.t[:, 2, :], gt.t[:, 3, :], ALU.add, [gt.b], [gt.b])
                P.op("dve", lambda e: e.max(out=m8.t[:], in_=gt.t[:, 2, :]), [gt.b], [m8.b])
                _ts(cx, "dve", gt.t[:, 4, :], gt.t[:, 2, :], m8.t[:, 2:3], None, ALU.is_ge, None, [gt.b, m8.b], [gt.b])
                _tt(cx, "dve", gt.t[:, 4, :], gt.t[:, 4, :], gt.t[:, 0, :], ALU.mult, [gt.b], [gt.b])
                _tt(cx, "dve", gt.t[:, 4, :], gt.t[:, 4, :], gt.t[:, 1, :], ALU.add, [gt.b], [gt.b])
                _ts(cx, "dve", gt.t[:, 5, :], gt.t[:, 4, :], -1.0, BIGB, ALU.add, ALU.mult, [gt.b], [gt.b])
                ps2 = env["pss"].next()
                _mm(cx, ps2.t[0:32, 0:128], gt.t[:, 5, :], identf.t[:], True, True, [gt.b, identf.b], [ps2.b])
                _cp(cx, "act", biasT.t[:, i * 128:(i + 1) * 128], ps2.t[0:32, 0:128], [ps2.b], [biasT.b])
            attn_pass(cx, env, KT, slice(0, 128), QT, slice(0, 128), Vt, 128 ** -0.5, all_kt, "moba", O, R, biasT=biasT,
                      selrows=selrows)
            finalize(mixedT.t[:, h, :], mixedT.b)
        for h in range(4):
            KT, Vt, QT = load_head(KCH["dk"] + h, None, QCH["dq"] + h, 6 + h)
            attn_pass(cx, env, KT, slice(0, 64), QT, slice(0, 64), Vt, 64 ** -0.5, all_kt, "causal", O, R)
            finalize(None, None, to_f32_tb=o1)
            attn_pass(cx, env, KT, slice(64, 128), QT, slice(64, 128), Vt, 64 ** -0.5, all_kt, "causal", O, R)
            finalize(None, None, to_f32_tb=o2)
            _stt(cx, o1.t[:], o2.t[:], lam.t[:, 3:4], o1.t[:], ALU.mult, ALU.add, [o2.b, lam.b, o1.b], [o1.b])
            _act(cx, sqd.t[:], o1.t[:], AF.Square, [o1.b], [sqd.b])
            for bk in range(2):
                sl_ = slice(bk * 512, (bk + 1) * 512)
                ps = env["pss"].next()
                _mm(cx, ps.t[:], cm.t[:, 2, :], sqd.t[:, sl_], True, True, [cm.b, sqd.b], [ps.b])
                _act(cx, o2.t[:, sl_], ps.t[:], AF.Sqrt, [ps.b, epsb.b], [o2.b], bias=epsb.t[:, 0:1])
            _recip(cx, o2.t[:], o2.t[:], [o2.b], [o2.b])
            _stt(cx, mixedT.t[:, 6 + h, :], o1.t[:], slg.t[:, 0:1], o2.t[:], ALU.mult, ALU.mult, [o1.b, slg.b, o2.b], [mixedT.b])

    outer.__exit__(None, None, None)
    if dbg:
        dM = cx.dram("dbgM", [128, DC, TL], BF16, out=True)
        cx.store(dM, mixedT.t[:], mixedT.b)

    with Scope(cx):
        pss = cx.ring(4, [128, 512], F32, "pss", psum=True)
        env = dict(pss=pss, ptr=cx.ring(2, [128, TL], BF16, "pt"), ones=ones, Mj=Mj)
        O = [cx.ps([128, 512], F32, "O0"), cx.ps([128, 512], F32, "O1")]
        R = [cx.ps([128, 512], F32, "R0"), cx.ps([128, 512], F32, "R1")]
        x1T = cx.sb([128, DC, TL], F32, "x1T")
        h2T = cx.sb([128, DC, TL], BF16, "h2T")
        wst = cx.ring(1, [128, DC, 128], F32, "wst")
        wbf = cx.ring(2, [128, DC, 128], BF16, "wbf")
        xr = cx.ring(2, [128, TL], F32, "xr")
        sq_r = cx.ring(1, [128, TL], F32, "sq")
        nld = 0
        for dmc in range(DC):
            ws = wst.next()
            cx.load(ws.t[:], wOd[dmc], ws.b, eng="sp" if nld % 2 == 0 else "pool")
            wb = wbf.next()
            _cp(cx, "pool" if nld % 2 == 0 else "dve", wb.t[:], ws.t[:], [ws.b], [wb.b])
            nld += 1
            x_ = xr.next()
            cx.load(x_.t[:], xT[dmc], x_.b)
            for hf in range(2):
                sl_ = slice(hf * 512, (hf + 1) * 512)
                ps = pss.next()
                for hc in range(DC):
                    _mm(cx, ps.t[:], wb.t[:, hc, :], mixedT.t[:, hc, sl_], hc == 0, hc == DC - 1, [wb.b, mixedT.b], [ps.b])
                _tt(cx, "dve", x1T.t[:, dmc, sl_], ps.t[:], x_.t[:, sl_], ALU.add, [ps.b, x_.b], [x1T.b])
            sq = sq_r.next()
            _act(cx, sq.t[:], x1T.t[:, dmc, :], AF.Square, [x1T.b], [sq.b])
            for hf in range(2):
                _mm(cx, R[hf].t[:], cm.t[:, 2, :], sq.t[:, hf * 512:(hf + 1) * 512], dmc == 0, dmc == DC - 1, [cm.b, sq.b],
                    [R[hf].b])
        rstd = cx.sb([128, TL], F32, "rstd")
        for hf in range(2):
            _act(cx, rstd.t[:, hf * 512:(hf + 1) * 512], R[hf].t[:], AF.Sqrt, [R[hf].b, epsb.b], [rstd.b], scale=1.0 / DC,
                 bias=epsb.t[:, 0:1])
        _recip(cx, rstd.t[:], rstd.t[:], [rstd.b], [rstd.b])
        for dc in range(DC):
            _stt(cx, h2T.t[:, dc, :], x1T.t[:, dc, :], gc.t[:, dc:dc + 1], rstd.t[:], ALU.mult, ALU.mult,
                 [x1T.b, gc.b, rstd.b], [h2T.b])
        memr = cx.ring(2, [128, 256], F32, "memr")
        msqr = cx.ring(2, [128, 256], F32, "msqr")
        psm = pss.next()
        for dc in range(DC):
            mf = memr.next()
            cx.load(mf.t[:], memd[dc], mf.b)
            mq_ = msqr.next()
            _act(cx, mq_.t[:], mf.t[:], AF.Square, [mf.b], [mq_.b])
            _mm(cx, psm.t[:, 0:256], cm.t[:, 2, :], mq_.t[:], dc == 0, dc == DC - 1, [cm.b, mq_.b], [psm.b])
        rm = cx.sb([128, 256], F32, "rm")
        _act(cx, rm.t[:], psm.t[:, 0:256], AF.Sqrt, [psm.b, epsb.b], [rm.b], scale=1.0 / DC, bias=epsb.t[:, 0:1])
        _recip(cx, rm.t[:], rm.t[:], [rm.b], [rm.b])
        mT = cx.sb([128, DC, 256], BF16, "mT")
        for dc in range(DC):
            mf = memr.next()
            cx.load(mf.t[:], memd[dc], mf.b)
            _stt(cx, mT.t[:, dc, :], mf.t[:], gc.t[:, DC + dc:DC + dc + 1], rm.t[:], ALU.mult, ALU.mult,
                 [mf.b, gc.b, rm.b], [mT.b])
        ckT = cx.sb([128, 4, 256], BF16, "ckT")
        cv = cx.sb([128, 4, 2, 128], BF16, "cv")
        cqT = TB(mixedT.t[:, 0:4, :], "cqT")
        coT = TB(mixedT.t[:, 4:8, :], "coT")
        for _tb in (cqT, coT):
            _tb.b.rs = list(mixedT.b.rs) + ([mixedT.b.w] if mixedT.b.w is not None else [])
        tmpf = cx.ring(1, [128, 512], F32, "tmpf")
        tmps = cx.ring(1, [128, 512], F32, "tmps")
        tmpr = cx.ring(1, [128, 512], F32, "tmpr")

        def headnorm(ps, w, gcol, dst_ap, dst_b):
            xf = tmpf.next()
            _cp(cx, "act", xf.t[:, 0:w], ps.t[:, 0:w], [ps.b], [xf.b])
            s2 = tmps.next()
            _act(cx, s2.t[:, 0:w], xf.t[:, 0:w], AF.Square, [xf.b], [s2.b])
            pm = pss.next()
            _mm(cx, pm.t[:, 0:w], cm.t[:, 2, :], s2.t[:, 0:w], True, True, [cm.b, s2.b], [pm.b])
            rr = tmpr.next()
            _act(cx, rr.t[:, 0:w], pm.t[:, 0:w], AF.Sqrt, [pm.b, epsb.b], [rr.b], bias=epsb.t[:, 0:1])
            _recip(cx, rr.t[:, 0:w], rr.t[:, 0:w], [rr.b], [rr.b])
            _stt(cx, dst_ap, xf.t[:, 0:w], gc.t[:, gcol:gcol + 1], rr.t[:, 0:w], ALU.mult, ALU.mult, [xf.b, gc.b, rr.b], [dst_b])

        for h in range(4):
            ws = wst.next()
            cx.load(ws.t[:], wkd[h], ws.b)
            wb = wbf.next()
            _cp(cx, "pool", wb.t[:], ws.t[:], [ws.b], [wb.b])
            ps = pss.next()
            for dc in range(DC):
                _mm(cx, ps.t[:, 0:256], wb.t[:, dc, :], mT.t[:, dc, :], dc == 0, dc == DC - 1, [wb.b, mT.b], [ps.b])
            headnorm(ps, 256, 2 * DC + 1, ckT.t[:, h, :], ckT.b)
            ws = wst.next()
            cx.load(ws.t[:], wvd[h], ws.b)
            wb = wbf.next()
            _cp(cx, "dve", wb.t[:], ws.t[:], [ws.b], [wb.b])
            for mt in range(2):
                ps = pss.next()
                for dc in range(DC):
                    _mm(cx, ps.t[:, 0:128], mT.t[:, dc, mt * 128:(mt + 1) * 128], wb.t[:, dc, :], dc == 0, dc == DC - 1,
                        [wb.b, mT.b], [ps.b])
                _cp(cx, "act", cv.t[:, h, mt, :], ps.t[:, 0:128], [ps.b], [cv.b])
            ws = wst.next()
            cx.load(ws.t[:], wqd[h], ws.b)
            wb = wbf.next()
            _cp(cx, "pool", wb.t[:], ws.t[:], [ws.b], [wb.b])
            for hf in range(2):
                sl_ = slice(hf * 512, (hf + 1) * 512)
                ps = pss.next()
                for dc in range(DC):
                    _mm(cx, ps.t[:], wb.t[:, dc, :], h2T.t[:, dc, sl_], dc == 0, dc == DC - 1, [wb.b, h2T.b], [ps.b])
                headnorm(ps, 512, 2 * DC, cqT.t[:, h, sl_], cqT.b)
        for h in range(4):
            KTv = TB(ckT.t[:, h, :])
            KTv.b = ckT.b
            QTv = TB(cqT.t[:, h, :])
            QTv.b = cqT.b
            Vv = TB(cv.t[:, h, :, :])
            Vv.b = cv.b
            attn_pass(cx, env, KTv, slice(0, 128), QTv, slice(0, 128), Vv, 128 ** -0.5, [0, 1], "none", O, R, suffix=False)
            ri = tmpf.next()
            ri2 = tmps.next()
            for bk, rt in ((0, ri), (1, ri2)):
                sl_ = slice(bk * 512, (bk + 1) * 512)
                _recip(cx, rt.t[:], R[bk].t[:], [R[bk].b], [rt.b])
                _tt(cx, "dve", coT.t[:, h, sl_], O[bk].t[:], rt.t[:], ALU.mult, [O[bk].b, rt.b], [coT.b])
        wcs = cx.ring(2, [128, 4, 128], F32, "wcs")
        wcb = cx.ring(2, [128, 4, 128], BF16, "wcb")
        xo_r = xr
        for dmc in range(DC):
            ws = wcs.next()
            cx.load(ws.t[:], wcod[dmc], ws.b)
            wb = wcb.next()
            _cp(cx, "pool", wb.t[:], ws.t[:], [ws.b], [wb.b])
            xo = xo_r.next()
            for hf in range(2):
                sl_ = slice(hf * 512, (hf + 1) * 512)
                ps = pss.next()
                for hc in range(4):
                    _mm(cx, ps.t[:], wb.t[:, hc, :], coT.t[:, hc, sl_], hc == 0, hc == 3, [wb.b, coT.b], [ps.b])
                _tt(cx, "dve", xo.t[:, sl_], ps.t[:], x1T.t[:, dmc, sl_], ALU.add, [ps.b, x1T.b], [xo.b])
            cx.store(x2T[dmc], xo.t[:], xo.b)
    return cx.finish()


def prep_B_weights(inp, l):
    import math
    wO = np.stack([wtile(inp["w_out"][l][:, 128 * k:128 * (k + 1)]) for k in range(DC)])
    gc = np.zeros((128, 2 * DC + 2), np.float32)
    gc[:, 0:DC] = inp["cross_norm"][l].reshape(DC, 128).T
    gc[:, DC:2 * DC] = inp["mem_norm"][l].reshape(DC, 128).T
    gc[:, 2 * DC] = inp["cross_qk_gain"][l, 0]
    gc[:, 2 * DC + 1] = inp["cross_qk_gain"][l, 1]
    wq = np.stack([wtile(inp["cross_wq"][l][:, 128 * h:128 * (h + 1)]) for h in range(4)])
    wk = np.stack([wtile(inp["cross_wkv"][l][:, 128 * h:128 * (h + 1)]) for h in range(4)])
    wv = np.stack([wtile(inp["cross_wkv"][l][:, 512 + 128 * h:512 + 128 * (h + 1)]) for h in range(4)])
    wo = inp["cross_wo"][l]
    wco = np.ascontiguousarray(wo.reshape(4, 128, DC, 128).transpose(2, 1, 0, 3))
    rc, cm = rope_consts()
    lam_init = 0.8 - 0.6 * math.exp(-0.3 * l)
    return dict(wO=wO, gc=gc, wq=wq, wk=wk, wv=wv, wco=wco, cm=cm, memT=fm(inp["mem"][0]),
                dl=np.ascontiguousarray(inp["diff_lambda"][l].reshape(1, 256)),
                sl=np.ascontiguousarray(inp["diff_subln"][l].reshape(128, 1))), lam_init


def glue_A_to_B(resA):
    kT = np.zeros((NKC, 128, 8192), ml_dtypes.bfloat16)
    vfull = np.zeros((8192, 2048), ml_dtypes.bfloat16)
    qTs = []
    ksrc = ([FIDX["mk"] + k for k in range(6)] + [FIDX["dk"] + k for k in range(4)] + [FIDX["sk"] + k for k in range(6)]
            + [FIDX["ik"]])
    qsrc = ([FIDX["mq"] + k for k in range(6)] + [FIDX["dq"] + k for k in range(4)] + [FIDX["sq"] + k for k in range(6)]
            + [FIDX["iq"] + k for k in range(4)])
    for c in range(NCORES):
        ti = tok_index(c)
        fT = np.asarray(resA[c]["fT"])
        kT[:, :, ti] = fT[ksrc]
        vfull[ti] = np.asarray(resA[c]["vtok"])
        qTs.append(np.ascontiguousarray(fT[qsrc]))
    vG = np.ascontiguousarray(vfull.reshape(NKT, 128, 16, 128).transpose(2, 1, 0, 3))
    return qTs, kT, vG


_PROGS = {}


def _prog(name):
    if name not in _PROGS:
        _PROGS[name] = {"A": build_A, "B": build_B, "C": build_C}[name]()
    return _PROGS[name]


def kernel(**inputs):
    import math
    inp = {k: np.asarray(v) for k, v in inputs.items()}
    cores = list(range(NCORES))
    toks = [tok_index(c) for c in cores]
    x0 = inp["x"][0]
    xT_loc = [fm(x0[toks[c]]) for c in cores]
    pos_loc = [np.ascontiguousarray(inp["positions"][0][toks[c]].reshape(1, TL)).astype(np.int32) for c in cores]
    for l in range(2):
        wa = prep_A_weights(inp, l)
        maps = []
        for c in cores:
            m = dict(wa)
            m["xT"] = xT_loc[c]
            m["pos"] = pos_loc[c]
            maps.append(m)
        resA = run_bass_kernel_spmd(_prog("A"), maps, core_ids=cores).results
        del maps, wa
        qTs, kT, vG = glue_A_to_B(resA)
        wb, lam_init = prep_B_weights(inp, l)
        maps = []
        for c in cores:
            m = dict(wb)
            m["xT"] = xT_loc[c]
            m["qT"] = qTs[c]
            m["kT"] = kT
            m["vG"] = vG
            m["iw"] = np.asarray(resA[c]["iwo"])
            pc = np.zeros((128, 4), np.float32)
            pc[:, 0] = c
            pc[:, 1] = c // 2
            pc[:, 2] = lam_init
            pc[:, 3] = 1.0 - lam_init
            m["pc"] = pc
            maps.append(m)
        resB = run_bass_kernel_spmd(_prog("B"), maps, core_ids=cores).results
        del maps, wb, qTs, kT, vG, resA
        xfull = np.zeros((D_MODEL, 8192), np.float32)
        for c in cores:
            xfull[:, toks[c]] = np.asarray(resB[c]["x2T"]).reshape(D_MODEL, TL)
        wc = prep_C_weights(inp, l)
        maps = []
        for c in cores:
            m = dict(wc)
            m["x2h"] = halo_cols(xfull, c)
            maps.append(m)
        resC = run_bass_kernel_spmd(_prog("C"), maps, core_ids=cores).results
        del maps, wc, resB
        xT_loc = [np.ascontiguousarray(np.asarray(resC[c]["x3T"])) for c in cores]
    out = np.zeros((1, 8192, D_MODEL), np.float32)
    for c in cores:
        out[0, toks[c]] = xT_loc[c].reshape(D_MODEL, TL).T
    return out
```

```python
import numpy as np
import ml_dtypes
from contextlib import ExitStack
import concourse.bass as bass
import concourse.mybir as mybir
from concourse.bass_utils import run_bass_kernel_spmd

F32 = mybir.dt.float32
BF16 = mybir.dt.bfloat16
I32 = mybir.dt.int32
ALU = mybir.AluOpType
AF = mybir.ActivationFunctionType
AX = mybir.AxisListType

NCORES = 8
ENGS = ("pe", "act", "dve", "pool", "sp")


class Buf:
    __slots__ = ("name", "w", "rs")

    def __init__(self, name=""):
        self.name = name
        self.w = None
        self.rs = []


class Op:
    __slots__ = ("eng", "fn", "deps", "sig", "sigidx", "dma", "dn")

    def __init__(self, eng, fn, dma):
        self.eng = eng
        self.fn = fn
        self.deps = []
        self.sig = False
        self.sigidx = -1
        self.dma = dma
        self.dn = -1


class Prog:
    NDS = 40
    CAP = 30000

    def __init__(self, nc):
        self.nc = nc
        self.ops = []
        self.ndma = 0

    def op(self, eng, fn, reads=(), writes=(), dma=False):
        o = Op(eng, fn, dma)
        deps = {}
        for b in reads:
            if b.w is not None:
                deps[id(b.w)] = b.w
        for b in writes:
            if b.w is not None:
                deps[id(b.w)] = b.w
            for r in b.rs:
                deps[id(r)] = r
        deps.pop(id(o), None)
        o.deps = list(deps.values())
        for b in writes:
            b.w = o
            b.rs = []
        for b in reads:
            if not b.rs or b.rs[-1] is not o:
                b.rs.append(o)
        if dma:
            o.dn = self.ndma
            self.ndma += 1
        self.ops.append(o)
        return o

    def dma(self, out, in_, reads=(), writes=(), eng="sp", **kw):
        return self.op(eng, lambda e: e.dma_start(out=out, in_=in_, **kw), reads, writes, dma=True)

    def emit(self, final_bufs=()):
        nc = self.nc
        ops = self.ops
        fin = Op("sp", None, False)
        fd = {}
        for b in final_bufs:
            if b.w is not None:
                fd[id(b.w)] = b.w
        fin.deps = list(fd.values())
        ops = ops + [fin]
        for o in ops:
            for p in o.deps:
                if not p.dma:
                    if p.eng == "pe" and o.eng == "pe" and not o.dma:
                        continue
                    p.sig = True
        cnt = {e: 0 for e in ENGS}
        for o in ops:
            if o.sig:
                o.sigidx = cnt[o.eng]
                cnt[o.eng] += 1
        with ExitStack() as st:
            esems = {}
            for e in ENGS:
                n = (cnt[e] + self.CAP - 1) // self.CAP
                esems[e] = [st.enter_context(nc.semaphore(f"s_{e}{k}")) for k in range(n)]
            nds = min(self.NDS, max(1, self.ndma))
            dsems = [st.enter_context(nc.semaphore(f"s_dma{k}")) for k in range(nds)]
            block = st.enter_context(nc.Block())
            per = {e: [o for o in ops if o.eng == e] for e in ENGS}

            def run(eng_name, eobj):
                waited_e = {e: -1 for e in ENGS}
                waited_d = {}
                for o in per[eng_name]:
                    need_e = {}
                    need_d = {}
                    for p in o.deps:
                        if p.dma:
                            k = p.dn % nds
                            v = 16 * (p.dn // nds + 1)
                            if waited_d.get(k, 0) < v:
                                need_d[k] = max(need_d.get(k, 0), v)
                        else:
                            if p.eng == "pe" and eng_name == "pe" and not o.dma:
                                continue
                            if waited_e[p.eng] < p.sigidx:
                                need_e[p.eng] = max(need_e.get(p.eng, -1), p.sigidx)
                    if o.dma:
                        k = o.dn % nds
                        v = 16 * (o.dn // nds)
                        if v > 0 and waited_d.get(k, 0) < v:
                            need_d[k] = max(need_d.get(k, 0), v)
                    for pe_, si in need_e.items():
                        eobj.wait_ge(esems[pe_][si // self.CAP], si % self.CAP + 1)
                        waited_e[pe_] = si
                    for k, v in need_d.items():
                        eobj.wait_ge(dsems[k], v)
                        waited_d[k] = v
                    if o.fn is None:
                        continue
                    ins = o.fn(eobj)
                    if o.dma:
                        ins.then_inc(dsems[o.dn % nds], 16)
                    elif o.sig:
                        ins.then_inc(esems[eng_name][o.sigidx // self.CAP], 1)

            @block.tensor
            def _(e):
                run("pe", e)

            @block.scalar
            def _(e):
                run("act", e)

            @block.vector
            def _(e):
                run("dve", e)

            @block.gpsimd
            def _(e):
                run("pool", e)

            @block.sync
            def _(e):
                run("sp", e)


class TB:
    __slots__ = ("t", "b")

    def __init__(self, t, name=""):
        self.t = t
        self.b = Buf(name)


class Ctx:
    def __init__(self):
        self.nc = bass.Bass("TRN2", target_bir_lowering=False)
        self.P = Prog(self.nc)
        self.st = ExitStack()
        self.finals = []
        self.n = 0

    def dram(self, name, shape, dt, out=False):
        return self.nc.dram_tensor(name, list(shape), dt, kind="ExternalOutput" if out else "ExternalInput").ap()

    def sb(self, shape, dt, name=None):
        self.n += 1
        name = f"sb_{name or 't'}_{self.n}"
        return TB(self.st.enter_context(self.nc.sbuf_tensor(name, list(shape), dt)), name)

    def ps(self, shape, dt=F32, name=None):
        self.n += 1
        name = f"ps_{name or 'p'}_{self.n}"
        return TB(self.st.enter_context(self.nc.psum_tensor(name, list(shape), dt)), name)

    def ring(self, n, shape, dt, name, psum=False):
        return Ring([(self.ps if psum else self.sb)(shape, dt, f"{name}{k}") for k in range(n)])

    def load(self, dst_ap, src_ap, dst_b, eng="sp"):
        return self.P.dma(dst_ap, src_ap, writes=[dst_b], eng=eng)

    def store(self, dst_ap, src_ap, src_b, eng="sp"):
        fb = Buf("out")
        self.finals.append(fb)
        return self.P.dma(dst_ap, src_ap, reads=[src_b], writes=[fb], eng=eng)

    def finish(self):
        self.P.emit(self.finals)
        self.st.close()
        return self.nc


class Ring:
    def __init__(self, items):
        self.items = items
        self.k = 0

    def next(self):
        it = self.items[self.k % len(self.items)]
        self.k += 1
        return it


D_MODEL = 2048
DC = 16
TL = 1024
D_IN = 6728
D_FF = 5504
FC = 43
EPS = 1e-6
PI = float(np.pi)

OFF = {}
_o = 0
for _n, _w in (("mq", 768), ("mk", 768), ("mv", 768), ("dq", 512), ("dk", 512), ("dv", 512),
               ("sq", 768), ("sk", 768), ("sv", 768), ("iq", 512), ("ik", 64), ("iw", 8)):
    OFF[_n] = _o
    _o += _w
FCH = []
for _n, _nch, _kind in (("mq", 6, 0), ("mk", 6, 0), ("dq", 4, 1), ("dk", 4, 1), ("sq", 6, 0), ("sk", 6, 0),
                        ("iq", 4, 2), ("ik", 1, 2)):
    for _j in range(_nch):
        FCH.append((_n, OFF[_n] + 128 * _j, _kind))
NF = len(FCH)
FIDX = {}
for _i, (_n, _c, _k) in enumerate(FCH):
    FIDX.setdefault(_n, _i)


def emit_rope_tables(cx, pos_ap, rc):
    P = cx.P
    pi_ = cx.sb([128, TL], I32, "posi")
    cx.load(pi_.t[:], pos_ap.partition_broadcast(128), pi_.b)
    pf = cx.sb([128, TL], F32, "posf")
    P.op("dve", lambda e: e.tensor_copy(out=pf.t[:], in_=pi_.t[:]), [pi_.b], [pf.b])
    tabs = {}
    a = cx.sb([128, TL], F32, "ra")
    ki = cx.sb([128, TL], I32, "rki")
    kf = cx.sb([128, TL], F32, "rkf")
    for hd, col in ((128, 0), (64, 1)):
        for nm, shift in (("cos", PI / 2), ("sin", 0.0)):
            out = cx.sb([128, TL], F32, f"{nm}{hd}")
            P.op("dve", lambda e, col=col: e.tensor_scalar(out=a.t[:], in0=pf.t[:], scalar1=rc.t[:, col:col + 1],
                                                           scalar2=None, op0=ALU.mult), [pf.b, rc.b], [a.b])
            P.op("dve", lambda e, shift=shift: e.tensor_scalar(out=a.t[:], in0=a.t[:], scalar1=shift - PI,
                                                               scalar2=None, op0=ALU.add), [a.b], [a.b])
            P.op("dve", lambda e: e.tensor_scalar(out=ki.t[:], in0=a.t[:], scalar1=1.0 / (2 * PI), scalar2=None,
                                                  op0=ALU.mult), [a.b], [ki.b])
            P.op("dve", lambda e: e.tensor_copy(out=kf.t[:], in_=ki.t[:]), [ki.b], [kf.b])
            P.op("dve", lambda e: e.scalar_tensor_tensor(out=a.t[:], in0=kf.t[:], scalar=-2 * PI, in1=a.t[:],
                                                         op0=ALU.mult, op1=ALU.add), [kf.b, a.b], [a.b])
            P.op("dve", lambda e: e.tensor_scalar(out=a.t[:], in0=a.t[:], scalar1=PI, scalar2=-PI, op0=ALU.min,
                                                  op1=ALU.max), [a.b], [a.b])
            P.op("act", lambda e, out=out: e.activation(out=out.t[:], in_=a.t[:], func=AF.Sin, scale=-1.0),
                 [a.b], [out.b])
            if nm == "sin":
                P.op("dve", lambda e, out=out, col=col: e.tensor_scalar(out=out.t[:], in0=out.t[:],
                                                                        scalar1=rc.t[:, 2 + col:3 + col], scalar2=None,
                                                                        op0=ALU.mult), [out.b, rc.b], [out.b])
            tabs[(nm, hd)] = out
    return tabs


def build_A():
    cx = Ctx()
    P = cx.P
    xT = cx.dram("xT", [DC, 128, TL], F32)
    pos = cx.dram("pos", [1, TL], I32)
    gnorm = cx.dram("gnorm", [128, DC], F32)
    wF = cx.dram("wF", [NF, 128, DC, 128], F32)
    wV = cx.dram("wV", [8, 128, DC, 256], F32)
    wI = cx.dram("wI", [128, DC, 8], F32)
    gq = cx.dram("gq", [128, NF], F32)
    rcd = cx.dram("rc", [128, 4], F32)
    cmd = cx.dram("cm", [4, 128, 128], F32)
    fT = cx.dram("fT", [NF, 128, TL], BF16, out=True)
    vtok = cx.dram("vtok", [TL, 2048], BF16, out=True)
    iwo = cx.dram("iwo", [TL, 8], F32, out=True)

    gn = cx.sb([128, DC], F32, "gn")
    cx.load(gn.t[:], gnorm, gn.b)
    gqt = cx.sb([128, NF], F32, "gqt")
    cx.load(gqt.t[:], gq, gqt.b)
    rc = cx.sb([128, 4], F32, "rc")
    cx.load(rc.t[:], rcd, rc.b)
    cm = cx.sb([128, 4, 128], F32, "cm")
    for k in range(4):
        cx.load(cm.t[:, k, :], cmd[k], cm.b)
    epsb = cx.sb([128, 1], F32, "epsb")
    P.op("dve", lambda e: e.memset(epsb.t[:], EPS), [], [epsb.b])
    tabs = emit_rope_tables(cx, pos, rc)

    hT = cx.sb([128, DC, TL], BF16, "hT")
    xr = cx.ring(3, [128, TL], F32, "xr")
    sqr = cx.ring(2, [128, TL], F32, "sqr")
    psA = cx.ring(2, [128, 512], F32, "psA", psum=True)
    psM = cx.ring(2, [128, 512], F32, "psM", psum=True)
    psW = cx.ring(2, [128, 512], F32, "psW", psum=True)
    psV = cx.ring(2, [128, 512], F32, "psV", psum=True)
    m0, m1 = psM.next(), psM.next()
    for dc in range(DC):
        x_ = xr.next()
        cx.load(x_.t[:], xT[dc], x_.b)
        s_ = sqr.next()
        P.op("act", lambda e, x_=x_, s_=s_: e.activation(out=s_.t[:], in_=x_.t[:], func=AF.Square), [x_.b], [s_.b])
        for hf, m_ in ((0, m0), (1, m1)):
            P.op("pe", lambda e, m_=m_, s_=s_, hf=hf, dc=dc: e.matmul(m_.t[:], lhsT=cm.t[:, 2, :],
                                                                     rhs=s_.t[:, hf * 512:(hf + 1) * 512],
                                                                     start=(dc == 0), stop=(dc == DC - 1)),
                 [cm.b, s_.b], [m_.b])
    rstd = cx.sb([128, TL], F32, "rstd")
    for hf, m_ in ((0, m0), (1, m1)):
        sl = slice(hf * 512, (hf + 1) * 512)
        P.op("act", lambda e, m_=m_, sl=sl: e.activation(out=rstd.t[:, sl], in_=m_.t[:], func=AF.Sqrt, scale=1.0 / DC,
                                                         bias=epsb.t[:, 0:1]), [m_.b, epsb.b], [rstd.b])
    P.op("dve", lambda e: e.reciprocal(out=rstd.t[:], in_=rstd.t[:]), [rstd.b], [rstd.b])
    for dc in range(DC):
        x_ = xr.next()
        cx.load(x_.t[:], xT[dc], x_.b)
        P.op("dve", lambda e, x_=x_, dc=dc: e.scalar_tensor_tensor(
            out=hT.t[:, dc, :], in0=x_.t[:], scalar=gn.t[:, dc:dc + 1], in1=rstd.t[:], op0=ALU.mult, op1=ALU.mult),
            [x_.b, gn.b, rstd.b], [hT.b])

    wst = cx.ring(2, [128, DC, 128], F32, "wst")
    wbf = cx.ring(2, [128, DC, 128], BF16, "wbf")
    xs_r = cx.ring(2, [128, 512], F32, "xs")
    sq_r = cx.ring(2, [128, 512], F32, "sq")
    rr_r = cx.ring(2, [128, 512], F32, "rr")
    y_r = cx.ring(2, [128, 512], F32, "y")
    t1_r = cx.ring(2, [128, 512], F32, "t1")
    t2_r = cx.ring(2, [128, 512], F32, "t2")
    ob_r = cx.ring(3, [128, 512], BF16, "ob")
    for j, (nm, c0, kind) in enumerate(FCH):
        ws = wst.next()
        cx.load(ws.t[:], wF[j], ws.b, eng="sp" if j % 2 == 0 else "pool")
        wb = wbf.next()
        P.op("dve" if j % 2 else "pool", lambda e, ws=ws, wb=wb: e.tensor_copy(out=wb.t[:], in_=ws.t[:]), [ws.b], [wb.b])
        hd = 128 if kind == 0 else 64
        perm_k = 0 if kind == 0 else 1
        mm_k = 2 if kind == 0 else 3
        cosT, sinT = tabs[("cos", hd)], tabs[("sin", hd)]
        for hf in range(2):
            sl = slice(hf * 512, (hf + 1) * 512)
            pa = psA.next()
            for dc in range(DC):
                P.op("pe", lambda e, pa=pa, wb=wb, dc=dc, sl=sl: e.matmul(pa.t[:], lhsT=wb.t[:, dc, :], rhs=hT.t[:, dc, sl],
                                                                         start=(dc == 0), stop=(dc == DC - 1)),
                     [wb.b, hT.b], [pa.b])
            xs = xs_r.next()
            P.op("act", lambda e, xs=xs, pa=pa: e.copy(out=xs.t[:], in_=pa.t[:]), [pa.b], [xs.b])
            if kind != 2:
                sq = sq_r.next()
                P.op("act", lambda e, sq=sq, xs=xs: e.activation(out=sq.t[:], in_=xs.t[:], func=AF.Square), [xs.b], [sq.b])
                pm = psM.next()
                P.op("pe", lambda e, pm=pm, sq=sq, mm_k=mm_k: e.matmul(pm.t[:], lhsT=cm.t[:, mm_k, :], rhs=sq.t[:],
                                                                      start=True, stop=True), [cm.b, sq.b], [pm.b])
                rr = rr_r.next()
                P.op("act", lambda e, rr=rr, pm=pm: e.activation(out=rr.t[:], in_=pm.t[:], func=AF.Sqrt,
                                                                 bias=epsb.t[:, 0:1]), [pm.b, epsb.b], [rr.b])
                P.op("dve", lambda e, rr=rr: e.reciprocal(out=rr.t[:], in_=rr.t[:]), [rr.b], [rr.b])
                y = y_r.next()
                P.op("dve", lambda e, y=y, xs=xs, rr=rr, j=j: e.scalar_tensor_tensor(
                    out=y.t[:], in0=xs.t[:], scalar=gqt.t[:, j:j + 1], in1=rr.t[:], op0=ALU.mult, op1=ALU.mult),
                    [xs.b, gqt.b, rr.b], [y.b])
            else:
                y = xs
            pw = psW.next()
            P.op("pe", lambda e, pw=pw, y=y, perm_k=perm_k: e.matmul(pw.t[:], lhsT=cm.t[:, perm_k, :], rhs=y.t[:],
                                                                    start=True, stop=True), [cm.b, y.b], [pw.b])
            t1 = t1_r.next()
            P.op("pool", lambda e, t1=t1, y=y, sl=sl, cosT=cosT: e.tensor_tensor(out=t1.t[:], in0=y.t[:], in1=cosT.t[:, sl],
                                                                               op=ALU.mult), [y.b, cosT.b], [t1.b])
            t2 = t2_r.next()
            P.op("dve", lambda e, t2=t2, pw=pw, sl=sl, sinT=sinT: e.tensor_tensor(out=t2.t[:], in0=pw.t[:], in1=sinT.t[:, sl],
                                                                                 op=ALU.mult), [pw.b, sinT.b], [t2.b])
            ob = ob_r.next()
            P.op("dve", lambda e, ob=ob, t1=t1, t2=t2: e.tensor_tensor(out=ob.t[:], in0=t1.t[:], in1=t2.t[:], op=ALU.add),
                 [t1.b, t2.b], [ob.b])
            cx.store(fT[j, :, sl], ob.t[:], ob.b)

    vst = cx.sb([128, DC, 256], F32, "vst")
    vbf = cx.ring(2, [128, DC, 256], BF16, "vbf")
    vo_r = cx.ring(3, [128, 256], BF16, "vo")
    for cb in range(8):
        cx.load(vst.t[:], wV[cb], vst.b)
        vb = vbf.next()
        P.op("dve" if cb % 2 else "pool", lambda e, vb=vb: e.tensor_copy(out=vb.t[:], in_=vst.t[:]), [vst.b], [vb.b])
        for tt in range(8):
            pv = psV.next()
            for dc in range(DC):
                P.op("pe", lambda e, pv=pv, vb=vb, dc=dc, tt=tt: e.matmul(pv.t[:, 0:256], lhsT=hT.t[:, dc, tt * 128:(tt + 1) * 128],
                                                                         rhs=vb.t[:, dc, :], start=(dc == 0), stop=(dc == DC - 1)),
                     [hT.b, vb.b], [pv.b])
            vo = vo_r.next()
            P.op("act", lambda e, vo=vo, pv=pv: e.copy(out=vo.t[:], in_=pv.t[:, 0:256]), [pv.b], [vo.b])
            cx.store(vtok[tt * 128:(tt + 1) * 128, cb * 256:(cb + 1) * 256], vo.t[:], vo.b)
    wis = cx.sb([128, DC, 8], F32, "wis")
    cx.load(wis.t[:], wI, wis.b)
    wib = cx.sb([128, DC, 8], BF16, "wib")
    P.op("dve", lambda e: e.tensor_copy(out=wib.t[:], in_=wis.t[:]), [wis.b], [wib.b])
    io_r = cx.ring(2, [128, 8], F32, "io")
    for tt in range(8):
        pv = psV.next()
        for dc in range(DC):
            P.op("pe", lambda e, pv=pv, dc=dc, tt=tt: e.matmul(pv.t[:, 0:8], lhsT=hT.t[:, dc, tt * 128:(tt + 1) * 128],
                                                              rhs=wib.t[:, dc, :], start=(dc == 0), stop=(dc == DC - 1)),
                 [hT.b, wib.b], [pv.b])
        io = io_r.next()
        P.op("act", lambda e, io=io, pv=pv: e.activation(out=io.t[:], in_=pv.t[:, 0:8], func=AF.Copy, scale=8 ** -0.5),
             [pv.b], [io.b])
        cx.store(iwo[tt * 128:(tt + 1) * 128, :], io.t[:], io.b)
    return cx.finish()


def tok_index(c):
    i = np.arange(8)[:, None]
    r = np.arange(128)[None, :]
    return ((8 * i + c) * 128 + r).reshape(-1)


def fm(a):
    T, F = a.shape
    return np.ascontiguousarray(a.T.reshape(F // 128, 128, T))


def wtile(w):
    return np.ascontiguousarray(w.reshape(DC, 128, w.shape[1]).transpose(1, 0, 2))


def rope_consts():
    rc = np.zeros((128, 4), np.float32)
    p = np.arange(128)
    inv128 = np.float32(10000.0) ** (-(np.arange(64, dtype=np.float32) * np.float32(2.0) / np.float32(128)))
    inv64 = np.float32(10000.0) ** (-(np.arange(32, dtype=np.float32) * np.float32(2.0) / np.float32(64)))
    rc[:, 0] = inv128[p % 64]
    rc[:, 1] = inv64[p % 32]
    rc[:, 2] = np.where(p < 64, -1.0, 1.0)
    rc[:, 3] = np.where((p % 64) < 32, -1.0, 1.0)
    cm = np.zeros((4, 128, 128), np.float32)
    m = np.arange(128)
    cm[0, (m + 64) % 128, m] = 1.0
    cm[1, 64 * (m // 64) + ((m % 64) + 32) % 64, m] = 1.0
    cm[2] = 1.0 / 128
    cm[3, :64, :64] = 1.0 / 64
    cm[3, 64:, 64:] = 1.0 / 64
    return rc, cm


def prep_A_weights(inp, l):
    w_in = inp["w_in"][l]
    wF = np.zeros((NF, 128, DC, 128), np.float32)
    gq = np.ones((128, NF), np.float32)
    gains = {"mq": inp["moba_qk_gain"][l, 0], "mk": inp["moba_qk_gain"][l, 1],
             "dq": np.tile(inp["diff_qk_gain"][l, 0], 2), "dk": np.tile(inp["diff_qk_gain"][l, 1], 2),
             "sq": inp["dsa_qk_gain"][l, 0], "sk": inp["dsa_qk_gain"][l, 1]}
    for j, (nm, c0, kind) in enumerate(FCH):
        ncol = 64 if nm == "ik" else 128
        wF[j, :, :, :ncol] = wtile(w_in[:, c0:c0 + ncol])
        if nm in gains:
            gq[:, j] = gains[nm]
    wv = np.concatenate([w_in[:, OFF["mv"]:OFF["mv"] + 768], w_in[:, OFF["dv"]:OFF["dv"] + 512],
                         w_in[:, OFF["sv"]:OFF["sv"] + 768]], axis=1)
    wV = np.stack([wtile(wv[:, 256 * k:256 * (k + 1)]) for k in range(8)])
    wI = wtile(w_in[:, OFF["iw"]:OFF["iw"] + 8])
    rc, cm = rope_consts()
    return dict(wF=wF, wV=wV, wI=wI, gq=gq, rc=rc, cm=cm,
                gnorm=np.ascontiguousarray(inp["attn_norm"][l].reshape(DC, 128).T))


TLH = 8 * 130
CGRP = ((0, 3), (3, 3), (6, 2))


def build_C(dbg=False):
    cx = Ctx()
    P = cx.P
    x2h = cx.dram("x2h", [DC, 128, TLH], F32)
    gnorm = cx.dram("gnorm", [128, DC], F32)
    wU = cx.dram("wU", [2 * FC, 128, DC, 128], F32)
    cwd = cx.dram("cw", [128, 2 * FC, 4], F32)
    wD = cx.dram("wD", [DC, 3, 128, 16, 128], F32)
    cmd = cx.dram("cm", [4, 128, 128], F32)
    x3T = cx.dram("x3T", [DC, 128, TL], F32, out=True)

    gn = cx.sb([128, DC], F32, "gn")
    cx.load(gn.t[:], gnorm, gn.b)
    cw = cx.sb([128, 2 * FC, 4], F32, "cw")
    cx.load(cw.t[:], cwd, cw.b)
    cm = cx.sb([128, 128], F32, "cm")
    cx.load(cm.t[:], cmd[2], cm.b)
    epsb = cx.sb([128, 1], F32, "epsb")
    P.op("dve", lambda e: e.memset(epsb.t[:], EPS), [], [epsb.b])

    psU = cx.ring(6, [128, 512], F32, "psU", psum=True)
    psD = cx.ring(2, [128, 512], F32, "psD", psum=True)
    hT = cx.sb([128, DC, TLH], BF16, "hT")
    actT = cx.sb([128, FC, TL], BF16, "actT")
    xr = cx.ring(3, [128, TLH], F32, "xr")
    sqr = cx.ring(2, [128, TLH], F32, "sqr")
    ms = [psU.next() for _ in range(3)]
    for dc in range(DC):
        x_ = xr.next()
        cx.load(x_.t[:], x2h[dc], x_.b)
        s_ = sqr.next()
        P.op("act", lambda e, x_=x_, s_=s_: e.activation(out=s_.t[:], in_=x_.t[:], func=AF.Square), [x_.b], [s_.b])
        for (t0, nt), m_ in zip(CGRP, ms):
            P.op("pe", lambda e, m_=m_, s_=s_, t0=t0, nt=nt, dc=dc: e.matmul(
                m_.t[:, 0:nt * 130], lhsT=cm.t[:], rhs=s_.t[:, t0 * 130:(t0 + nt) * 130], start=(dc == 0),
                stop=(dc == DC - 1)), [cm.b, s_.b], [m_.b])
    rstd = cx.sb([128, TLH], F32, "rstd")
    for (t0, nt), m_ in zip(CGRP, ms):
        P.op("act", lambda e, m_=m_, t0=t0, nt=nt: e.activation(out=rstd.t[:, t0 * 130:(t0 + nt) * 130], in_=m_.t[:, 0:nt * 130],
                                                               func=AF.Sqrt, scale=1.0 / DC, bias=epsb.t[:, 0:1]),
             [m_.b, epsb.b], [rstd.b])
    P.op("dve", lambda e: e.reciprocal(out=rstd.t[:], in_=rstd.t[:]), [rstd.b], [rstd.b])
    for dc in range(DC):
        x_ = xr.next()
        cx.load(x_.t[:], x2h[dc], x_.b)
        P.op("dve", lambda e, x_=x_, dc=dc: e.scalar_tensor_tensor(
            out=hT.t[:, dc, :], in0=x_.t[:], scalar=gn.t[:, dc:dc + 1], in1=rstd.t[:], op0=ALU.mult, op1=ALU.mult),
            [x_.b, gn.b, rstd.b], [hT.b])

    wst = cx.ring(2, [128, DC, 128], F32, "wst")
    wbf = cx.ring(2, [128, DC, 128], BF16, "wbf")
    c0_r = cx.ring(2, [128, 3, 128], F32, "c0")
    c1_r = cx.ring(2, [128, 3, 128], F32, "c1")
    gv_r = cx.ring(2, [128, TL], F32, "gv")
    nload = 0
    for fc in range(FC):
        gsil = gv_r.next()
        for which in (0, 1):
            ch = fc + which * FC
            ws = wst.next()
            cx.load(ws.t[:], wU[ch], ws.b, eng="sp" if nload % 2 == 0 else "pool")
            wb = wbf.next()
            P.op("pool" if nload % 2 == 0 else "dve", lambda e, ws=ws, wb=wb: e.tensor_copy(out=wb.t[:], in_=ws.t[:]),
                 [ws.b], [wb.b])
            nload += 1
            for (t0, nt) in CGRP:
                pu = psU.next()
                for dc in range(DC):
                    P.op("pe", lambda e, pu=pu, wb=wb, dc=dc, t0=t0, nt=nt: e.matmul(
                        pu.t[:, 0:nt * 130], lhsT=wb.t[:, dc, :], rhs=hT.t[:, dc, t0 * 130:(t0 + nt) * 130],
                        start=(dc == 0), stop=(dc == DC - 1)), [wb.b, hT.b], [pu.b])
                uv = pu.t[:, 0:nt * 130].rearrange("p (t c) -> p t c", c=130)
                c0 = c0_r.next()
                P.op("act", lambda e, c0=c0, uv=uv, nt=nt, ch=ch: e.activation(
                    out=c0.t[:, 0:nt, :], in_=uv[:, :, 2:130], func=AF.Identity, scale=cw.t[:, ch, 2:3],
                    bias=cw.t[:, ch, 3:4]), [pu.b, cw.b], [c0.b])
                c1 = c1_r.next()
                P.op("dve", lambda e, c0=c0, c1=c1, uv=uv, nt=nt, ch=ch: e.scalar_tensor_tensor(
                    out=c1.t[:, 0:nt, :], in0=uv[:, :, 1:129], scalar=cw.t[:, ch, 1:2], in1=c0.t[:, 0:nt, :],
                    op0=ALU.mult, op1=ALU.add), [pu.b, cw.b, c0.b], [c1.b])
                osl = slice(t0 * 128, (t0 + nt) * 128)
                if which == 0:
                    gview = gsil.t[:, osl].rearrange("p (t c) -> p t c", c=128)
                    P.op("dve", lambda e, c1=c1, uv=uv, nt=nt, ch=ch, gview=gview: e.scalar_tensor_tensor(
                        out=gview, in0=uv[:, :, 0:128], scalar=cw.t[:, ch, 0:1], in1=c1.t[:, 0:nt, :],
                        op0=ALU.mult, op1=ALU.add), [pu.b, cw.b, c1.b], [gsil.b])
                    P.op("act", lambda e, osl=osl, gsil=gsil: e.activation(out=gsil.t[:, osl], in_=gsil.t[:, osl], func=AF.Silu),
                         [gsil.b], [gsil.b])
                else:
                    P.op("dve", lambda e, c1=c1, c0=c0, uv=uv, nt=nt, ch=ch: e.scalar_tensor_tensor(
                        out=c0.t[:, 0:nt, :], in0=uv[:, :, 0:128], scalar=cw.t[:, ch, 0:1], in1=c1.t[:, 0:nt, :],
                        op0=ALU.mult, op1=ALU.add), [pu.b, cw.b, c1.b], [c0.b])
                    aview = actT.t[:, fc, osl].rearrange("p (t c) -> p t c", c=128)
                    gview = gsil.t[:, osl].rearrange("p (t c) -> p t c", c=128)
                    P.op("pool", lambda e, c0=c0, nt=nt, aview=aview, gview=gview: e.tensor_tensor(
                        out=aview, in0=c0.t[:, 0:nt, :], in1=gview, op=ALU.mult), [c0.b, gsil.b], [actT.b])

    if dbg:
        dA = cx.dram('dbgA', [128, FC, TL], BF16, out=True)
        cx.store(dA, actT.t[:], actT.b)
        dH = cx.dram('dbgH', [128, DC, TLH], BF16, out=True)
        cx.store(dH, hT.t[:], hT.b)
    xo_r = cx.ring(2, [128, TL], F32, "xo")
    xres_r = cx.ring(2, [128, 8, 128], F32, "xres")
    for dcc in range(DC):
        xres = xres_r.next()
        cx.load(xres.t[:], x2h[dcc].rearrange("p (t c) -> p t c", c=130)[:, :, 2:130], xres.b)
        pd = [psD.next(), psD.next()]
        for gi in range(3):
            nk = 16 if gi < 2 else FC - 32
            ws = wst.next()
            cx.load(ws.t[:], wD[dcc, gi], ws.b, eng="sp" if nload % 2 == 0 else "pool")
            wb = wbf.next()
            P.op("pool" if nload % 2 == 0 else "dve", lambda e, ws=ws, wb=wb: e.tensor_copy(out=wb.t[:], in_=ws.t[:]),
                 [ws.b], [wb.b])
            nload += 1
            for k in range(nk):
                fc = gi * 16 + k
                for hf in range(2):
                    P.op("pe", lambda e, hf=hf, wb=wb, k=k, fc=fc, pd=pd: e.matmul(
                        pd[hf].t[:], lhsT=wb.t[:, k, :], rhs=actT.t[:, fc, hf * 512:(hf + 1) * 512],
                        start=(fc == 0), stop=(fc == FC - 1)), [wb.b, actT.b], [pd[hf].b])
        xo = xo_r.next()
        for hf in range(2):
            P.op("dve", lambda e, hf=hf, xo=xo, xres=xres, pd=pd: e.tensor_tensor(
                out=xo.t[:, hf * 512:(hf + 1) * 512], in0=pd[hf].t[:],
                in1=xres.t[:, hf * 4:(hf + 1) * 4, :].rearrange("p t c -> p (t c)"), op=ALU.add), [pd[hf].b, xres.b], [xo.b])
        cx.store(x3T[dcc], xo.t[:], xo.b)
    return cx.finish()


def prep_C_weights(inp, l):
    wup = inp["ffn_w_up"][l]
    wU = np.ascontiguousarray(wup.reshape(DC, 128, 2 * FC, 128).transpose(2, 1, 0, 3))
    cwv = np.concatenate([inp["ffn_conv_w"][l], inp["ffn_conv_b"][l][None]], 0)
    cw = np.ascontiguousarray(cwv.reshape(4, 2 * FC, 128).transpose(2, 1, 0))
    wd = inp["ffn_w_down"][l]
    wdp = np.zeros((48 * 128, D_MODEL), np.float32)
    wdp[:D_FF] = wd
    wD = np.ascontiguousarray(wdp.reshape(3, 16, 128, DC, 128).transpose(3, 0, 2, 1, 4))
    rc, cm = rope_consts()
    return dict(wU=wU, cw=cw, wD=wD, cm=cm, gnorm=np.ascontiguousarray(inp["ffn_norm"][l].reshape(DC, 128).T))


def halo_cols(xfull_T, c):
    out = np.zeros((D_MODEL, TLH), np.float32)
    for i in range(8):
        t0 = (8 * i + c) * 128
        lo = max(t0 - 2, 0)
        out[:, i * 130 + (2 - (t0 - lo)):i * 130 + 130] = xfull_T[:, lo:t0 + 128]
    return np.ascontiguousarray(out.reshape(DC, 128, TLH))


def _mm(cx, out, lhsT, rhs, start, stop, reads, writes, skip=False):
    if skip:
        return cx.P.op("pe", lambda e: e.matmul(out, lhsT=lhsT, rhs=rhs, start=start, stop=stop, skip_group_check=True),
                       reads, writes)
    return cx.P.op("pe", lambda e: e.matmul(out, lhsT=lhsT, rhs=rhs, start=start, stop=stop), reads, writes)


def _tr(cx, out, in_, ident, reads, writes):
    return cx.P.op("pe", lambda e: e.transpose(out=out, in_=in_, identity=ident), reads, writes)


def _act(cx, out, in_, func, reads, writes, **kw):
    return cx.P.op("act", lambda e: e.activation(out=out, in_=in_, func=func, **kw), reads, writes)


def _tt(cx, eng, out, in0, in1, op, reads, writes):
    return cx.P.op(eng, lambda e: e.tensor_tensor(out=out, in0=in0, in1=in1, op=op), reads, writes)


def _ts(cx, eng, out, in0, s1, s2, op0, op1, reads, writes, accum_out=None):
    if op1 is None:
        return cx.P.op(eng, lambda e: e.tensor_scalar(out=out, in0=in0, scalar1=s1, scalar2=None, op0=op0), reads, writes)
    if accum_out is not None:
        return cx.P.op(eng, lambda e: e.tensor_scalar(out=out, in0=in0, scalar1=s1, scalar2=s2, op0=op0, op1=op1,
                                                      accum_out=accum_out), reads, writes)
    return cx.P.op(eng, lambda e: e.tensor_scalar(out=out, in0=in0, scalar1=s1, scalar2=s2, op0=op0, op1=op1), reads, writes)


def _stt(cx, out, in0, scalar, in1, op0, op1, reads, writes):
    return cx.P.op("dve", lambda e: e.scalar_tensor_tensor(out=out, in0=in0, scalar=scalar, in1=in1, op0=op0, op1=op1),
                   reads, writes)


def _cp(cx, eng, out, in_, reads, writes):
    if eng == "act":
        return cx.P.op("act", lambda e: e.copy(out=out, in_=in_), reads, writes)
    return cx.P.op(eng, lambda e: e.tensor_copy(out=out, in_=in_), reads, writes)


def _recip(cx, out, in_, reads, writes):
    return cx.P.op("dve", lambda e: e.reciprocal(out=out, in_=in_), reads, writes)


def _memset(cx, eng, out, val, writes):
    return cx.P.op(eng, lambda e: e.memset(out, val), [], writes)


class Scope:
    def __init__(self, cx):
        self.cx = cx

    def __enter__(self):
        self.saved = self.cx.st
        self.cx.st = ExitStack()
        return self

    def __exit__(self, *a):
        cx = self.cx
        cx.st.close()
        cx.st = self.saved
        last = {}
        dmas = []
        for o in cx.P.ops:
            if o.dma:
                dmas.append(o)
            else:
                last[o.eng] = o
        cx.fence = list(last.values()) + dmas[-Prog.NDS:]
        return False


def _fenced_tb(cx, tb):
    f = getattr(cx, "fence", None)
    if f:
        tb.b.rs = list(f)
    return tb


_orig_sb = Ctx.sb
_orig_ps = Ctx.ps
Ctx.sb = lambda self, shape, dt, name=None: _fenced_tb(self, _orig_sb(self, shape, dt, name))
Ctx.ps = lambda self, shape, dt=F32, name=None: _fenced_tb(self, _orig_ps(self, shape, dt, name))


NKT = 64
OFFK = [0]
for _kt in range(NKT):
    OFFK.append(OFFK[-1] + 8 - _kt // 8)
NSLOT = OFFK[-1]
QCH = {"mq": 0, "dq": 6, "sq": 10, "iq": 16}
KCH = {"mk": 0, "dk": 6, "sk": 10, "ik": 16}
NQC = 20
NKC = 17
BIG = 1.0e30
BIGB = 30000.0


def attn_pass(cx, env, KT, krows, QT, qrows, Vt, scale, key_tiles, mode, O, R, maskT=None, biasT=None, selrows=None,
              suffix=True):
    pss, ptr = env["pss"], env["ptr"]
    ones, Mj = env["ones"], env["Mj"]
    nkt = len(key_tiles)

    def stage1(n, kt):
        imin = (kt // 8) if suffix else 0
        c0 = imin * 128
        groups = []
        a = c0
        while a < TL:
            b = min(TL, (a // 512 + 1) * 512)
            groups.append((a, b))
            a = b
        pt = ptr.next()
        for (a, b) in groups:
            ps = pss.next()
            w = b - a
            last = (mode != "moba")
            _mm(cx, ps.t[:, 0:w], KT.t[krows, kt * 128:(kt + 1) * 128], QT.t[qrows, a:b], True, last,
                [KT.b, QT.b], [ps.b])
            if mode == "moba":
                nb = kt // 2
                _mm(cx, ps.t[:, 0:w], selrows.t[0:32, nb:nb + 1].to_broadcast([32, 128]), biasT.t[:, a:b], False, True,
                    [selrows.b, biasT.b], [ps.b])
            _act(cx, pt.t[:, a:b], ps.t[:, 0:w], AF.Exp, [ps.b], [pt.b], scale=scale)
        if mode in ("causal", "moba"):
            j = kt % 8
            _tt(cx, "pool", pt.t[:, c0:c0 + 128], pt.t[:, c0:c0 + 128], Mj.t[:, j, :], ALU.mult, [pt.b, Mj.b], [pt.b])
        elif mode == "dsa":
            nq = TL - c0
            mv = maskT.t[:, OFFK[kt]:OFFK[kt] + nq // 128, :].rearrange("p s q -> p (s q)")
            _tt(cx, "dve", pt.t[:, c0:TL], pt.t[:, c0:TL], mv, ALU.mult, [pt.b, maskT.b], [pt.b])
        return pt, groups

    def stage2(n, pt, groups):
        for (a, b) in groups:
            bank = a // 512
            o0 = a - bank * 512
            w = b - a
            _mm(cx, O[bank].t[:, o0:o0 + w], Vt.t[:, n, :], pt.t[:, a:b], n == 0, n == nkt - 1, [Vt.b, pt.b], [O[bank].b],
                skip=True)
            _mm(cx, R[bank].t[:, o0:o0 + w], ones.t[:], pt.t[:, a:b], n == 0, n == nkt - 1, [ones.b, pt.b], [R[bank].b],
                skip=True)

    prev = None
    for n, kt in enumerate(key_tiles):
        pt, groups = stage1(n, kt)
        if prev is not None:
            stage2(*prev)
        prev = (n, pt, groups)
    stage2(*prev)


def build_B(dbg=False):
    cx = Ctx()
    P = cx.P
    xT = cx.dram("xT", [DC, 128, TL], F32)
    qT = cx.dram("qT", [NQC, 128, TL], BF16)
    kT = cx.dram("kT", [NKC, 128, 8192], BF16)
    vG = cx.dram("vG", [16, 128, NKT, 128], BF16)
    iwd = cx.dram("iw", [TL, 8], F32)
    pcd = cx.dram("pc", [128, 4], F32)
    wOd = cx.dram("wO", [DC, 128, DC, 128], F32)
    dld = cx.dram("dl", [1, 256], F32)
    sld = cx.dram("sl", [128, 1], F32)
    cmd = cx.dram("cm", [4, 128, 128], F32)
    gcd = cx.dram("gc", [128, 2 * DC + 2], F32)
    memd = cx.dram("memT", [DC, 128, 256], F32)
    wqd = cx.dram("wq", [4, 128, DC, 128], F32)
    wkd = cx.dram("wk", [4, 128, DC, 128], F32)
    wvd = cx.dram("wv", [4, 128, DC, 128], F32)
    wcod = cx.dram("wco", [DC, 128, 4, 128], F32)
    x2T = cx.dram("x2T", [DC, 128, TL], F32, out=True)

    pc = cx.sb([128, 4], F32, "pc")
    cx.load(pc.t[:], pcd, pc.b)
    cm = cx.sb([128, 4, 128], F32, "cm")
    for k in range(4):
        cx.load(cm.t[:, k, :], cmd[k], cm.b)
    epsb = cx.sb([128, 1], F32, "epsb")
    _memset(cx, "dve", epsb.t[:], EPS, [epsb.b])
    dkq = cx.sb([128, 128], F32, "dkq")
    P.op("pool", lambda e: e.iota(dkq.t[:], [[-1, 128]], base=0, channel_multiplier=1,
                                  allow_small_or_imprecise_dtypes=True), [], [dkq.b])
    ident = cx.sb([128, 128], BF16, "ident")
    _ts(cx, "dve", ident.t[:], dkq.t[:], 0.0, None, ALU.is_equal, None, [dkq.b], [ident.b])
    identf = cx.sb([128, 128], F32, "identf")
    _ts(cx, "dve", identf.t[:], dkq.t[:], 0.0, None, ALU.is_equal, None, [dkq.b], [identf.b])
    ones = cx.sb([128, 128], BF16, "ones")
    _memset(cx, "dve", ones.t[:], 1.0, [ones.b])
    thrj = cx.sb([128, 8], F32, "thrj")
    for j in range(8):
        _ts(cx, "dve", thrj.t[:, j:j + 1], pc.t[:, 0:1], 128.0, -128.0 * j, ALU.mult, ALU.add, [pc.b], [thrj.b])
    Mj = cx.sb([128, 8, 128], BF16, "Mj")
    for j in range(8):
        _ts(cx, "dve", Mj.t[:, j, :], dkq.t[:], thrj.t[:, j:j + 1], None, ALU.is_le, None, [dkq.b, thrj.b], [Mj.b])
    iwt = cx.sb([128, 8, 8], F32, "iwt")
    cx.load(iwt.t[:], iwd.rearrange("(i p) h -> p i h", p=128), iwt.b)
    dl = cx.sb([128, 256], F32, "dl")
    cx.load(dl.t[:], dld.partition_broadcast(128), dl.b)
    lam = cx.sb([128, 4], F32, "lam")
    dlp = cx.sb([128, 2, 64], F32, "dlp")
    _tt(cx, "dve", dlp.t[:, 0, :], dl.t[:, 0:64], dl.t[:, 64:128], ALU.mult, [dl.b], [dlp.b])
    _tt(cx, "dve", dlp.t[:, 1, :], dl.t[:, 128:192], dl.t[:, 192:256], ALU.mult, [dl.b], [dlp.b])
    P.op("dve", lambda e: e.reduce_sum(out=lam.t[:, 0:2], in_=dlp.t[:], axis=AX.X), [dlp.b], [lam.b])
    _act(cx, lam.t[:, 0:2], lam.t[:, 0:2], AF.Exp, [lam.b], [lam.b])
    _tt(cx, "dve", lam.t[:, 2:3], lam.t[:, 0:1], lam.t[:, 1:2], ALU.subtract, [lam.b], [lam.b])
    _ts(cx, "dve", lam.t[:, 3:4], lam.t[:, 2:3], pc.t[:, 2:3], -1.0, ALU.add, ALU.mult, [lam.b, pc.b], [lam.b])
    sl = cx.sb([128, 1], F32, "sl")
    cx.load(sl.t[:], sld, sl.b)
    slg = cx.sb([128, 1], F32, "slg")
    _tt(cx, "dve", slg.t[:], sl.t[:], pc.t[:, 3:4], ALU.mult, [sl.b, pc.b], [slg.b])
    gc = cx.sb([128, 2 * DC + 2], F32, "gc")
    cx.load(gc.t[:], gcd, gc.b)

    mixedT = cx.sb([128, DC, TL], BF16, "mixedT")
    outer = Scope(cx)
    outer.__enter__()
    maskT = cx.sb([128, NSLOT, 128], BF16, "maskT")

    with Scope(cx):
        pss = cx.ring(4, [128, 512], F32, "pss", psum=True)
        pst = cx.ring(2, [128, 4, 128], BF16, "pst", psum=True)
        MTj = cx.sb([128, 8, 128], F32, "MTj")
        NEGj = cx.sb([128, 8, 128], F32, "NEGj")
        POSj = cx.sb([128, 8, 128], F32, "POSj")
        for j in range(8):
            _ts(cx, "dve", MTj.t[:, j, :], dkq.t[:], -1.0, thrj.t[:, j:j + 1], ALU.mult, ALU.is_le, [dkq.b, thrj.b], [MTj.b])
        _ts(cx, "dve", NEGj.t[:], MTj.t[:], -1.0, BIG, ALU.add, ALU.mult, [MTj.b], [NEGj.b])
        _ts(cx, "dve", POSj.t[:], NEGj.t[:], -1.0, None, ALU.mult, None, [NEGj.b], [POSj.b])
        kdup = cx.sb([128, 8192], BF16, "kdup")
        cx.load(kdup.t[0:64, :], kT[KCH["ik"], 0:64, :], kdup.b)
        cx.load(kdup.t[64:128, :], kT[KCH["ik"], 0:64, :], kdup.b, eng="pool")
        qiT = cx.sb([128, 4, TL], BF16, "qiT")
        for k in range(4):
            cx.load(qiT.t[:, k, :], qT[QCH["iq"] + k], qiT.b)
        Isc = cx.sb([128, 8192], F32, "Isc")
        msk = cx.sb([128, 8192], BF16, "msk")
        rl_r = cx.ring(3, [128, 512], F32, "rl")
        st_r = cx.ring(2, [128, 8], F32, "bst")
        dg_r = cx.ring(1, [128, 1024], F32, "dg")
        for i in range(8):
            nk = 8 * i + 8
            nkeys = nk * 128
            for cg in range(nk // 4):
                ks = slice(cg * 512, (cg + 1) * 512)
                for h in range(8):
                    rows = slice(64 * (h % 2), 64 * (h % 2) + 64)
                    ps = pss.next()
                    _mm(cx, ps.t[:], qiT.t[rows, h // 2, i * 128:(i + 1) * 128], kdup.t[rows, ks], True, True,
                        [qiT.b, kdup.b], [ps.b])
                    rl = rl_r.next()
                    _act(cx, rl.t[:], ps.t[:], AF.Relu, [ps.b], [rl.b])
                    if h == 0:
                        _ts(cx, "dve", Isc.t[:, ks], rl.t[:], iwt.t[:, i, 0:1], None, ALU.mult, None, [rl.b, iwt.b], [Isc.b])
                    else:
                        _stt(cx, Isc.t[:, ks], rl.t[:], iwt.t[:, i, h:h + 1], Isc.t[:, ks], ALU.mult, ALU.add,
                             [rl.b, iwt.b, Isc.b], [Isc.b])
            dsl = slice(nkeys - 1024, nkeys)
            dg = dg_r.next()
            _tt(cx, "dve", dg.t[:], Isc.t[:, dsl], MTj.t[:].rearrange("p j k -> p (j k)"), ALU.mult, [Isc.b, MTj.b], [dg.b])
            _tt(cx, "dve", Isc.t[:, dsl], dg.t[:], NEGj.t[:].rearrange("p j k -> p (j k)"), ALU.add, [dg.b, NEGj.b], [Isc.b])
            _tt(cx, "dve", dg.t[:], dg.t[:], POSj.t[:].rearrange("p j k -> p (j k)"), ALU.add, [dg.b, POSj.b], [dg.b])
            bs = st_r.next()
            P.op("dve", lambda e, bs=bs, nkeys=nkeys: e.tensor_reduce(out=bs.t[:, 1:2], in_=Isc.t[:, 0:nkeys], axis=AX.X,
                                                                     op=ALU.max), [Isc.b], [bs.b])
            P.op("dve", lambda e, bs=bs, dg=dg: e.tensor_reduce(out=bs.t[:, 0:1], in_=dg.t[:], axis=AX.X, op=ALU.min),
                 [dg.b], [bs.b])
            if i > 0:
                P.op("dve", lambda e, bs=bs, nkeys=nkeys: e.tensor_reduce(out=bs.t[:, 5:6], in_=Isc.t[:, 0:nkeys - 1024],
                                                                         axis=AX.X, op=ALU.min), [Isc.b], [bs.b])
                _tt(cx, "dve", bs.t[:, 0:1], bs.t[:, 0:1], bs.t[:, 5:6], ALU.min, [bs.b], [bs.b])
            _ts(cx, "dve", bs.t[:, 1:2], bs.t[:, 1:2], 1.0, None, ALU.add, None, [bs.b], [bs.b])
            for it in range(18):
                _tt(cx, "dve", bs.t[:, 2:3], bs.t[:, 0:1], bs.t[:, 1:2], ALU.add, [bs.b], [bs.b])
                _ts(cx, "dve", bs.t[:, 2:3], bs.t[:, 2:3], 0.5, None, ALU.mult, None, [bs.b], [bs.b])
                _memset(cx, "dve", bs.t[:, 3:4], 0.0, [bs.b])
                _ts(cx, "dve", msk.t[:, 0:nkeys], Isc.t[:, 0:nkeys], bs.t[:, 2:3], 0.0, ALU.is_ge, ALU.add, [Isc.b, bs.b],
                    [msk.b, bs.b], accum_out=bs.t[:, 3:4])
                _ts(cx, "dve", bs.t[:, 4:5], bs.t[:, 3:4], 256.0, None, ALU.is_ge, None, [bs.b], [bs.b])
                _tt(cx, "dve", bs.t[:, 5:6], bs.t[:, 2:3], bs.t[:, 0:1], ALU.subtract, [bs.b], [bs.b])
                _stt(cx, bs.t[:, 0:1], bs.t[:, 5:6], bs.t[:, 4:5], bs.t[:, 0:1], ALU.mult, ALU.add, [bs.b], [bs.b])
                _tt(cx, "dve", bs.t[:, 5:6], bs.t[:, 1:2], bs.t[:, 2:3], ALU.subtract, [bs.b], [bs.b])
                _stt(cx, bs.t[:, 1:2], bs.t[:, 5:6], bs.t[:, 4:5], bs.t[:, 2:3], ALU.mult, ALU.add, [bs.b], [bs.b])
            _ts(cx, "dve", msk.t[:, 0:nkeys], Isc.t[:, 0:nkeys], bs.t[:, 0:1], None, ALU.is_ge, None, [Isc.b, bs.b], [msk.b])
            for g in range(nk // 4):
                pT = pst.next()
                for jj in range(4):
                    kt = g * 4 + jj
                    _tr(cx, pT.t[:, jj, :], msk.t[:, kt * 128:(kt + 1) * 128], ident.t[:], [msk.b, ident.b], [pT.b])
                for jj in range(4):
                    kt = g * 4 + jj
                    slot = OFFK[kt] + (i - kt // 8)
                    _cp(cx, "act" if jj % 2 else "dve", maskT.t[:, slot, :], pT.t[:, jj, :], [pT.b], [maskT.b])

    with Scope(cx):
        env = dict(pss=cx.ring(4, [128, 512], F32, "pss", psum=True), ptr=cx.ring(3, [128, TL], BF16, "pt"),
                   ones=ones, Mj=Mj)
        O = [cx.ps([128, 512], F32, "O0"), cx.ps([128, 512], F32, "O1")]
        R = [cx.ps([128, 512], F32, "R0"), cx.ps([128, 512], F32, "R1")]
        KT_r = cx.ring(2, [128, 8192], BF16, "KT")
        V_r = cx.ring(2, [128, NKT, 128], BF16, "V")
        QT_r = cx.ring(2, [128, TL], BF16, "QT")
        rinv_r = cx.ring(1, [128, TL], F32, "rinv")
        o1 = cx.sb([128, TL], F32, "o1")
        o2 = cx.sb([128, TL], F32, "o2")
        sqd = rinv_r.items[0]
        nidx = cx.sb([128, 32], F32, "nidx")
        P.op("pool", lambda e: e.iota(nidx.t[:], [[1, 32]], base=0, channel_multiplier=0,
                                      allow_small_or_imprecise_dtypes=True), [], [nidx.b])
        curv = cx.sb([128, 8], F32, "curv")
        for i in range(8):
            _ts(cx, "dve", curv.t[:, i:i + 1], pc.t[:, 1:2], 4.0 * i, None, ALU.add, None, [pc.b], [curv.b])
        selrows = ident
        biasT = cx.sb([32, TL], BF16, "biasT")
        kb = cx.sb([128, 32], F32, "kb")
        qf_r = cx.ring(2, [128, 128], F32, "qf")
        gt = cx.sb([128, 6, 32], F32, "gt")
        m8 = cx.sb([128, 8], F32, "m8")
        all_kt = list(range(NKT))

        def load_head(kc, krow_all, qc, vh):
            KT = KT_r.next()
            cx.load(KT.t[:], kT[kc], KT.b, eng="sp")
            Vt = V_r.next()
            cx.load(Vt.t[:], vG[vh], Vt.b, eng="sp")
            QT = QT_r.next()
            cx.load(QT.t[:], qT[qc], QT.b, eng="sp")
            return KT, Vt, QT

        def finalize(dst_ap, dst_b, to_f32_tb=None):
            ri = rinv_r.next()
            for bk in range(2):
                sl_ = slice(bk * 512, (bk + 1) * 512)
                _recip(cx, ri.t[:, sl_], R[bk].t[:], [R[bk].b], [ri.b])
                if to_f32_tb is None:
                    _tt(cx, "dve", dst_ap[:, sl_], O[bk].t[:], ri.t[:, sl_], ALU.mult, [O[bk].b, ri.b], [dst_b])
                else:
                    _tt(cx, "dve", to_f32_tb.t[:, sl_], O[bk].t[:], ri.t[:, sl_], ALU.mult, [O[bk].b, ri.b], [to_f32_tb.b])

        for h in range(6):
            KT, Vt, QT = load_head(KCH["sk"] + h, None, QCH["sq"] + h, 10 + h)
            attn_pass(cx, env, KT, slice(0, 128), QT, slice(0, 128), Vt, 128 ** -0.5, all_kt, "dsa", O, R, maskT=maskT)
            finalize(mixedT.t[:, 10 + h, :], mixedT.b)
        for h in range(6):
            KT, Vt, QT = load_head(KCH["mk"] + h, None, QCH["mq"] + h, h)
            P.op("dve", lambda e, KT=KT: e.tensor_reduce(out=kb.t[:], in_=KT.t[:].rearrange("p (n k) -> p n k", k=256),
                                                         axis=AX.X, op=ALU.add), [KT.b], [kb.b])
            for i in range(8):
                qf = qf_r.next()
                _cp(cx, "pool", qf.t[:], QT.t[:, i * 128:(i + 1) * 128], [QT.b], [qf.b])
                ps = env["pss"].next()
                _mm(cx, ps.t[:, 0:32], qf.t[:], kb.t[:], True, True, [qf.b, kb.b], [ps.b])
                _ts(cx, "dve", gt.t[:, 0, :], nidx.t[:], curv.t[:, i:i + 1], None, ALU.is_lt, None, [nidx.b, curv.b], [gt.b])
                _ts(cx, "dve", gt.t[:, 1, :], nidx.t[:], curv.t[:, i:i + 1], None, ALU.is_equal, None, [nidx.b, curv.b], [gt.b])
                _tt(cx, "dve", gt.t[:, 2, :], ps.t[:, 0:32], gt.t[:, 0, :], ALU.mult, [ps.b, gt.b], [gt.b])
                _ts(cx, "dve", gt.t[:, 3, :], gt.t[:, 0, :], -1.0, BIG, ALU.add, ALU.mult, [gt.b], [gt.b])
                _tt(cx, "dve", gt.t[:, 2, :], gt.t[:, 2, :], gt.t[:, 3, :], ALU.add, [gt.b], [gt.b])
                P.op("dve", lambda e: e.max(out=m8.t[:], in_=gt.t[:, 2, :]), [gt.b], [m8.b])
                _ts(cx, "dve", gt.t[:, 4, :], gt.t[:, 2, :], m8.t[:, 2:3], None, ALU.is_ge, None, [gt.b, m8.b], [gt.b])
                _tt(cx, "dve", gt.t[:, 4, :], gt.t[:, 4, :], gt.t[:, 0, :], ALU.mult, [gt.b], [gt.b])
                _tt(cx, "dve", gt.t[:, 4, :], gt.t[:, 4, :], gt.t[:, 1, :], ALU.add, [gt.b], [gt.b])
                _ts(cx, "dve", gt.t[:, 5, :], gt.t[:, 4, :], -1.0, BIGB, ALU.add, ALU.mult, [gt.b], [gt.b])
                ps2 = env["pss"].next()
                _mm(cx, ps2.t[0:32, 0:128], gt.t[:, 5, :], identf.t[:], True, True, [gt.b, identf.b], [ps2.b])
                _cp(cx, "act", biasT.t[:, i * 128:(i + 1) * 128], ps2.t[0:32, 0:128], [ps2.b], [biasT.b])
            attn_pass(cx, env, KT, slice(0, 128), QT, slice(0, 128), Vt, 128 ** -0.5, all_kt, "moba", O, R, biasT=biasT,
                      selrows=selrows)
            finalize(mixedT.t[:, h, :], mixedT.b)
        for h in range(4):
            KT, Vt, QT = load_head(KCH["dk"] + h, None, QCH["dq"] + h, 6 + h)
            attn_pass(cx, env, KT, slice(0, 64), QT, slice(0, 64), Vt, 64 ** -0.5, all_kt, "causal", O, R)
            finalize(None, None, to_f32_tb=o1)
            attn_pass(cx, env, KT, slice(64, 128), QT, slice(64, 128), Vt, 64 ** -0.5, all_kt, "causal", O, R)
            finalize(None, None, to_f32_tb=o2)
            _stt(cx, o1.t[:], o2.t[:], lam.t[:, 3:4], o1.t[:], ALU.mult, ALU.add, [o2.b, lam.b, o1.b], [o1.b])
            _act(cx, sqd.t[:], o1.t[:], AF.Square, [o1.b], [sqd.b])
            for bk in range(2):
                sl_ = slice(bk * 512, (bk + 1) * 512)
                ps = env["pss"].next()
                _mm(cx, ps.t[:], cm.t[:, 2, :], sqd.t[:, sl_], True, True, [cm.b, sqd.b], [ps.b])
                _act(cx, o2.t[:, sl_], ps.t[:], AF.Sqrt, [ps.b, epsb.b], [o2.b], bias=epsb.t[:, 0:1])
            _recip(cx, o2.t[:], o2.t[:], [o2.b], [o2.b])
            _stt(cx, mixedT.t[:, 6 + h, :], o1.t[:], slg.t[:, 0:1], o2.t[:], ALU.mult, ALU.mult, [o1.b, slg.b, o2.b], [mixedT.b])

    outer.__exit__(None, None, None)
    if dbg:
        dM = cx.dram("dbgM", [128, DC, TL], BF16, out=True)
        cx.store(dM, mixedT.t[:], mixedT.b)

    with Scope(cx):
        pss = cx.ring(4, [128, 512], F32, "pss", psum=True)
        env = dict(pss=pss, ptr=cx.ring(2, [128, TL], BF16, "pt"), ones=ones, Mj=Mj)
        O = [cx.ps([128, 512], F32, "O0"), cx.ps([128, 512], F32, "O1")]
        R = [cx.ps([128, 512], F32, "R0"), cx.ps([128, 512], F32, "R1")]
        x1T = cx.sb([128, DC, TL], F32, "x1T")
        h2T = cx.sb([128, DC, TL], BF16, "h2T")
        wst = cx.ring(1, [128, DC, 128], F32, "wst")
        wbf = cx.ring(2, [128, DC, 128], BF16, "wbf")
        xr = cx.ring(2, [128, TL], F32, "xr")
        sq_r = cx.ring(1, [128, TL], F32, "sq")
        nld = 0
        for dmc in range(DC):
            ws = wst.next()
            cx.load(ws.t[:], wOd[dmc], ws.b, eng="sp" if nld % 2 == 0 else "pool")
            wb = wbf.next()
            _cp(cx, "pool" if nld % 2 == 0 else "dve", wb.t[:], ws.t[:], [ws.b], [wb.b])
            nld += 1
            x_ = xr.next()
            cx.load(x_.t[:], xT[dmc], x_.b)
            for hf in range(2):
                sl_ = slice(hf * 512, (hf + 1) * 512)
                ps = pss.next()
                for hc in range(DC):
                    _mm(cx, ps.t[:], wb.t[:, hc, :], mixedT.t[:, hc, sl_], hc == 0, hc == DC - 1, [wb.b, mixedT.b], [ps.b])
                _tt(cx, "dve", x1T.t[:, dmc, sl_], ps.t[:], x_.t[:, sl_], ALU.add, [ps.b, x_.b], [x1T.b])
            sq = sq_r.next()
            _act(cx, sq.t[:], x1T.t[:, dmc, :], AF.Square, [x1T.b], [sq.b])
            for hf in range(2):
                _mm(cx, R[hf].t[:], cm.t[:, 2, :], sq.t[:, hf * 512:(hf + 1) * 512], dmc == 0, dmc == DC - 1, [cm.b, sq.b],
                    [R[hf].b])
        rstd = cx.sb([128, TL], F32, "rstd")
        for hf in range(2):
            _act(cx, rstd.t[:, hf * 512:(hf + 1) * 512], R[hf].t[:], AF.Sqrt, [R[hf].b, epsb.b], [rstd.b], scale=1.0 / DC,
                 bias=epsb.t[:, 0:1])
        _recip(cx, rstd.t[:], rstd.t[:], [rstd.b], [rstd.b])
        for dc in range(DC):
            _stt(cx, h2T.t[:, dc, :], x1T.t[:, dc, :], gc.t[:, dc:dc + 1], rstd.t[:], ALU.mult, ALU.mult,
                 [x1T.b, gc.b, rstd.b], [h2T.b])
        memr = cx.ring(2, [128, 256], F32, "memr")
        msqr = cx.ring(2, [128, 256], F32, "msqr")
        psm = pss.next()
        for dc in range(DC):
            mf = memr.next()
            cx.load(mf.t[:], memd[dc], mf.b)
            mq_ = msqr.next()
            _act(cx, mq_.t[:], mf.t[:], AF.Square, [mf.b], [mq_.b])
            _mm(cx, psm.t[:, 0:256], cm.t[:, 2, :], mq_.t[:], dc == 0, dc == DC - 1, [cm.b, mq_.b], [psm.b])
        rm = cx.sb([128, 256], F32, "rm")
        _act(cx, rm.t[:], psm.t[:, 0:256], AF.Sqrt, [psm.b, epsb.b], [rm.b], scale=1.0 / DC, bias=epsb.t[:, 0:1])
        _recip(cx, rm.t[:], rm.t[:], [rm.b], [rm.b])
        mT = cx.sb([128, DC, 256], BF16, "mT")
        for dc in range(DC):
            mf = memr.next()
            cx.load(mf.t[:], memd[dc], mf.b)
            _stt(cx, mT.t[:, dc, :], mf.t[:], gc.t[:, DC + dc:DC + dc + 1], rm.t[:], ALU.mult, ALU.mult,
                 [mf.b, gc.b, rm.b], [mT.b])
        ckT = cx.sb([128, 4, 256], BF16, "ckT")
        cv = cx.sb([128, 4, 2, 128], BF16, "cv")
        cqT = TB(mixedT.t[:, 0:4, :], "cqT")
        coT = TB(mixedT.t[:, 4:8, :], "coT")
        for _tb in (cqT, coT):
            _tb.b.rs = list(mixedT.b.rs) + ([mixedT.b.w] if mixedT.b.w is not None else [])
        tmpf = cx.ring(1, [128, 512], F32, "tmpf")
        tmps = cx.ring(1, [128, 512], F32, "tmps")
        tmpr = cx.ring(1, [128, 512], F32, "tmpr")

        def headnorm(ps, w, gcol, dst_ap, dst_b):
            xf = tmpf.next()
            _cp(cx, "act", xf.t[:, 0:w], ps.t[:, 0:w], [ps.b], [xf.b])
            s2 = tmps.next()
            _act(cx, s2.t[:, 0:w], xf.t[:, 0:w], AF.Square, [xf.b], [s2.b])
            pm = pss.next()
            _mm(cx, pm.t[:, 0:w], cm.t[:, 2, :], s2.t[:, 0:w], True, True, [cm.b, s2.b], [pm.b])
            rr = tmpr.next()
            _act(cx, rr.t[:, 0:w], pm.t[:, 0:w], AF.Sqrt, [pm.b, epsb.b], [rr.b], bias=epsb.t[:, 0:1])
            _recip(cx, rr.t[:, 0:w], rr.t[:, 0:w], [rr.b], [rr.b])
            _stt(cx, dst_ap, xf.t[:, 0:w], gc.t[:, gcol:gcol + 1], rr.t[:, 0:w], ALU.mult, ALU.mult, [xf.b, gc.b, rr.b], [dst_b])

        for h in range(4):
            ws = wst.next()
            cx.load(ws.t[:], wkd[h], ws.b)
            wb = wbf.next()
            _cp(cx, "pool", wb.t[:], ws.t[:], [ws.b], [wb.b])
            ps = pss.next()
            for dc in range(DC):
                _mm(cx, ps.t[:, 0:256], wb.t[:, dc, :], mT.t[:, dc, :], dc == 0, dc == DC - 1, [wb.b, mT.b], [ps.b])
            headnorm(ps, 256, 2 * DC + 1, ckT.t[:, h, :], ckT.b)
            ws = wst.next()
            cx.load(ws.t[:], wvd[h], ws.b)
            wb = wbf.next()
            _cp(cx, "dve", wb.t[:], ws.t[:], [ws.b], [wb.b])
            for mt in range(2):
                ps = pss.next()
                for dc in range(DC):
                    _mm(cx, ps.t[:, 0:128], mT.t[:, dc, mt * 128:(mt + 1) * 128], wb.t[:, dc, :], dc == 0, dc == DC - 1,
                        [wb.b, mT.b], [ps.b])
                _cp(cx, "act", cv.t[:, h, mt, :], ps.t[:, 0:128], [ps.b], [cv.b])
            ws = wst.next()
            cx.load(ws.t[:], wqd[h], ws.b)
            wb = wbf.next()
            _cp(cx, "pool", wb.t[:], ws.t[:], [ws.b], [wb.b])
            for hf in range(2):
                sl_ = slice(hf * 512, (hf + 1) * 512)
                ps = pss.next()
                for dc in range(DC):
                    _mm(cx, ps.t[:], wb.t[:, dc, :], h2T.t[:, dc, sl_], dc == 0, dc == DC - 1, [wb.b, h2T.b], [ps.b])
                headnorm(ps, 512, 2 * DC, cqT.t[:, h, sl_], cqT.b)
        for h in range(4):
            KTv = TB(ckT.t[:, h, :])
            KTv.b = ckT.b
            QTv = TB(cqT.t[:, h, :])
            QTv.b = cqT.b
            Vv = TB(cv.t[:, h, :, :])
            Vv.b = cv.b
            attn_pass(cx, env, KTv, slice(0, 128), QTv, slice(0, 128), Vv, 128 ** -0.5, [0, 1], "none", O, R, suffix=False)
            ri = tmpf.next()
            ri2 = tmps.next()
            for bk, rt in ((0, ri), (1, ri2)):
                sl_ = slice(bk * 512, (bk + 1) * 512)
                _recip(cx, rt.t[:], R[bk].t[:], [R[bk].b], [rt.b])
                _tt(cx, "dve", coT.t[:, h, sl_], O[bk].t[:], rt.t[:], ALU.mult, [O[bk].b, rt.b], [coT.b])
        wcs = cx.ring(2, [128, 4, 128], F32, "wcs")
        wcb = cx.ring(2, [128, 4, 128], BF16, "wcb")
        xo_r = xr
        for dmc in range(DC):
            ws = wcs.next()
            cx.load(ws.t[:], wcod[dmc], ws.b)
            wb = wcb.next()
            _cp(cx, "pool", wb.t[:], ws.t[:], [ws.b], [wb.b])
            xo = xo_r.next()
            for hf in range(2):
                sl_ = slice(hf * 512, (hf + 1) * 512)
                ps = pss.next()
                for hc in range(4):
                    _mm(cx, ps.t[:], wb.t[:, hc, :], coT.t[:, hc, sl_], hc == 0, hc == 3, [wb.b, coT.b], [ps.b])
                _tt(cx, "dve", xo.t[:, sl_], ps.t[:], x1T.t[:, dmc, sl_], ALU.add, [ps.b, x1T.b], [xo.b])
            cx.store(x2T[dmc], xo.t[:], xo.b)
    return cx.finish()


def prep_B_weights(inp, l):
    import math
    wO = np.stack([wtile(inp["w_out"][l][:, 128 * k:128 * (k + 1)]) for k in range(DC)])
    gc = np.zeros((128, 2 * DC + 2), np.float32)
    gc[:, 0:DC] = inp["cross_norm"][l].reshape(DC, 128).T
    gc[:, DC:2 * DC] = inp["mem_norm"][l].reshape(DC, 128).T
    gc[:, 2 * DC] = inp["cross_qk_gain"][l, 0]
    gc[:, 2 * DC + 1] = inp["cross_qk_gain"][l, 1]
    wq = np.stack([wtile(inp["cross_wq"][l][:, 128 * h:128 * (h + 1)]) for h in range(4)])
    wk = np.stack([wtile(inp["cross_wkv"][l][:, 128 * h:128 * (h + 1)]) for h in range(4)])
    wv = np.stack([wtile(inp["cross_wkv"][l][:, 512 + 128 * h:512 + 128 * (h + 1)]) for h in range(4)])
    wo = inp["cross_wo"][l]
    wco = np.ascontiguousarray(wo.reshape(4, 128, DC, 128).transpose(2, 1, 0, 3))
    rc, cm = rope_consts()
    lam_init = 0.8 - 0.6 * math.exp(-0.3 * l)
    return dict(wO=wO, gc=gc, wq=wq, wk=wk, wv=wv, wco=wco, cm=cm, memT=fm(inp["mem"][0]),
                dl=np.ascontiguousarray(inp["diff_lambda"][l].reshape(1, 256)),
                sl=np.ascontiguousarray(inp["diff_subln"][l].reshape(128, 1))), lam_init


def glue_A_to_B(resA):
    kT = np.zeros((NKC, 128, 8192), ml_dtypes.bfloat16)
    vfull = np.zeros((8192, 2048), ml_dtypes.bfloat16)
    qTs = []
    ksrc = ([FIDX["mk"] + k for k in range(6)] + [FIDX["dk"] + k for k in range(4)] + [FIDX["sk"] + k for k in range(6)]
            + [FIDX["ik"]])
    qsrc = ([FIDX["mq"] + k for k in range(6)] + [FIDX["dq"] + k for k in range(4)] + [FIDX["sq"] + k for k in range(6)]
            + [FIDX["iq"] + k for k in range(4)])
    for c in range(NCORES):
        ti = tok_index(c)
        fT = np.asarray(resA[c]["fT"])
        kT[:, :, ti] = fT[ksrc]
        vfull[ti] = np.asarray(resA[c]["vtok"])
        qTs.append(np.ascontiguousarray(fT[qsrc]))
    vG = np.ascontiguousarray(vfull.reshape(NKT, 128, 16, 128).transpose(2, 1, 0, 3))
    return qTs, kT, vG


_PROGS = {}


def _prog(name):
    if name not in _PROGS:
        _PROGS[name] = {"A": build_A, "B": build_B, "C": build_C}[name]()
    return _PROGS[name]


def kernel(**inputs):
    import math
    inp = {k: np.asarray(v) for k, v in inputs.items()}
    cores = list(range(NCORES))
    toks = [tok_index(c) for c in cores]
    x0 = inp["x"][0]
    xT_loc = [fm(x0[toks[c]]) for c in cores]
    pos_loc = [np.ascontiguousarray(inp["positions"][0][toks[c]].reshape(1, TL)).astype(np.int32) for c in cores]
    for l in range(2):
        wa = prep_A_weights(inp, l)
        maps = []
        for c in cores:
            m = dict(wa)
            m["xT"] = xT_loc[c]
            m["pos"] = pos_loc[c]
            maps.append(m)
        resA = run_bass_kernel_spmd(_prog("A"), maps, core_ids=cores).results
        del maps, wa
        qTs, kT, vG = glue_A_to_B(resA)
        wb, lam_init = prep_B_weights(inp, l)
        maps = []
        for c in cores:
            m = dict(wb)
            m["xT"] = xT_loc[c]
            m["qT"] = qTs[c]
            m["kT"] = kT
            m["vG"] = vG
            m["iw"] = np.asarray(resA[c]["iwo"])
            pc = np.zeros((128, 4), np.float32)
            pc[:, 0] = c
            pc[:, 1] = c // 2
            pc[:, 2] = lam_init
            pc[:, 3] = 1.0 - lam_init
            m["pc"] = pc
            maps.append(m)
        resB = run_bass_kernel_spmd(_prog("B"), maps, core_ids=cores).results
        del maps, wb, qTs, kT, vG, resA
        xfull = np.zeros((D_MODEL, 8192), np.float32)
        for c in cores:
            xfull[:, toks[c]] = np.asarray(resB[c]["x2T"]).reshape(D_MODEL, TL)
        wc = prep_C_weights(inp, l)
        maps = []
        for c in cores:
            m = dict(wc)
            m["x2h"] = halo_cols(xfull, c)
            maps.append(m)
        resC = run_bass_kernel_spmd(_prog("C"), maps, core_ids=cores).results
        del maps, wc, resB
        xT_loc = [np.ascontiguousarray(np.asarray(resC[c]["x3T"])) for c in cores]
    out = np.zeros((1, 8192, D_MODEL), np.float32)
    for c in cores:
        out[0, toks[c]] = xT_loc[c].reshape(D_MODEL, TL).T
    return out
```

```python
import numpy as np
import ml_dtypes
from contextlib import ExitStack
import concourse.bass as bass
import concourse.mybir as mybir
from concourse.bass_utils import run_bass_kernel_spmd

F32 = mybir.dt.float32
BF16 = mybir.dt.bfloat16
I32 = mybir.dt.int32
ALU = mybir.AluOpType
AF = mybir.ActivationFunctionType
AX = mybir.AxisListType

NCORES = 8
ENGS = ("pe", "act", "dve", "pool", "sp")


class Buf:
    __slots__ = ("name", "w", "rs")

    def __init__(self, name=""):
        self.name = name
        self.w = None
        self.rs = []


class Op:
    __slots__ = ("eng", "fn", "deps", "sig", "sigidx", "dma", "dn")

    def __init__(self, eng, fn, dma):
        self.eng = eng
        self.fn = fn
        self.deps = []
        self.sig = False
        self.sigidx = -1
        self.dma = dma
        self.dn = -1


class Prog:
    NDS = 40
    CAP = 30000

    def __init__(self, nc):
        self.nc = nc
        self.ops = []
        self.ndma = 0

    def op(self, eng, fn, reads=(), writes=(), dma=False):
        o = Op(eng, fn, dma)
        deps = {}
        for b in reads:
            if b.w is not None:
                deps[id(b.w)] = b.w
        for b in writes:
            if b.w is not None:
                deps[id(b.w)] = b.w
            for r in b.rs:
                deps[id(r)] = r
        deps.pop(id(o), None)
        o.deps = list(deps.values())
        for b in writes:
            b.w = o
            b.rs = []
        for b in reads:
            if not b.rs or b.rs[-1] is not o:
                b.rs.append(o)
        if dma:
            o.dn = self.ndma
            self.ndma += 1
        self.ops.append(o)
        return o

    def dma(self, out, in_, reads=(), writes=(), eng="sp", **kw):
        return self.op(eng, lambda e: e.dma_start(out=out, in_=in_, **kw), reads, writes, dma=True)

    def emit(self, final_bufs=()):
        nc = self.nc
        ops = self.ops
        fin = Op("sp", None, False)
        fd = {}
        for b in final_bufs:
            if b.w is not None:
                fd[id(b.w)] = b.w
        fin.deps = list(fd.values())
        ops = ops + [fin]
        for o in ops:
            for p in o.deps:
                if not p.dma:
                    if p.eng == "pe" and o.eng == "pe" and not o.dma:
                        continue
                    p.sig = True
        cnt = {e: 0 for e in ENGS}
        for o in ops:
            if o.sig:
                o.sigidx = cnt[o.eng]
                cnt[o.eng] += 1
        with ExitStack() as st:
            esems = {}
            for e in ENGS:
                n = (cnt[e] + self.CAP - 1) // self.CAP
                esems[e] = [st.enter_context(nc.semaphore(f"s_{e}{k}")) for k in range(n)]
            nds = min(self.NDS, max(1, self.ndma))
            dsems = [st.enter_context(nc.semaphore(f"s_dma{k}")) for k in range(nds)]
            block = st.enter_context(nc.Block())
            per = {e: [o for o in ops if o.eng == e] for e in ENGS}

            def run(eng_name, eobj):
                waited_e = {e: -1 for e in ENGS}
                waited_d = {}
                for o in per[eng_name]:
                    need_e = {}
                    need_d = {}
                    for p in o.deps:
                        if p.dma:
                            k = p.dn % nds
                            v = 16 * (p.dn // nds + 1)
                            if waited_d.get(k, 0) < v:
                                need_d[k] = max(need_d.get(k, 0), v)
                        else:
                            if p.eng == "pe" and eng_name == "pe" and not o.dma:
                                continue
                            if waited_e[p.eng] < p.sigidx:
                                need_e[p.eng] = max(need_e.get(p.eng, -1), p.sigidx)
                    if o.dma:
                        k = o.dn % nds
                        v = 16 * (o.dn // nds)
                        if v > 0 and waited_d.get(k, 0) < v:
                            need_d[k] = max(need_d.get(k, 0), v)
                    for pe_, si in need_e.items():
                        eobj.wait_ge(esems[pe_][si // self.CAP], si % self.CAP + 1)
                        waited_e[pe_] = si
                    for k, v in need_d.items():
                        eobj.wait_ge(dsems[k], v)
                        waited_d[k] = v
                    if o.fn is None:
                        continue
                    ins = o.fn(eobj)
                    if o.dma:
                        ins.then_inc(dsems[o.dn % nds], 16)
                    elif o.sig:
                        ins.then_inc(esems[eng_name][o.sigidx // self.CAP], 1)

            @block.tensor
            def _(e):
                run("pe", e)

            @block.scalar
            def _(e):
                run("act", e)

            @block.vector
            def _(e):
                run("dve", e)

            @block.gpsimd
            def _(e):
                run("pool", e)

            @block.sync
            def _(e):
                run("sp", e)


class TB:
    __slots__ = ("t", "b")

    def __init__(self, t, name=""):
        self.t = t
        self.b = Buf(name)


class Ctx:
    def __init__(self):
        self.nc = bass.Bass("TRN2", target_bir_lowering=False)
        self.P = Prog(self.nc)
        self.st = ExitStack()
        self.finals = []
        self.n = 0

    def dram(self, name, shape, dt, out=False):
        return self.nc.dram_tensor(name, list(shape), dt, kind="ExternalOutput" if out else "ExternalInput").ap()

    def sb(self, shape, dt, name=None):
        self.n += 1
        name = f"sb_{name or 't'}_{self.n}"
        return TB(self.st.enter_context(self.nc.sbuf_tensor(name, list(shape), dt)), name)

    def ps(self, shape, dt=F32, name=None):
        self.n += 1
        name = f"ps_{name or 'p'}_{self.n}"
        return TB(self.st.enter_context(self.nc.psum_tensor(name, list(shape), dt)), name)

    def ring(self, n, shape, dt, name, psum=False):
        return Ring([(self.ps if psum else self.sb)(shape, dt, f"{name}{k}") for k in range(n)])

    def load(self, dst_ap, src_ap, dst_b, eng="sp"):
        return self.P.dma(dst_ap, src_ap, writes=[dst_b], eng=eng)

    def store(self, dst_ap, src_ap, src_b, eng="sp"):
        fb = Buf("out")
        self.finals.append(fb)
        return self.P.dma(dst_ap, src_ap, reads=[src_b], writes=[fb], eng=eng)

    def finish(self):
        self.P.emit(self.finals)
        self.st.close()
        return self.nc


class Ring:
    def __init__(self, items):
        self.items = items
        self.k = 0

    def next(self):
        it = self.items[self.k % len(self.items)]
        self.k += 1
        return it


D_MODEL = 2048
DC = 16
TL = 1024
D_IN = 6728
D_FF = 5504
FC = 43
EPS = 1e-6
PI = float(np.pi)

OFF = {}
_o = 0
for _n, _w in (("mq", 768), ("mk", 768), ("mv", 768), ("dq", 512), ("dk", 512), ("dv", 512),
               ("sq", 768), ("sk", 768), ("sv", 768), ("iq", 512), ("ik", 64), ("iw", 8)):
    OFF[_n] = _o
    _o += _w
FCH = []
for _n, _nch, _kind in (("mq", 6, 0), ("mk", 6, 0), ("dq", 4, 1), ("dk", 4, 1), ("sq", 6, 0), ("sk", 6, 0),
                        ("iq", 4, 2), ("ik", 1, 2)):
    for _j in range(_nch):
        FCH.append((_n, OFF[_n] + 128 * _j, _kind))
NF = len(FCH)
FIDX = {}
for _i, (_n, _c, _k) in enumerate(FCH):
    FIDX.setdefault(_n, _i)


def emit_rope_tables(cx, pos_ap, rc):
    P = cx.P
    pi_ = cx.sb([128, TL], I32, "posi")
    cx.load(pi_.t[:], pos_ap.partition_broadcast(128), pi_.b)
    pf = cx.sb([128, TL], F32, "posf")
    P.op("dve", lambda e: e.tensor_copy(out=pf.t[:], in_=pi_.t[:]), [pi_.b], [pf.b])
    tabs = {}
    a = cx.sb([128, TL], F32, "ra")
    ki = cx.sb([128, TL], I32, "rki")
    kf = cx.sb([128, TL], F32, "rkf")
    for hd, col in ((128, 0), (64, 1)):
        for nm, shift in (("cos", PI / 2), ("sin", 0.0)):
            out = cx.sb([128, TL], F32, f"{nm}{hd}")
            P.op("dve", lambda e, col=col: e.tensor_scalar(out=a.t[:], in0=pf.t[:], scalar1=rc.t[:, col:col + 1],
                                                           scalar2=None, op0=ALU.mult), [pf.b, rc.b], [a.b])
            P.op("dve", lambda e, shift=shift: e.tensor_scalar(out=a.t[:], in0=a.t[:], scalar1=shift - PI,
                                                               scalar2=None, op0=ALU.add), [a.b], [a.b])
            P.op("dve", lambda e: e.tensor_scalar(out=ki.t[:], in0=a.t[:], scalar1=1.0 / (2 * PI), scalar2=None,
                                                  op0=ALU.mult), [a.b], [ki.b])
            P.op("dve", lambda e: e.tensor_copy(out=kf.t[:], in_=ki.t[:]), [ki.b], [kf.b])
            P.op("dve", lambda e: e.scalar_tensor_tensor(out=a.t[:], in0=kf.t[:], scalar=-2 * PI, in1=a.t[:],
                                                         op0=ALU.mult, op1=ALU.add), [kf.b, a.b], [a.b])
            P.op("dve", lambda e: e.tensor_scalar(out=a.t[:], in0=a.t[:], scalar1=PI, scalar2=-PI, op0=ALU.min,
                                                  op1=ALU.max), [a.b], [a.b])
            P.op("act", lambda e, out=out: e.activation(out=out.t[:], in_=a.t[:], func=AF.Sin, scale=-1.0),
                 [a.b], [out.b])
            if nm == "sin":
                P.op("dve", lambda e, out=out, col=col: e.tensor_scalar(out=out.t[:], in0=out.t[:],
                                                                        scalar1=rc.t[:, 2 + col:3 + col], scalar2=None,
                                                                        op0=ALU.mult), [out.b, rc.b], [out.b])
            tabs[(nm, hd)] = out
    return tabs


def build_A():
    cx = Ctx()
    P = cx.P
    xT = cx.dram("xT", [DC, 128, TL], F32)
    pos = cx.dram("pos", [1, TL], I32)
    gnorm = cx.dram("gnorm", [128, DC], F32)
    wF = cx.dram("wF", [NF, 128, DC, 128], F32)
    wV = cx.dram("wV", [8, 128, DC, 256], F32)
    wI = cx.dram("wI", [128, DC, 8], F32)
    gq = cx.dram("gq", [128, NF], F32)
    rcd = cx.dram("rc", [128, 4], F32)
    cmd = cx.dram("cm", [4, 128, 128], F32)
    fT = cx.dram("fT", [NF, 128, TL], BF16, out=True)
    vtok = cx.dram("vtok", [TL, 2048], BF16, out=True)
    iwo = cx.dram("iwo", [TL, 8], F32, out=True)

    gn = cx.sb([128, DC], F32, "gn")
    cx.load(gn.t[:], gnorm, gn.b)
    gqt = cx.sb([128, NF], F32, "gqt")
    cx.load(gqt.t[:], gq, gqt.b)
    rc = cx.sb([128, 4], F32, "rc")
    cx.load(rc.t[:], rcd, rc.b)
    cm = cx.sb([128, 4, 128], F32, "cm")
    for k in range(4):
        cx.load(cm.t[:, k, :], cmd[k], cm.b)
    epsb = cx.sb([128, 1], F32, "epsb")
    P.op("dve", lambda e: e.memset(epsb.t[:], EPS), [], [epsb.b])
    tabs = emit_rope_tables(cx, pos, rc)

    hT = cx.sb([128, DC, TL], BF16, "hT")
    xr = cx.ring(3, [128, TL], F32, "xr")
    sqr = cx.ring(2, [128, TL], F32, "sqr")
    psA = cx.ring(2, [128, 512], F32, "psA", psum=True)
    psM = cx.ring(2, [128, 512], F32, "psM", psum=True)
    psW = cx.ring(2, [128, 512], F32, "psW", psum=True)
    psV = cx.ring(2, [128, 512], F32, "psV", psum=True)
    m0, m1 = psM.next(), psM.next()
    for dc in range(DC):
        x_ = xr.next()
        cx.load(x_.t[:], xT[dc], x_.b)
        s_ = sqr.next()
        P.op("act", lambda e, x_=x_, s_=s_: e.activation(out=s_.t[:], in_=x_.t[:], func=AF.Square), [x_.b], [s_.b])
        for hf, m_ in ((0, m0), (1, m1)):
            P.op("pe", lambda e, m_=m_, s_=s_, hf=hf, dc=dc: e.matmul(m_.t[:], lhsT=cm.t[:, 2, :],
                                                                     rhs=s_.t[:, hf * 512:(hf + 1) * 512],
                                                                     start=(dc == 0), stop=(dc == DC - 1)),
                 [cm.b, s_.b], [m_.b])
    rstd = cx.sb([128, TL], F32, "rstd")
    for hf, m_ in ((0, m0), (1, m1)):
        sl = slice(hf * 512, (hf + 1) * 512)
        P.op("act", lambda e, m_=m_, sl=sl: e.activation(out=rstd.t[:, sl], in_=m_.t[:], func=AF.Sqrt, scale=1.0 / DC,
                                                         bias=epsb.t[:, 0:1]), [m_.b, epsb.b], [rstd.b])
    P.op("dve", lambda e: e.reciprocal(out=rstd.t[:], in_=rstd.t[:]), [rstd.b], [rstd.b])
    for dc in range(DC):
        x_ = xr.next()
        cx.load(x_.t[:], xT[dc], x_.b)
        P.op("dve", lambda e, x_=x_, dc=dc: e.scalar_tensor_tensor(
            out=hT.t[:, dc, :], in0=x_.t[:], scalar=gn.t[:, dc:dc + 1], in1=rstd.t[:], op0=ALU.mult, op1=ALU.mult),
            [x_.b, gn.b, rstd.b], [hT.b])

    wst = cx.ring(3, [128, DC, 128], F32, "wst")
    wbf = cx.ring(2, [128, DC, 128], BF16, "wbf")

    def issue_wF(jj):
        w_ = wst.next()
        cx.load(w_.t[:], wF[jj], w_.b, eng="sp")
        return w_

    pend_w = [issue_wF(jj) for jj in range(3)]
    xs_r = cx.ring(2, [128, 512], F32, "xs")
    sq_r = cx.ring(2, [128, 512], F32, "sq")
    rr_r = cx.ring(2, [128, 512], F32, "rr")
    y_r = cx.ring(2, [128, 512], F32, "y")
    t1_r = cx.ring(2, [128, 512], F32, "t1")
    t2_r = cx.ring(2, [128, 512], F32, "t2")
    ob_r = cx.ring(3, [128, 512], BF16, "ob")
    for j, (nm, c0, kind) in enumerate(FCH):
        ws = pend_w.pop(0)
        wb = wbf.next()
        P.op("dve" if j % 2 else "pool", lambda e, ws=ws, wb=wb: e.tensor_copy(out=wb.t[:], in_=ws.t[:]), [ws.b], [wb.b])
        if j + 3 < NF:
            pend_w.append(issue_wF(j + 3))
        hd = 128 if kind == 0 else 64
        perm_k = 0 if kind == 0 else 1
        mm_k = 2 if kind == 0 else 3
        cosT, sinT = tabs[("cos", hd)], tabs[("sin", hd)]
        for hf in range(2):
            sl = slice(hf * 512, (hf + 1) * 512)
            pa = psA.next()
            for dc in range(DC):
                P.op("pe", lambda e, pa=pa, wb=wb, dc=dc, sl=sl: e.matmul(pa.t[:], lhsT=wb.t[:, dc, :], rhs=hT.t[:, dc, sl],
                                                                         start=(dc == 0), stop=(dc == DC - 1)),
                     [wb.b, hT.b], [pa.b])
            xs = xs_r.next()
            P.op("act", lambda e, xs=xs, pa=pa: e.copy(out=xs.t[:], in_=pa.t[:]), [pa.b], [xs.b])
            if kind != 2:
                sq = sq_r.next()
                P.op("act", lambda e, sq=sq, xs=xs: e.activation(out=sq.t[:], in_=xs.t[:], func=AF.Square), [xs.b], [sq.b])
                pm = psM.next()
                P.op("pe", lambda e, pm=pm, sq=sq, mm_k=mm_k: e.matmul(pm.t[:], lhsT=cm.t[:, mm_k, :], rhs=sq.t[:],
                                                                      start=True, stop=True), [cm.b, sq.b], [pm.b])
                rr = rr_r.next()
                P.op("act", lambda e, rr=rr, pm=pm: e.activation(out=rr.t[:], in_=pm.t[:], func=AF.Sqrt,
                                                                 bias=epsb.t[:, 0:1]), [pm.b, epsb.b], [rr.b])
                P.op("dve", lambda e, rr=rr: e.reciprocal(out=rr.t[:], in_=rr.t[:]), [rr.b], [rr.b])
                y = y_r.next()
                P.op("dve", lambda e, y=y, xs=xs, rr=rr, j=j: e.scalar_tensor_tensor(
                    out=y.t[:], in0=xs.t[:], scalar=gqt.t[:, j:j + 1], in1=rr.t[:], op0=ALU.mult, op1=ALU.mult),
                    [xs.b, gqt.b, rr.b], [y.b])
            else:
                y = xs
            pw = psW.next()
            P.op("pe", lambda e, pw=pw, y=y, perm_k=perm_k: e.matmul(pw.t[:], lhsT=cm.t[:, perm_k, :], rhs=y.t[:],
                                                                    start=True, stop=True), [cm.b, y.b], [pw.b])
            t1 = t1_r.next()
            P.op("pool", lambda e, t1=t1, y=y, sl=sl, cosT=cosT: e.tensor_tensor(out=t1.t[:], in0=y.t[:], in1=cosT.t[:, sl],
                                                                               op=ALU.mult), [y.b, cosT.b], [t1.b])
            t2 = t2_r.next()
            P.op("dve", lambda e, t2=t2, pw=pw, sl=sl, sinT=sinT: e.tensor_tensor(out=t2.t[:], in0=pw.t[:], in1=sinT.t[:, sl],
                                                                                 op=ALU.mult), [pw.b, sinT.b], [t2.b])
            ob = ob_r.next()
            P.op("dve", lambda e, ob=ob, t1=t1, t2=t2: e.tensor_tensor(out=ob.t[:], in0=t1.t[:], in1=t2.t[:], op=ALU.add),
                 [t1.b, t2.b], [ob.b])
            cx.store(fT[j, :, sl], ob.t[:], ob.b)

    vst_r = cx.ring(2, [128, DC, 256], F32, "vst")
    vbf = cx.ring(2, [128, DC, 256], BF16, "vbf")
    vo_r = cx.ring(3, [128, 256], BF16, "vo")

    def issue_wV(cc):
        v_ = vst_r.next()
        cx.load(v_.t[:], wV[cc], v_.b, eng="sp")
        return v_

    pend_v = [issue_wV(0), issue_wV(1)]
    for cb in range(8):
        vst = pend_v.pop(0)
        vb = vbf.next()
        P.op("dve" if cb % 2 else "pool", lambda e, vb=vb, vst=vst: e.tensor_copy(out=vb.t[:], in_=vst.t[:]), [vst.b], [vb.b])
        if cb + 2 < 8:
            pend_v.append(issue_wV(cb + 2))
        for tt in range(8):
            pv = psV.next()
            for dc in range(DC):
                P.op("pe", lambda e, pv=pv, vb=vb, dc=dc, tt=tt: e.matmul(pv.t[:, 0:256], lhsT=hT.t[:, dc, tt * 128:(tt + 1) * 128],
                                                                         rhs=vb.t[:, dc, :], start=(dc == 0), stop=(dc == DC - 1)),
                     [hT.b, vb.b], [pv.b])
            vo = vo_r.next()
            P.op("act", lambda e, vo=vo, pv=pv: e.copy(out=vo.t[:], in_=pv.t[:, 0:256]), [pv.b], [vo.b])
            cx.store(vtok[tt * 128:(tt + 1) * 128, cb * 256:(cb + 1) * 256], vo.t[:], vo.b)
    wis = cx.sb([128, DC, 8], F32, "wis")
    cx.load(wis.t[:], wI, wis.b)
    wib = cx.sb([128, DC, 8], BF16, "wib")
    P.op("dve", lambda e: e.tensor_copy(out=wib.t[:], in_=wis.t[:]), [wis.b], [wib.b])
    io_r = cx.ring(2, [128, 8], F32, "io")
    for tt in range(8):
        pv = psV.next()
        for dc in range(DC):
            P.op("pe", lambda e, pv=pv, dc=dc, tt=tt: e.matmul(pv.t[:, 0:8], lhsT=hT.t[:, dc, tt * 128:(tt + 1) * 128],
                                                              rhs=wib.t[:, dc, :], start=(dc == 0), stop=(dc == DC - 1)),
                 [hT.b, wib.b], [pv.b])
        io = io_r.next()
        P.op("act", lambda e, io=io, pv=pv: e.activation(out=io.t[:], in_=pv.t[:, 0:8], func=AF.Copy, scale=8 ** -0.5),
             [pv.b], [io.b])
        cx.store(iwo[tt * 128:(tt + 1) * 128, :], io.t[:], io.b)
    return cx.finish()


def tok_index(c):
    i = np.arange(8)[:, None]
    r = np.arange(128)[None, :]
    return ((8 * i + c) * 128 + r).reshape(-1)


def fm(a):
    T, F = a.shape
    return np.ascontiguousarray(a.T.reshape(F // 128, 128, T))


def wtile(w):
    return np.ascontiguousarray(w.reshape(DC, 128, w.shape[1]).transpose(1, 0, 2))


def rope_consts():
    rc = np.zeros((128, 4), np.float32)
    p = np.arange(128)
    inv128 = np.float32(10000.0) ** (-(np.arange(64, dtype=np.float32) * np.float32(2.0) / np.float32(128)))
    inv64 = np.float32(10000.0) ** (-(np.arange(32, dtype=np.float32) * np.float32(2.0) / np.float32(64)))
    rc[:, 0] = inv128[p % 64]
    rc[:, 1] = inv64[p % 32]
    rc[:, 2] = np.where(p < 64, -1.0, 1.0)
    rc[:, 3] = np.where((p % 64) < 32, -1.0, 1.0)
    cm = np.zeros((4, 128, 128), np.float32)
    m = np.arange(128)
    cm[0, (m + 64) % 128, m] = 1.0
    cm[1, 64 * (m // 64) + ((m % 64) + 32) % 64, m] = 1.0
    cm[2] = 1.0 / 128
    cm[3, :64, :64] = 1.0 / 64
    cm[3, 64:, 64:] = 1.0 / 64
    return rc, cm


def prep_A_weights(inp, l):
    w_in = inp["w_in"][l]
    wF = np.zeros((NF, 128, DC, 128), np.float32)
    gq = np.ones((128, NF), np.float32)
    gains = {"mq": inp["moba_qk_gain"][l, 0], "mk": inp["moba_qk_gain"][l, 1],
             "dq": np.tile(inp["diff_qk_gain"][l, 0], 2), "dk": np.tile(inp["diff_qk_gain"][l, 1], 2),
             "sq": inp["dsa_qk_gain"][l, 0], "sk": inp["dsa_qk_gain"][l, 1]}
    for j, (nm, c0, kind) in enumerate(FCH):
        ncol = 64 if nm == "ik" else 128
        wF[j, :, :, :ncol] = wtile(w_in[:, c0:c0 + ncol])
        if nm in gains:
            gq[:, j] = gains[nm]
    wv = np.concatenate([w_in[:, OFF["mv"]:OFF["mv"] + 768], w_in[:, OFF["dv"]:OFF["dv"] + 512],
                         w_in[:, OFF["sv"]:OFF["sv"] + 768]], axis=1)
    wV = np.stack([wtile(wv[:, 256 * k:256 * (k + 1)]) for k in range(8)])
    wI = wtile(w_in[:, OFF["iw"]:OFF["iw"] + 8])
    rc, cm = rope_consts()
    return dict(wF=wF, wV=wV, wI=wI, gq=gq, rc=rc, cm=cm,
                gnorm=np.ascontiguousarray(inp["attn_norm"][l].reshape(DC, 128).T))


TLH = 8 * 130
CGRP = ((0, 3), (3, 3), (6, 2))


def build_C(dbg=False):
    cx = Ctx()
    P = cx.P
    x2h = cx.dram("x2h", [DC, 128, TLH], F32)
    gnorm = cx.dram("gnorm", [128, DC], F32)
    wU = cx.dram("wU", [2 * FC, 128, DC, 128], F32)
    cwd = cx.dram("cw", [128, 2 * FC, 4], F32)
    wD = cx.dram("wD", [DC, 3, 128, 16, 128], F32)
    cmd = cx.dram("cm", [4, 128, 128], F32)
    x3T = cx.dram("x3T", [DC, 128, TL], F32, out=True)

    gn = cx.sb([128, DC], F32, "gn")
    cx.load(gn.t[:], gnorm, gn.b)
    cw = cx.sb([128, 2 * FC, 4], F32, "cw")
    cx.load(cw.t[:], cwd, cw.b)
    cm = cx.sb([128, 128], F32, "cm")
    cx.load(cm.t[:], cmd[2], cm.b)
    epsb = cx.sb([128, 1], F32, "epsb")
    P.op("dve", lambda e: e.memset(epsb.t[:], EPS), [], [epsb.b])

    psU = cx.ring(6, [128, 512], F32, "psU", psum=True)
    psD = cx.ring(2, [128, 512], F32, "psD", psum=True)
    hT = cx.sb([128, DC, TLH], BF16, "hT")
    actT = cx.sb([128, FC, TL], BF16, "actT")
    xr = cx.ring(3, [128, TLH], F32, "xr")
    sqr = cx.ring(2, [128, TLH], F32, "sqr")
    ms = [psU.next() for _ in range(3)]
    for dc in range(DC):
        x_ = xr.next()
        cx.load(x_.t[:], x2h[dc], x_.b)
        s_ = sqr.next()
        P.op("act", lambda e, x_=x_, s_=s_: e.activation(out=s_.t[:], in_=x_.t[:], func=AF.Square), [x_.b], [s_.b])
        for (t0, nt), m_ in zip(CGRP, ms):
            P.op("pe", lambda e, m_=m_, s_=s_, t0=t0, nt=nt, dc=dc: e.matmul(
                m_.t[:, 0:nt * 130], lhsT=cm.t[:], rhs=s_.t[:, t0 * 130:(t0 + nt) * 130], start=(dc == 0),
                stop=(dc == DC - 1)), [cm.b, s_.b], [m_.b])
    rstd = cx.sb([128, TLH], F32, "rstd")
    for (t0, nt), m_ in zip(CGRP, ms):
        P.op("act", lambda e, m_=m_, t0=t0, nt=nt: e.activation(out=rstd.t[:, t0 * 130:(t0 + nt) * 130], in_=m_.t[:, 0:nt * 130],
                                                               func=AF.Sqrt, scale=1.0 / DC, bias=epsb.t[:, 0:1]),
             [m_.b, epsb.b], [rstd.b])
    P.op("dve", lambda e: e.reciprocal(out=rstd.t[:], in_=rstd.t[:]), [rstd.b], [rstd.b])
    for dc in range(DC):
        x_ = xr.next()
        cx.load(x_.t[:], x2h[dc], x_.b)
        P.op("dve", lambda e, x_=x_, dc=dc: e.scalar_tensor_tensor(
            out=hT.t[:, dc, :], in0=x_.t[:], scalar=gn.t[:, dc:dc + 1], in1=rstd.t[:], op0=ALU.mult, op1=ALU.mult),
            [x_.b, gn.b, rstd.b], [hT.b])

    wst = cx.ring(3, [128, DC, 128], F32, "wst")
    wbf = cx.ring(2, [128, DC, 128], BF16, "wbf")
    wsrc = []
    for _fc in range(FC):
        wsrc.append(wU[_fc])
        wsrc.append(wU[_fc + FC])
    for _d in range(DC):
        for _g in range(3):
            wsrc.append(wD[_d, _g])
    wpos = [0]

    def issue_w():
        if wpos[0] >= len(wsrc):
            return None
        w_ = wst.next()
        cx.load(w_.t[:], wsrc[wpos[0]], w_.b, eng="sp")
        wpos[0] += 1
        return w_

    pend_w = [issue_w() for _ in range(3)]
    c0_r = cx.ring(2, [128, 3, 128], F32, "c0")
    c1_r = cx.ring(2, [128, 3, 128], F32, "c1")
    gv_r = cx.ring(2, [128, TL], F32, "gv")
    nload = 0
    for fc in range(FC):
        gsil = gv_r.next()
        for which in (0, 1):
            ch = fc + which * FC
            ws = pend_w.pop(0)
            wb = wbf.next()
            P.op("pool" if nload % 2 == 0 else "dve", lambda e, ws=ws, wb=wb: e.tensor_copy(out=wb.t[:], in_=ws.t[:]),
                 [ws.b], [wb.b])
            nload += 1
            pend_w.append(issue_w())
            for (t0, nt) in CGRP:
                pu = psU.next()
                for dc in range(DC):
                    P.op("pe", lambda e, pu=pu, wb=wb, dc=dc, t0=t0, nt=nt: e.matmul(
                        pu.t[:, 0:nt * 130], lhsT=wb.t[:, dc, :], rhs=hT.t[:, dc, t0 * 130:(t0 + nt) * 130],
                        start=(dc == 0), stop=(dc == DC - 1)), [wb.b, hT.b], [pu.b])
                uv = pu.t[:, 0:nt * 130].rearrange("p (t c) -> p t c", c=130)
                c0 = c0_r.next()
                P.op("act", lambda e, c0=c0, uv=uv, nt=nt, ch=ch: e.activation(
                    out=c0.t[:, 0:nt, :], in_=uv[:, :, 2:130], func=AF.Identity, scale=cw.t[:, ch, 2:3],
                    bias=cw.t[:, ch, 3:4]), [pu.b, cw.b], [c0.b])
                c1 = c1_r.next()
                P.op("dve", lambda e, c0=c0, c1=c1, uv=uv, nt=nt, ch=ch: e.scalar_tensor_tensor(
                    out=c1.t[:, 0:nt, :], in0=uv[:, :, 1:129], scalar=cw.t[:, ch, 1:2], in1=c0.t[:, 0:nt, :],
                    op0=ALU.mult, op1=ALU.add), [pu.b, cw.b, c0.b], [c1.b])
                osl = slice(t0 * 128, (t0 + nt) * 128)
                if which == 0:
                    gview = gsil.t[:, osl].rearrange("p (t c) -> p t c", c=128)
                    P.op("dve", lambda e, c1=c1, uv=uv, nt=nt, ch=ch, gview=gview: e.scalar_tensor_tensor(
                        out=gview, in0=uv[:, :, 0:128], scalar=cw.t[:, ch, 0:1], in1=c1.t[:, 0:nt, :],
                        op0=ALU.mult, op1=ALU.add), [pu.b, cw.b, c1.b], [gsil.b])
                    P.op("act", lambda e, osl=osl, gsil=gsil: e.activation(out=gsil.t[:, osl], in_=gsil.t[:, osl], func=AF.Silu),
                         [gsil.b], [gsil.b])
                else:
                    P.op("dve", lambda e, c1=c1, c0=c0, uv=uv, nt=nt, ch=ch: e.scalar_tensor_tensor(
                        out=c0.t[:, 0:nt, :], in0=uv[:, :, 0:128], scalar=cw.t[:, ch, 0:1], in1=c1.t[:, 0:nt, :],
                        op0=ALU.mult, op1=ALU.add), [pu.b, cw.b, c1.b], [c0.b])
                    aview = actT.t[:, fc, osl].rearrange("p (t c) -> p t c", c=128)
                    gview = gsil.t[:, osl].rearrange("p (t c) -> p t c", c=128)
                    P.op("pool", lambda e, c0=c0, nt=nt, aview=aview, gview=gview: e.tensor_tensor(
                        out=aview, in0=c0.t[:, 0:nt, :], in1=gview, op=ALU.mult), [c0.b, gsil.b], [actT.b])

    if dbg:
        dA = cx.dram('dbgA', [128, FC, TL], BF16, out=True)
        cx.store(dA, actT.t[:], actT.b)
        dH = cx.dram('dbgH', [128, DC, TLH], BF16, out=True)
        cx.store(dH, hT.t[:], hT.b)
    xo_r = cx.ring(2, [128, TL], F32, "xo")
    xres_r = cx.ring(2, [128, 8, 128], F32, "xres")
    for dcc in range(DC):
        xres = xres_r.next()
        cx.load(xres.t[:], x2h[dcc].rearrange("p (t c) -> p t c", c=130)[:, :, 2:130], xres.b)
        pd = [psD.next(), psD.next()]
        for gi in range(3):
            nk = 16 if gi < 2 else FC - 32
            ws = pend_w.pop(0)
            wb = wbf.next()
            P.op("pool" if nload % 2 == 0 else "dve", lambda e, ws=ws, wb=wb: e.tensor_copy(out=wb.t[:], in_=ws.t[:]),
                 [ws.b], [wb.b])
            nload += 1
            pend_w.append(issue_w())
            for k in range(nk):
                fc = gi * 16 + k
                for hf in range(2):
                    P.op("pe", lambda e, hf=hf, wb=wb, k=k, fc=fc, pd=pd: e.matmul(
                        pd[hf].t[:], lhsT=wb.t[:, k, :], rhs=actT.t[:, fc, hf * 512:(hf + 1) * 512],
                        start=(fc == 0), stop=(fc == FC - 1)), [wb.b, actT.b], [pd[hf].b])
        xo = xo_r.next()
        for hf in range(2):
            P.op("dve", lambda e, hf=hf, xo=xo, xres=xres, pd=pd: e.tensor_tensor(
                out=xo.t[:, hf * 512:(hf + 1) * 512], in0=pd[hf].t[:],
                in1=xres.t[:, hf * 4:(hf + 1) * 4, :].rearrange("p t c -> p (t c)"), op=ALU.add), [pd[hf].b, xres.b], [xo.b])
        cx.store(x3T[dcc], xo.t[:], xo.b)
    return cx.finish()


def prep_C_weights(inp, l):
    wup = inp["ffn_w_up"][l]
    wU = np.ascontiguousarray(wup.reshape(DC, 128, 2 * FC, 128).transpose(2, 1, 0, 3))
    cwv = np.concatenate([inp["ffn_conv_w"][l], inp["ffn_conv_b"][l][None]], 0)
    cw = np.ascontiguousarray(cwv.reshape(4, 2 * FC, 128).transpose(2, 1, 0))
    wd = inp["ffn_w_down"][l]
    wdp = np.zeros((48 * 128, D_MODEL), np.float32)
    wdp[:D_FF] = wd
    wD = np.ascontiguousarray(wdp.reshape(3, 16, 128, DC, 128).transpose(3, 0, 2, 1, 4))
    rc, cm = rope_consts()
    return dict(wU=wU, cw=cw, wD=wD, cm=cm, gnorm=np.ascontiguousarray(inp["ffn_norm"][l].reshape(DC, 128).T))


def halo_cols(xfull_T, c):
    out = np.zeros((D_MODEL, TLH), np.float32)
    for i in range(8):
        t0 = (8 * i + c) * 128
        lo = max(t0 - 2, 0)
        out[:, i * 130 + (2 - (t0 - lo)):i * 130 + 130] = xfull_T[:, lo:t0 + 128]
    return np.ascontiguousarray(out.reshape(DC, 128, TLH))


def _mm(cx, out, lhsT, rhs, start, stop, reads, writes, skip=False):
    if skip:
        return cx.P.op("pe", lambda e: e.matmul(out, lhsT=lhsT, rhs=rhs, start=start, stop=stop, skip_group_check=True),
                       reads, writes)
    return cx.P.op("pe", lambda e: e.matmul(out, lhsT=lhsT, rhs=rhs, start=start, stop=stop), reads, writes)


def _tr(cx, out, in_, ident, reads, writes):
    return cx.P.op("pe", lambda e: e.transpose(out=out, in_=in_, identity=ident), reads, writes)


def _act(cx, out, in_, func, reads, writes, **kw):
    return cx.P.op("act", lambda e: e.activation(out=out, in_=in_, func=func, **kw), reads, writes)


def _tt(cx, eng, out, in0, in1, op, reads, writes):
    return cx.P.op(eng, lambda e: e.tensor_tensor(out=out, in0=in0, in1=in1, op=op), reads, writes)


def _ts(cx, eng, out, in0, s1, s2, op0, op1, reads, writes, accum_out=None):
    if op1 is None:
        return cx.P.op(eng, lambda e: e.tensor_scalar(out=out, in0=in0, scalar1=s1, scalar2=None, op0=op0), reads, writes)
    if accum_out is not None:
        return cx.P.op(eng, lambda e: e.tensor_scalar(out=out, in0=in0, scalar1=s1, scalar2=s2, op0=op0, op1=op1,
                                                      accum_out=accum_out), reads, writes)
    return cx.P.op(eng, lambda e: e.tensor_scalar(out=out, in0=in0, scalar1=s1, scalar2=s2, op0=op0, op1=op1), reads, writes)


def _stt(cx, out, in0, scalar, in1, op0, op1, reads, writes):
    return cx.P.op("dve", lambda e: e.scalar_tensor_tensor(out=out, in0=in0, scalar=scalar, in1=in1, op0=op0, op1=op1),
                   reads, writes)


def _cp(cx, eng, out, in_, reads, writes):
    if eng == "act":
        return cx.P.op("act", lambda e: e.copy(out=out, in_=in_), reads, writes)
    return cx.P.op(eng, lambda e: e.tensor_copy(out=out, in_=in_), reads, writes)


def _recip(cx, out, in_, reads, writes):
    return cx.P.op("dve", lambda e: e.reciprocal(out=out, in_=in_), reads, writes)


def _memset(cx, eng, out, val, writes):
    return cx.P.op(eng, lambda e: e.memset(out, val), [], writes)


class Scope:
    def __init__(self, cx):
        self.cx = cx

    def __enter__(self):
        self.saved = self.cx.st
        self.cx.st = ExitStack()
        return self

    def __exit__(self, *a):
        cx = self.cx
        cx.st.close()
        cx.st = self.saved
        last = {}
        dmas = []
        for o in cx.P.ops:
            if o.dma:
                dmas.append(o)
            else:
                last[o.eng] = o
        cx.fence = list(last.values()) + dmas[-Prog.NDS:]
        return False


def _fenced_tb(cx, tb):
    f = getattr(cx, "fence", None)
    if f:
        tb.b.rs = list(f)
    return tb


_orig_sb = Ctx.sb
_orig_ps = Ctx.ps
Ctx.sb = lambda self, shape, dt, name=None: _fenced_tb(self, _orig_sb(self, shape, dt, name))
Ctx.ps = lambda self, shape, dt=F32, name=None: _fenced_tb(self, _orig_ps(self, shape, dt, name))


NKT = 64
OFFK = [0]
for _kt in range(NKT):
    OFFK.append(OFFK[-1] + 8 - _kt // 8)
NSLOT = OFFK[-1]
QCH = {"mq": 0, "dq": 6, "sq": 10, "iq": 16}
KCH = {"mk": 0, "dk": 6, "sk": 10, "ik": 16}
NQC = 20
NKC = 17
BIG = 1.0e30
BIGB = 30000.0


def attn_pass(cx, env, KT, krows, QT, qrows, Vt, scale, key_tiles, mode, O, R, maskT=None, biasT=None, selrows=None,
              suffix=True):
    pss, ptr = env["pss"], env["ptr"]
    ones, Mj = env["ones"], env["Mj"]
    nkt = len(key_tiles)

    def stage1(n, kt):
        imin = (kt // 8) if suffix else 0
        c0 = imin * 128
        groups = []
        a = c0
        while a < TL:
            b = min(TL, (a // 512 + 1) * 512)
            groups.append((a, b))
            a = b
        pt = ptr.next()
        for (a, b) in groups:
            ps = pss.next()
            w = b - a
            last = (mode != "moba")
            _mm(cx, ps.t[:, 0:w], KT.t[krows, kt * 128:(kt + 1) * 128], QT.t[qrows, a:b], True, last,
                [KT.b, QT.b], [ps.b])
            if mode == "moba":
                nb = kt // 2
                _mm(cx, ps.t[:, 0:w], selrows.t[0:32, nb:nb + 1].to_broadcast([32, 128]), biasT.t[:, a:b], False, True,
                    [selrows.b, biasT.b], [ps.b])
            _act(cx, pt.t[:, a:b], ps.t[:, 0:w], AF.Exp, [ps.b], [pt.b], scale=scale)
        if mode in ("causal", "moba"):
            j = kt % 8
            _tt(cx, "pool", pt.t[:, c0:c0 + 128], pt.t[:, c0:c0 + 128], Mj.t[:, j, :], ALU.mult, [pt.b, Mj.b], [pt.b])
        elif mode == "dsa":
            nq = TL - c0
            mv = maskT.t[:, OFFK[kt]:OFFK[kt] + nq // 128, :].rearrange("p s q -> p (s q)")
            _tt(cx, "dve", pt.t[:, c0:TL], pt.t[:, c0:TL], mv, ALU.mult, [pt.b, maskT.b], [pt.b])
        return pt, groups

    def stage2(n, pt, groups):
        for (a, b) in groups:
            bank = a // 512
            o0 = a - bank * 512
            w = b - a
            _mm(cx, O[bank].t[:, o0:o0 + w], Vt.t[:, n, :], pt.t[:, a:b], n == 0, n == nkt - 1, [Vt.b, pt.b], [O[bank].b],
                skip=True)
            _mm(cx, R[bank].t[:, o0:o0 + w], ones.t[:], pt.t[:, a:b], n == 0, n == nkt - 1, [ones.b, pt.b], [R[bank].b],
                skip=True)

    prev = None
    for n, kt in enumerate(key_tiles):
        pt, groups = stage1(n, kt)
        if prev is not None:
            stage2(*prev)
        prev = (n, pt, groups)
    stage2(*prev)


def build_B(dbg=False):
    cx = Ctx()
    P = cx.P
    xT = cx.dram("xT", [DC, 128, TL], F32)
    qT = cx.dram("qT", [NQC, 128, TL], BF16)
    kT = cx.dram("kT", [NKC, 128, 8192], BF16)
    vG = cx.dram("vG", [16, 128, NKT, 128], BF16)
    iwd = cx.dram("iw", [TL, 8], F32)
    pcd = cx.dram("pc", [128, 4], F32)
    wOd = cx.dram("wO", [DC, 128, DC, 128], F32)
    dld = cx.dram("dl", [1, 256], F32)
    sld = cx.dram("sl", [128, 1], F32)
    cmd = cx.dram("cm", [4, 128, 128], F32)
    gcd = cx.dram("gc", [128, 2 * DC + 2], F32)
    memd = cx.dram("memT", [DC, 128, 256], F32)
    wqd = cx.dram("wq", [4, 128, DC, 128], F32)
    wkd = cx.dram("wk", [4, 128, DC, 128], F32)
    wvd = cx.dram("wv", [4, 128, DC, 128], F32)
    wcod = cx.dram("wco", [DC, 128, 4, 128], F32)
    x2T = cx.dram("x2T", [DC, 128, TL], F32, out=True)

    pc = cx.sb([128, 4], F32, "pc")
    cx.load(pc.t[:], pcd, pc.b)
    cm = cx.sb([128, 4, 128], F32, "cm")
    for k in range(4):
        cx.load(cm.t[:, k, :], cmd[k], cm.b)
    epsb = cx.sb([128, 1], F32, "epsb")
    _memset(cx, "dve", epsb.t[:], EPS, [epsb.b])
    dkq = cx.sb([128, 128], F32, "dkq")
    P.op("pool", lambda e: e.iota(dkq.t[:], [[-1, 128]], base=0, channel_multiplier=1,
                                  allow_small_or_imprecise_dtypes=True), [], [dkq.b])
    ident = cx.sb([128, 128], BF16, "ident")
    _ts(cx, "dve", ident.t[:], dkq.t[:], 0.0, None, ALU.is_equal, None, [dkq.b], [ident.b])
    identf = cx.sb([128, 128], F32, "identf")
    _ts(cx, "dve", identf.t[:], dkq.t[:], 0.0, None, ALU.is_equal, None, [dkq.b], [identf.b])
    ones = cx.sb([128, 128], BF16, "ones")
    _memset(cx, "dve", ones.t[:], 1.0, [ones.b])
    thrj = cx.sb([128, 8], F32, "thrj")
    for j in range(8):
        _ts(cx, "dve", thrj.t[:, j:j + 1], pc.t[:, 0:1], 128.0, -128.0 * j, ALU.mult, ALU.add, [pc.b], [thrj.b])
    Mj = cx.sb([128, 8, 128], BF16, "Mj")
    for j in range(8):
        _ts(cx, "dve", Mj.t[:, j, :], dkq.t[:], thrj.t[:, j:j + 1], None, ALU.is_le, None, [dkq.b, thrj.b], [Mj.b])
    iwt = cx.sb([128, 8, 8], F32, "iwt")
    cx.load(iwt.t[:], iwd.rearrange("(i p) h -> p i h", p=128), iwt.b)
    dl = cx.sb([128, 256], F32, "dl")
    cx.load(dl.t[:], dld.partition_broadcast(128), dl.b)
    lam = cx.sb([128, 4], F32, "lam")
    dlp = cx.sb([128, 2, 64], F32, "dlp")
    _tt(cx, "dve", dlp.t[:, 0, :], dl.t[:, 0:64], dl.t[:, 64:128], ALU.mult, [dl.b], [dlp.b])
    _tt(cx, "dve", dlp.t[:, 1, :], dl.t[:, 128:192], dl.t[:, 192:256], ALU.mult, [dl.b], [dlp.b])
    P.op("dve", lambda e: e.reduce_sum(out=lam.t[:, 0:2], in_=dlp.t[:], axis=AX.X), [dlp.b], [lam.b])
    _act(cx, lam.t[:, 0:2], lam.t[:, 0:2], AF.Exp, [lam.b], [lam.b])
    _tt(cx, "dve", lam.t[:, 2:3], lam.t[:, 0:1], lam.t[:, 1:2], ALU.subtract, [lam.b], [lam.b])
    _ts(cx, "dve", lam.t[:, 3:4], lam.t[:, 2:3], pc.t[:, 2:3], -1.0, ALU.add, ALU.mult, [lam.b, pc.b], [lam.b])
    sl = cx.sb([128, 1], F32, "sl")
    cx.load(sl.t[:], sld, sl.b)
    slg = cx.sb([128, 1], F32, "slg")
    _tt(cx, "dve", slg.t[:], sl.t[:], pc.t[:, 3:4], ALU.mult, [sl.b, pc.b], [slg.b])
    gc = cx.sb([128, 2 * DC + 2], F32, "gc")
    cx.load(gc.t[:], gcd, gc.b)

    mixedT = cx.sb([128, DC, TL], BF16, "mixedT")
    outer = Scope(cx)
    outer.__enter__()
    maskT = cx.sb([128, NSLOT, 128], BF16, "maskT")

    with Scope(cx):
        pss = cx.ring(4, [128, 512], F32, "pss", psum=True)
        pst = cx.ring(2, [128, 4, 128], BF16, "pst", psum=True)
        MTj = cx.sb([128, 8, 128], F32, "MTj")
        NEGj = cx.sb([128, 8, 128], F32, "NEGj")
        POSj = cx.sb([128, 8, 128], F32, "POSj")
        for j in range(8):
            _ts(cx, "dve", MTj.t[:, j, :], dkq.t[:], -1.0, thrj.t[:, j:j + 1], ALU.mult, ALU.is_le, [dkq.b, thrj.b], [MTj.b])
        _ts(cx, "dve", NEGj.t[:], MTj.t[:], -1.0, BIG, ALU.add, ALU.mult, [MTj.b], [NEGj.b])
        _ts(cx, "dve", POSj.t[:], NEGj.t[:], -1.0, None, ALU.mult, None, [NEGj.b], [POSj.b])
        kdup = cx.sb([128, 8192], BF16, "kdup")
        cx.load(kdup.t[0:64, :], kT[KCH["ik"], 0:64, :], kdup.b)
        cx.load(kdup.t[64:128, :], kT[KCH["ik"], 0:64, :], kdup.b, eng="pool")
        qiT = cx.sb([128, 4, TL], BF16, "qiT")
        for k in range(4):
            cx.load(qiT.t[:, k, :], qT[QCH["iq"] + k], qiT.b)
        Isc = cx.sb([128, 8192], F32, "Isc")
        msk = cx.sb([128, 8192], BF16, "msk")
        rl_r = cx.ring(3, [128, 512], F32, "rl")
        st_r = cx.ring(2, [128, 8], F32, "bst")
        dg_r = cx.ring(1, [128, 1024], F32, "dg")
        for i in range(8):
            nk = 8 * i + 8
            nkeys = nk * 128
            for cg in range(nk // 4):
                ks = slice(cg * 512, (cg + 1) * 512)
                for h in range(8):
                    rows = slice(64 * (h % 2), 64 * (h % 2) + 64)
                    ps = pss.next()
                    _mm(cx, ps.t[:], qiT.t[rows, h // 2, i * 128:(i + 1) * 128], kdup.t[rows, ks], True, True,
                        [qiT.b, kdup.b], [ps.b])
                    rl = rl_r.next()
                    _act(cx, rl.t[:], ps.t[:], AF.Relu, [ps.b], [rl.b])
                    if h == 0:
                        _ts(cx, "dve", Isc.t[:, ks], rl.t[:], iwt.t[:, i, 0:1], None, ALU.mult, None, [rl.b, iwt.b], [Isc.b])
                    else:
                        _stt(cx, Isc.t[:, ks], rl.t[:], iwt.t[:, i, h:h + 1], Isc.t[:, ks], ALU.mult, ALU.add,
                             [rl.b, iwt.b, Isc.b], [Isc.b])
            dsl = slice(nkeys - 1024, nkeys)
            dg = dg_r.next()
            _tt(cx, "dve", dg.t[:], Isc.t[:, dsl], MTj.t[:].rearrange("p j k -> p (j k)"), ALU.mult, [Isc.b, MTj.b], [dg.b])
            _tt(cx, "dve", Isc.t[:, dsl], dg.t[:], NEGj.t[:].rearrange("p j k -> p (j k)"), ALU.add, [dg.b, NEGj.b], [Isc.b])
            _tt(cx, "dve", dg.t[:], dg.t[:], POSj.t[:].rearrange("p j k -> p (j k)"), ALU.add, [dg.b, POSj.b], [dg.b])
            bs = st_r.next()
            P.op("dve", lambda e, bs=bs, nkeys=nkeys: e.tensor_reduce(out=bs.t[:, 1:2], in_=Isc.t[:, 0:nkeys], axis=AX.X,
                                                                     op=ALU.max), [Isc.b], [bs.b])
            P.op("dve", lambda e, bs=bs, dg=dg: e.tensor_reduce(out=bs.t[:, 0:1], in_=dg.t[:], axis=AX.X, op=ALU.min),
                 [dg.b], [bs.b])
            if i > 0:
                P.op("dve", lambda e, bs=bs, nkeys=nkeys: e.tensor_reduce(out=bs.t[:, 5:6], in_=Isc.t[:, 0:nkeys - 1024],
                                                                         axis=AX.X, op=ALU.min), [Isc.b], [bs.b])
                _tt(cx, "dve", bs.t[:, 0:1], bs.t[:, 0:1], bs.t[:, 5:6], ALU.min, [bs.b], [bs.b])
            _ts(cx, "dve", bs.t[:, 1:2], bs.t[:, 1:2], 1.0, None, ALU.add, None, [bs.b], [bs.b])
            for it in range(18):
                _tt(cx, "dve", bs.t[:, 2:3], bs.t[:, 0:1], bs.t[:, 1:2], ALU.add, [bs.b], [bs.b])
                _ts(cx, "dve", bs.t[:, 2:3], bs.t[:, 2:3], 0.5, None, ALU.mult, None, [bs.b], [bs.b])
                _memset(cx, "dve", bs.t[:, 3:4], 0.0, [bs.b])
                _ts(cx, "dve", msk.t[:, 0:nkeys], Isc.t[:, 0:nkeys], bs.t[:, 2:3], 0.0, ALU.is_ge, ALU.add, [Isc.b, bs.b],
                    [msk.b, bs.b], accum_out=bs.t[:, 3:4])
                _ts(cx, "dve", bs.t[:, 4:5], bs.t[:, 3:4], 256.0, None, ALU.is_ge, None, [bs.b], [bs.b])
                _tt(cx, "dve", bs.t[:, 5:6], bs.t[:, 2:3], bs.t[:, 0:1], ALU.subtract, [bs.b], [bs.b])
                _stt(cx, bs.t[:, 0:1], bs.t[:, 5:6], bs.t[:, 4:5], bs.t[:, 0:1], ALU.mult, ALU.add, [bs.b], [bs.b])
                _tt(cx, "dve", bs.t[:, 5:6], bs.t[:, 1:2], bs.t[:, 2:3], ALU.subtract, [bs.b], [bs.b])
                _stt(cx, bs.t[:, 1:2], bs.t[:, 5:6], bs.t[:, 4:5], bs.t[:, 2:3], ALU.mult, ALU.add, [bs.b], [bs.b])
            _ts(cx, "dve", msk.t[:, 0:nkeys], Isc.t[:, 0:nkeys], bs.t[:, 0:1], None, ALU.is_ge, None, [Isc.b, bs.b], [msk.b])
            for g in range(nk // 4):
                pT = pst.next()
                for jj in range(4):
                    kt = g * 4 + jj
                    _tr(cx, pT.t[:, jj, :], msk.t[:, kt * 128:(kt + 1) * 128], ident.t[:], [msk.b, ident.b], [pT.b])
                for jj in range(4):
                    kt = g * 4 + jj
                    slot = OFFK[kt] + (i - kt // 8)
                    _cp(cx, "act" if jj % 2 else "dve", maskT.t[:, slot, :], pT.t[:, jj, :], [pT.b], [maskT.b])

    with Scope(cx):
        env = dict(pss=cx.ring(4, [128, 512], F32, "pss", psum=True), ptr=cx.ring(3, [128, TL], BF16, "pt"),
                   ones=ones, Mj=Mj)
        O = [cx.ps([128, 512], F32, "O0"), cx.ps([128, 512], F32, "O1")]
        R = [cx.ps([128, 512], F32, "R0"), cx.ps([128, 512], F32, "R1")]
        KT_r = cx.ring(2, [128, 8192], BF16, "KT")
        V_r = cx.ring(2, [128, NKT, 128], BF16, "V")
        QT_r = cx.ring(2, [128, TL], BF16, "QT")
        rinv_r = cx.ring(1, [128, TL], F32, "rinv")
        o1 = cx.sb([128, TL], F32, "o1")
        o2 = cx.sb([128, TL], F32, "o2")
        sqd = rinv_r.items[0]
        nidx = cx.sb([128, 32], F32, "nidx")
        P.op("pool", lambda e: e.iota(nidx.t[:], [[1, 32]], base=0, channel_multiplier=0,
                                      allow_small_or_imprecise_dtypes=True), [], [nidx.b])
        curv = cx.sb([128, 8], F32, "curv")
        for i in range(8):
            _ts(cx, "dve", curv.t[:, i:i + 1], pc.t[:, 1:2], 4.0 * i, None, ALU.add, None, [pc.b], [curv.b])
        selrows = ident
        biasT = cx.sb([32, TL], BF16, "biasT")
        kb = cx.sb([128, 32], F32, "kb")
        qf_r = cx.ring(2, [128, 128], F32, "qf")
        gt = cx.sb([128, 6, 32], F32, "gt")
        m8 = cx.sb([128, 8], F32, "m8")
        all_kt = list(range(NKT))

        def load_head(kc, krow_all, qc, vh):
            KT = KT_r.next()
            cx.load(KT.t[:], kT[kc], KT.b, eng="sp")
            Vt = V_r.next()
            cx.load(Vt.t[:], vG[vh], Vt.b, eng="sp")
            QT = QT_r.next()
            cx.load(QT.t[:], qT[qc], QT.b, eng="sp")
            return KT, Vt, QT

        def finalize(dst_ap, dst_b, to_f32_tb=None):
            ri = rinv_r.next()
            for bk in range(2):
                sl_ = slice(bk * 512, (bk + 1) * 512)
                _recip(cx, ri.t[:, sl_], R[bk].t[:], [R[bk].b], [ri.b])
                if to_f32_tb is None:
                    _tt(cx, "dve", dst_ap[:, sl_], O[bk].t[:], ri.t[:, sl_], ALU.mult, [O[bk].b, ri.b], [dst_b])
                else:
                    _tt(cx, "dve", to_f32_tb.t[:, sl_], O[bk].t[:], ri.t[:, sl_], ALU.mult, [O[bk].b, ri.b], [to_f32_tb.b])

        for h in range(6):
            KT, Vt, QT = load_head(KCH["sk"] + h, None, QCH["sq"] + h, 10 + h)
            attn_pass(cx, env, KT, slice(0, 128), QT, slice(0, 128), Vt, 128 ** -0.5, all_kt, "dsa", O, R, maskT=maskT)
            finalize(mixedT.t[:, 10 + h, :], mixedT.b)
        for h in range(6):
            KT, Vt, QT = load_head(KCH["mk"] + h, None, QCH["mq"] + h, h)
            P.op("dve", lambda e, KT=KT: e.tensor_reduce(out=kb.t[:], in_=KT.t[:].rearrange("p (n k) -> p n k", k=256),
                                                         axis=AX.X, op=ALU.add), [KT.b], [kb.b])
            for i in range(8):
                qf = qf_r.next()
                _cp(cx, "pool", qf.t[:], QT.t[:, i * 128:(i + 1) * 128], [QT.b], [qf.b])
                ps = env["pss"].next()
                _mm(cx, ps.t[:, 0:32], qf.t[:], kb.t[:], True, True, [qf.b, kb.b], [ps.b])
                _ts(cx, "dve", gt.t[:, 0, :], nidx.t[:], curv.t[:, i:i + 1], None, ALU.is_lt, None, [nidx.b, curv.b], [gt.b])
                _ts(cx, "dve", gt.t[:, 1, :], nidx.t[:], curv.t[:, i:i + 1], None, ALU.is_equal, None, [nidx.b, curv.b], [gt.b])
                _tt(cx, "dve", gt.t[:, 2, :], ps.t[:, 0:32], gt.t[:, 0, :], ALU.mult, [ps.b, gt.b], [gt.b])
                _ts(cx, "dve", gt.t[:, 3, :], gt.t[:, 0, :], -1.0, BIG, ALU.add, ALU.mult, [gt.b], [gt.b])
                _tt(cx, "dve", gt.t[:, 2, :], gt.t[:, 2, :], gt.t[:, 3, :], ALU.add, [gt.b], [gt.b])
                P.op("dve", lambda e: e.max(out=m8.t[:], in_=gt.t[:, 2, :]), [gt.b], [m8.b])
                _ts(cx, "dve", gt.t[:, 4, :], gt.t[:, 2, :], m8.t[:, 2:3], None, ALU.is_ge, None, [gt.b, m8.b], [gt.b])
                _tt(cx, "dve", gt.t[:, 4, :], gt.t[:, 4, :], gt.t[:, 0, :], ALU.mult, [gt.b], [gt.b])
                _tt(cx, "dve", gt.t[:, 4, :], gt.t[:, 4, :], gt.t[:, 1, :], ALU.add, [gt.b], [gt.b])
                _ts(cx, "dve", gt.t[:, 5, :], gt.t[:, 4, :], -1.0, BIGB, ALU.add, ALU.mult, [gt.b], [gt.b])
                ps2 = env["pss"].next()
                _mm(cx, ps2.t[0:32, 0:128], gt.t[:, 5, :], identf.t[:], True, True, [gt.b, identf.b], [ps2.b])
                _cp(cx, "act", biasT.t[:, i * 128:(i + 1) * 128], ps2.t[0:32, 0:128], [ps2.b], [biasT.b])
            attn_pass(cx, env, KT, slice(0, 128), QT, slice(0, 128), Vt, 128 ** -0.5, all_kt, "moba", O, R, biasT=biasT,
                      selrows=selrows)
            finalize(mixedT.t[:, h, :], mixedT.b)
        for h in range(4):
            KT, Vt, QT = load_head(KCH["dk"] + h, None, QCH["dq"] + h, 6 + h)
            attn_pass(cx, env, KT, slice(0, 64), QT, slice(0, 64), Vt, 64 ** -0.5, all_kt, "causal", O, R)
            finalize(None, None, to_f32_tb=o1)
            attn_pass(cx, env, KT, slice(64, 128), QT, slice(64, 128), Vt, 64 ** -0.5, all_kt, "causal", O, R)
            finalize(None, None, to_f32_tb=o2)
            _stt(cx, o1.t[:], o2.t[:], lam.t[:, 3:4], o1.t[:], ALU.mult, ALU.add, [o2.b, lam.b, o1.b], [o1.b])
            _act(cx, sqd.t[:], o1.t[:], AF.Square, [o1.b], [sqd.b])
            for bk in range(2):
                sl_ = slice(bk * 512, (bk + 1) * 512)
                ps = env["pss"].next()
                _mm(cx, ps.t[:], cm.t[:, 2, :], sqd.t[:, sl_], True, True, [cm.b, sqd.b], [ps.b])
                _act(cx, o2.t[:, sl_], ps.t[:], AF.Sqrt, [ps.b, epsb.b], [o2.b], bias=epsb.t[:, 0:1])
            _recip(cx, o2.t[:], o2.t[:], [o2.b], [o2.b])
            _stt(cx, mixedT.t[:, 6 + h, :], o1.t[:], slg.t[:, 0:1], o2.t[:], ALU.mult, ALU.mult, [o1.b, slg.b, o2.b], [mixedT.b])

    outer.__exit__(None, None, None)
    if dbg:
        dM = cx.dram("dbgM", [128, DC, TL], BF16, out=True)
        cx.store(dM, mixedT.t[:], mixedT.b)

    with Scope(cx):
        pss = cx.ring(4, [128, 512], F32, "pss", psum=True)
        env = dict(pss=pss, ptr=cx.ring(2, [128, TL], BF16, "pt"), ones=ones, Mj=Mj)
        O = [cx.ps([128, 512], F32, "O0"), cx.ps([128, 512], F32, "O1")]
        R = [cx.ps([128, 512], F32, "R0"), cx.ps([128, 512], F32, "R1")]
        x1T = cx.sb([128, DC, TL], F32, "x1T")
        h2T = cx.sb([128, DC, TL], BF16, "h2T")
        wst = cx.ring(1, [128, DC, 128], F32, "wst")
        wbf = cx.ring(2, [128, DC, 128], BF16, "wbf")
        xr = cx.ring(2, [128, TL], F32, "xr")
        sq_r = cx.ring(1, [128, TL], F32, "sq")
        nld = 0
        for dmc in range(DC):
            ws = wst.next()
            cx.load(ws.t[:], wOd[dmc], ws.b, eng="sp" if nld % 2 == 0 else "pool")
            wb = wbf.next()
            _cp(cx, "pool" if nld % 2 == 0 else "dve", wb.t[:], ws.t[:], [ws.b], [wb.b])
            nld += 1
            x_ = xr.next()
            cx.load(x_.t[:], xT[dmc], x_.b)
            for hf in range(2):
                sl_ = slice(hf * 512, (hf + 1) * 512)
                ps = pss.next()
                for hc in range(DC):
                    _mm(cx, ps.t[:], wb.t[:, hc, :], mixedT.t[:, hc, sl_], hc == 0, hc == DC - 1, [wb.b, mixedT.b], [ps.b])
                _tt(cx, "dve", x1T.t[:, dmc, sl_], ps.t[:], x_.t[:, sl_], ALU.add, [ps.b, x_.b], [x1T.b])
            sq = sq_r.next()
            _act(cx, sq.t[:], x1T.t[:, dmc, :], AF.Square, [x1T.b], [sq.b])
            for hf in range(2):
                _mm(cx, R[hf].t[:], cm.t[:, 2, :], sq.t[:, hf * 512:(hf + 1) * 512], dmc == 0, dmc == DC - 1, [cm.b, sq.b],
                    [R[hf].b])
        rstd = cx.sb([128, TL], F32, "rstd")
        for hf in range(2):
            _act(cx, rstd.t[:, hf * 512:(hf + 1) * 512], R[hf].t[:], AF.Sqrt, [R[hf].b, epsb.b], [rstd.b], scale=1.0 / DC,
                 bias=epsb.t[:, 0:1])
        _recip(cx, rstd.t[:], rstd.t[:], [rstd.b], [rstd.b])
        for dc in range(DC):
            _stt(cx, h2T.t[:, dc, :], x1T.t[:, dc, :], gc.t[:, dc:dc + 1], rstd.t[:], ALU.mult, ALU.mult,
                 [x1T.b, gc.b, rstd.b], [h2T.b])
        memr = cx.ring(2, [128, 256], F32, "memr")
        msqr = cx.ring(2, [128, 256], F32, "msqr")
        psm = pss.next()
        for dc in range(DC):
            mf = memr.next()
            cx.load(mf.t[:], memd[dc], mf.b)
            mq_ = msqr.next()
            _act(cx, mq_.t[:], mf.t[:], AF.Square, [mf.b], [mq_.b])
            _mm(cx, psm.t[:, 0:256], cm.t[:, 2, :], mq_.t[:], dc == 0, dc == DC - 1, [cm.b, mq_.b], [psm.b])
        rm = cx.sb([128, 256], F32, "rm")
        _act(cx, rm.t[:], psm.t[:, 0:256], AF.Sqrt, [psm.b, epsb.b], [rm.b], scale=1.0 / DC, bias=epsb.t[:, 0:1])
        _recip(cx, rm.t[:], rm.t[:], [rm.b], [rm.b])
        mT = cx.sb([128, DC, 256], BF16, "mT")
        for dc in range(DC):
            mf = memr.next()
            cx.load(mf.t[:], memd[dc], mf.b)
            _stt(cx, mT.t[:, dc, :], mf.t[:], gc.t[:, DC + dc:DC + dc + 1], rm.t[:], ALU.mult, ALU.mult,
                 [mf.b, gc.b, rm.b], [mT.b])
        ckT = cx.sb([128, 4, 256], BF16, "ckT")
        cv = cx.sb([128, 4, 2, 128], BF16, "cv")
        cqT = TB(mixedT.t[:, 0:4, :], "cqT")
        coT = TB(mixedT.t[:, 4:8, :], "coT")
        for _tb in (cqT, coT):
            _tb.b.rs = list(mixedT.b.rs) + ([mixedT.b.w] if mixedT.b.w is not None else [])
        tmpf = cx.ring(1, [128, 512], F32, "tmpf")
        tmps = cx.ring(1, [128, 512], F32, "tmps")
        tmpr = cx.ring(1, [128, 512], F32, "tmpr")

        def headnorm(ps, w, gcol, dst_ap, dst_b):
            xf = tmpf.next()
            _cp(cx, "act", xf.t[:, 0:w], ps.t[:, 0:w], [ps.b], [xf.b])
            s2 = tmps.next()
            _act(cx, s2.t[:, 0:w], xf.t[:, 0:w], AF.Square, [xf.b], [s2.b])
            pm = pss.next()
            _mm(cx, pm.t[:, 0:w], cm.t[:, 2, :], s2.t[:, 0:w], True, True, [cm.b, s2.b], [pm.b])
            rr = tmpr.next()
            _act(cx, rr.t[:, 0:w], pm.t[:, 0:w], AF.Sqrt, [pm.b, epsb.b], [rr.b], bias=epsb.t[:, 0:1])
            _recip(cx, rr.t[:, 0:w], rr.t[:, 0:w], [rr.b], [rr.b])
            _stt(cx, dst_ap, xf.t[:, 0:w], gc.t[:, gcol:gcol + 1], rr.t[:, 0:w], ALU.mult, ALU.mult, [xf.b, gc.b, rr.b], [dst_b])

        for h in range(4):
            ws = wst.next()
            cx.load(ws.t[:], wkd[h], ws.b)
            wb = wbf.next()
            _cp(cx, "pool", wb.t[:], ws.t[:], [ws.b], [wb.b])
            ps = pss.next()
            for dc in range(DC):
                _mm(cx, ps.t[:, 0:256], wb.t[:, dc, :], mT.t[:, dc, :], dc == 0, dc == DC - 1, [wb.b, mT.b], [ps.b])
            headnorm(ps, 256, 2 * DC + 1, ckT.t[:, h, :], ckT.b)
            ws = wst.next()
            cx.load(ws.t[:], wvd[h], ws.b)
            wb = wbf.next()
            _cp(cx, "dve", wb.t[:], ws.t[:], [ws.b], [wb.b])
            for mt in range(2):
                ps = pss.next()
                for dc in range(DC):
                    _mm(cx, ps.t[:, 0:128], mT.t[:, dc, mt * 128:(mt + 1) * 128], wb.t[:, dc, :], dc == 0, dc == DC - 1,
                        [wb.b, mT.b], [ps.b])
                _cp(cx, "act", cv.t[:, h, mt, :], ps.t[:, 0:128], [ps.b], [cv.b])
            ws = wst.next()
            cx.load(ws.t[:], wqd[h], ws.b)
            wb = wbf.next()
            _cp(cx, "pool", wb.t[:], ws.t[:], [ws.b], [wb.b])
            for hf in range(2):
                sl_ = slice(hf * 512, (hf + 1) * 512)
                ps = pss.next()
                for dc in range(DC):
                    _mm(cx, ps.t[:], wb.t[:, dc, :], h2T.t[:, dc, sl_], dc == 0, dc == DC - 1, [wb.b, h2T.b], [ps.b])
                headnorm(ps, 512, 2 * DC, cqT.t[:, h, sl_], cqT.b)
        for h in range(4):
            KTv = TB(ckT.t[:, h, :])
            KTv.b = ckT.b
            QTv = TB(cqT.t[:, h, :])
            QTv.b = cqT.b
            Vv = TB(cv.t[:, h, :, :])
            Vv.b = cv.b
            attn_pass(cx, env, KTv, slice(0, 128), QTv, slice(0, 128), Vv, 128 ** -0.5, [0, 1], "none", O, R, suffix=False)
            ri = tmpf.next()
            ri2 = tmps.next()
            for bk, rt in ((0, ri), (1, ri2)):
                sl_ = slice(bk * 512, (bk + 1) * 512)
                _recip(cx, rt.t[:], R[bk].t[:], [R[bk].b], [rt.b])
                _tt(cx, "dve", coT.t[:, h, sl_], O[bk].t[:], rt.t[:], ALU.mult, [O[bk].b, rt.b], [coT.b])
        wcs = cx.ring(2, [128, 4, 128], F32, "wcs")
        wcb = cx.ring(2, [128, 4, 128], BF16, "wcb")
        xo_r = xr
        for dmc in range(DC):
            ws = wcs.next()
            cx.load(ws.t[:], wcod[dmc], ws.b)
            wb = wcb.next()
            _cp(cx, "pool", wb.t[:], ws.t[:], [ws.b], [wb.b])
            xo = xo_r.next()
            for hf in range(2):
                sl_ = slice(hf * 512, (hf + 1) * 512)
                ps = pss.next()
                for hc in range(4):
                    _mm(cx, ps.t[:], wb.t[:, hc, :], coT.t[:, hc, sl_], hc == 0, hc == 3, [wb.b, coT.b], [ps.b])
                _tt(cx, "dve", xo.t[:, sl_], ps.t[:], x1T.t[:, dmc, sl_], ALU.add, [ps.b, x1T.b], [xo.b])
            cx.store(x2T[dmc], xo.t[:], xo.b)
    return cx.finish()


def prep_B_weights(inp, l):
    import math
    wO = np.stack([wtile(inp["w_out"][l][:, 128 * k:128 * (k + 1)]) for k in range(DC)])
    gc = np.zeros((128, 2 * DC + 2), np.float32)
    gc[:, 0:DC] = inp["cross_norm"][l].reshape(DC, 128).T
    gc[:, DC:2 * DC] = inp["mem_norm"][l].reshape(DC, 128).T
    gc[:, 2 * DC] = inp["cross_qk_gain"][l, 0]
    gc[:, 2 * DC + 1] = inp["cross_qk_gain"][l, 1]
    wq = np.stack([wtile(inp["cross_wq"][l][:, 128 * h:128 * (h + 1)]) for h in range(4)])
    wk = np.stack([wtile(inp["cross_wkv"][l][:, 128 * h:128 * (h + 1)]) for h in range(4)])
    wv = np.stack([wtile(inp["cross_wkv"][l][:, 512 + 128 * h:512 + 128 * (h + 1)]) for h in range(4)])
    wo = inp["cross_wo"][l]
    wco = np.ascontiguousarray(wo.reshape(4, 128, DC, 128).transpose(2, 1, 0, 3))
    rc, cm = rope_consts()
    lam_init = 0.8 - 0.6 * math.exp(-0.3 * l)
    return dict(wO=wO, gc=gc, wq=wq, wk=wk, wv=wv, wco=wco, cm=cm, memT=fm(inp["mem"][0]),
                dl=np.ascontiguousarray(inp["diff_lambda"][l].reshape(1, 256)),
                sl=np.ascontiguousarray(inp["diff_subln"][l].reshape(128, 1))), lam_init


def glue_A_to_B(resA):
    kT = np.zeros((NKC, 128, 8192), ml_dtypes.bfloat16)
    vfull = np.zeros((8192, 2048), ml_dtypes.bfloat16)
    qTs = []
    ksrc = ([FIDX["mk"] + k for k in range(6)] + [FIDX["dk"] + k for k in range(4)] + [FIDX["sk"] + k for k in range(6)]
            + [FIDX["ik"]])
    qsrc = ([FIDX["mq"] + k for k in range(6)] + [FIDX["dq"] + k for k in range(4)] + [FIDX["sq"] + k for k in range(6)]
            + [FIDX["iq"] + k for k in range(4)])
    for c in range(NCORES):
        ti = tok_index(c)
        fT = np.asarray(resA[c]["fT"])
        kT[:, :, ti] = fT[ksrc]
        vfull[ti] = np.asarray(resA[c]["vtok"])
        qTs.append(np.ascontiguousarray(fT[qsrc]))
    vG = np.ascontiguousarray(vfull.reshape(NKT, 128, 16, 128).transpose(2, 1, 0, 3))
    return qTs, kT, vG


_PROGS = {}


def _prog(name):
    if name not in _PROGS:
        _PROGS[name] = {"A": build_A, "B": build_B, "C": build_C}[name]()
    return _PROGS[name]


def kernel(**inputs):
    import math
    inp = {k: np.asarray(v) for k, v in inputs.items()}
    cores = list(range(NCORES))
    toks = [tok_index(c) for c in cores]
    x0 = inp["x"][0]
    xT_loc = [fm(x0[toks[c]]) for c in cores]
    pos_loc = [np.ascontiguousarray(inp["positions"][0][toks[c]].reshape(1, TL)).astype(np.int32) for c in cores]
    for l in range(2):
        wa = prep_A_weights(inp, l)
        maps = []
        for c in cores:
            m = dict(wa)
            m["xT"] = xT_loc[c]
            m["pos"] = pos_loc[c]
            maps.append(m)
        resA = run_bass_kernel_spmd(_prog("A"), maps, core_ids=cores).results
        del maps, wa
        qTs, kT, vG = glue_A_to_B(resA)
        wb, lam_init = prep_B_weights(inp, l)
        maps = []
        for c in cores:
            m = dict(wb)
            m["xT"] = xT_loc[c]
            m["qT"] = qTs[c]
            m["kT"] = kT
            m["vG"] = vG
            m["iw"] = np.asarray(resA[c]["iwo"])
            pc = np.zeros((128, 4), np.float32)
            pc[:, 0] = c
            pc[:, 1] = c // 2
            pc[:, 2] = lam_init
            pc[:, 3] = 1.0 - lam_init
            m["pc"] = pc
            maps.append(m)
        resB = run_bass_kernel_spmd(_prog("B"), maps, core_ids=cores).results
        del maps, wb, qTs, kT, vG, resA
        xfull = np.zeros((D_MODEL, 8192), np.float32)
        for c in cores:
            xfull[:, toks[c]] = np.asarray(resB[c]["x2T"]).reshape(D_MODEL, TL)
        wc = prep_C_weights(inp, l)
        maps = []
        for c in cores:
            m = dict(wc)
            m["x2h"] = halo_cols(xfull, c)
            maps.append(m)
        resC = run_bass_kernel_spmd(_prog("C"), maps, core_ids=cores).results
        del maps, wc, resB
        xT_loc = [np.ascontiguousarray(np.asarray(resC[c]["x3T"])) for c in cores]
    out = np.zeros((1, 8192, D_MODEL), np.float32)
    for c in cores:
        out[0, toks[c]] = xT_loc[c].reshape(D_MODEL, TL).T
    return out
```

```python
import numpy as np
import ml_dtypes
from contextlib import ExitStack
import concourse.bass as bass
import concourse.mybir as mybir
from concourse.bass_utils import run_bass_kernel_spmd

F32 = mybir.dt.float32
BF16 = mybir.dt.bfloat16
I32 = mybir.dt.int32
ALU = mybir.AluOpType
AF = mybir.ActivationFunctionType
AX = mybir.AxisListType

NCORES = 8
ENGS = ("pe", "act", "dve", "pool", "sp")


class Buf:
    __slots__ = ("name", "w", "rs")

    def __init__(self, name=""):
        self.name = name
        self.w = None
        self.rs = []


class Op:
    __slots__ = ("eng", "fn", "deps", "sig", "sigidx", "dma", "dn")

    def __init__(self, eng, fn, dma):
        self.eng = eng
        self.fn = fn
        self.deps = []
        self.sig = False
        self.sigidx = -1
        self.dma = dma
        self.dn = -1


class Prog:
    NDS = 40
    CAP = 30000

    def __init__(self, nc):
        self.nc = nc
        self.ops = []
        self.ndma = 0

    def op(self, eng, fn, reads=(), writes=(), dma=False):
        o = Op(eng, fn, dma)
        deps = {}
        for b in reads:
            if b.w is not None:
                deps[id(b.w)] = b.w
        for b in writes:
            if b.w is not None:
                deps[id(b.w)] = b.w
            for r in b.rs:
                deps[id(r)] = r
        deps.pop(id(o), None)
        o.deps = list(deps.values())
        for b in writes:
            b.w = o
            b.rs = []
        for b in reads:
            if not b.rs or b.rs[-1] is not o:
                b.rs.append(o)
        if dma:
            o.dn = self.ndma
            self.ndma += 1
        self.ops.append(o)
        return o

    def dma(self, out, in_, reads=(), writes=(), eng="sp", **kw):
        return self.op(eng, lambda e: e.dma_start(out=out, in_=in_, **kw), reads, writes, dma=True)

    def emit(self, final_bufs=()):
        nc = self.nc
        ops = self.ops
        fin = Op("sp", None, False)
        fd = {}
        for b in final_bufs:
            if b.w is not None:
                fd[id(b.w)] = b.w
        fin.deps = list(fd.values())
        ops = ops + [fin]
        for o in ops:
            for p in o.deps:
                if not p.dma:
                    if p.eng == "pe" and o.eng == "pe" and not o.dma:
                        continue
                    p.sig = True
        cnt = {e: 0 for e in ENGS}
        for o in ops:
            if o.sig:
                o.sigidx = cnt[o.eng]
                cnt[o.eng] += 1
        with ExitStack() as st:
            esems = {}
            for e in ENGS:
                n = (cnt[e] + self.CAP - 1) // self.CAP
                esems[e] = [st.enter_context(nc.semaphore(f"s_{e}{k}")) for k in range(n)]
            nds = min(self.NDS, max(1, self.ndma))
            dsems = [st.enter_context(nc.semaphore(f"s_dma{k}")) for k in range(nds)]
            block = st.enter_context(nc.Block())
            per = {e: [o for o in ops if o.eng == e] for e in ENGS}

            def run(eng_name, eobj):
                waited_e = {e: -1 for e in ENGS}
                waited_d = {}
                for o in per[eng_name]:
                    need_e = {}
                    need_d = {}
                    for p in o.deps:
                        if p.dma:
                            k = p.dn % nds
                            v = 16 * (p.dn // nds + 1)
                            if waited_d.get(k, 0) < v:
                                need_d[k] = max(need_d.get(k, 0), v)
                        else:
                            if p.eng == "pe" and eng_name == "pe" and not o.dma:
                                continue
                            if waited_e[p.eng] < p.sigidx:
                                need_e[p.eng] = max(need_e.get(p.eng, -1), p.sigidx)
                    if o.dma:
                        k = o.dn % nds
                        v = 16 * (o.dn // nds)
                        if v > 0 and waited_d.get(k, 0) < v:
                            need_d[k] = max(need_d.get(k, 0), v)
                    for pe_, si in need_e.items():
                        eobj.wait_ge(esems[pe_][si // self.CAP], si % self.CAP + 1)
                        waited_e[pe_] = si
                    for k, v in need_d.items():
                        eobj.wait_ge(dsems[k], v)
                        waited_d[k] = v
                    if o.fn is None:
                        continue
                    ins = o.fn(eobj)
                    if o.dma:
                        ins.then_inc(dsems[o.dn % nds], 16)
                    elif o.sig:
                        ins.then_inc(esems[eng_name][o.sigidx // self.CAP], 1)

            @block.tensor
            def _(e):
                run("pe", e)

            @block.scalar
            def _(e):
                run("act", e)

            @block.vector
            def _(e):
                run("dve", e)

            @block.gpsimd
            def _(e):
                run("pool", e)

            @block.sync
            def _(e):
                run("sp", e)


class TB:
    __slots__ = ("t", "b")

    def __init__(self, t, name=""):
        self.t = t
        self.b = Buf(name)


class Ctx:
    def __init__(self):
        self.nc = bass.Bass("TRN2", target_bir_lowering=False)
        self.P = Prog(self.nc)
        self.st = ExitStack()
        self.finals = []
        self.n = 0

    def dram(self, name, shape, dt, out=False):
        return self.nc.dram_tensor(name, list(shape), dt, kind="ExternalOutput" if out else "ExternalInput").ap()

    def sb(self, shape, dt, name=None):
        self.n += 1
        name = f"sb_{name or 't'}_{self.n}"
        return TB(self.st.enter_context(self.nc.sbuf_tensor(name, list(shape), dt)), name)

    def ps(self, shape, dt=F32, name=None):
        self.n += 1
        name = f"ps_{name or 'p'}_{self.n}"
        return TB(self.st.enter_context(self.nc.psum_tensor(name, list(shape), dt)), name)

    def ring(self, n, shape, dt, name, psum=False):
        return Ring([(self.ps if psum else self.sb)(shape, dt, f"{name}{k}") for k in range(n)])

    def load(self, dst_ap, src_ap, dst_b, eng="sp"):
        return self.P.dma(dst_ap, src_ap, writes=[dst_b], eng=eng)

    def store(self, dst_ap, src_ap, src_b, eng="sp"):
        fb = Buf("out")
        self.finals.append(fb)
        return self.P.dma(dst_ap, src_ap, reads=[src_b], writes=[fb], eng=eng)

    def finish(self):
        self.P.emit(self.finals)
        self.st.close()
        return self.nc


class Ring:
    def __init__(self, items):
        self.items = items
        self.k = 0

    def next(self):
        it = self.items[self.k % len(self.items)]
        self.k += 1
        return it


D_MODEL = 2048
DC = 16
TL = 1024
D_IN = 6728
D_FF = 5504
FC = 43
EPS = 1e-6
PI = float(np.pi)

OFF = {}
_o = 0
for _n, _w in (("mq", 768), ("mk", 768), ("mv", 768), ("dq", 512), ("dk", 512), ("dv", 512),
               ("sq", 768), ("sk", 768), ("sv", 768), ("iq", 512), ("ik", 64), ("iw", 8)):
    OFF[_n] = _o
    _o += _w
FCH = []
for _n, _nch, _kind in (("mq", 6, 0), ("mk", 6, 0), ("dq", 4, 1), ("dk", 4, 1), ("sq", 6, 0), ("sk", 6, 0),
                        ("iq", 4, 2), ("ik", 1, 2)):
    for _j in range(_nch):
        FCH.append((_n, OFF[_n] + 128 * _j, _kind))
NF = len(FCH)
FIDX = {}
for _i, (_n, _c, _k) in enumerate(FCH):
    FIDX.setdefault(_n, _i)


def emit_rope_tables(cx, pos_ap, rc):
    P = cx.P
    pi_ = cx.sb([128, TL], I32, "posi")
    cx.load(pi_.t[:], pos_ap.partition_broadcast(128), pi_.b)
    pf = cx.sb([128, TL], F32, "posf")
    P.op("dve", lambda e: e.tensor_copy(out=pf.t[:], in_=pi_.t[:]), [pi_.b], [pf.b])
    tabs = {}
    a = cx.sb([128, TL], F32, "ra")
    ki = cx.sb([128, TL], I32, "rki")
    kf = cx.sb([128, TL], F32, "rkf")
    for hd, col in ((128, 0), (64, 1)):
        for nm, shift in (("cos", PI / 2), ("sin", 0.0)):
            out = cx.sb([128, TL], F32, f"{nm}{hd}")
            P.op("dve", lambda e, col=col: e.tensor_scalar(out=a.t[:], in0=pf.t[:], scalar1=rc.t[:, col:col + 1],
                                                           scalar2=None, op0=ALU.mult), [pf.b, rc.b], [a.b])
            P.op("dve", lambda e, shift=shift: e.tensor_scalar(out=a.t[:], in0=a.t[:], scalar1=shift - PI,
                                                               scalar2=None, op0=ALU.add), [a.b], [a.b])
            P.op("dve", lambda e: e.tensor_scalar(out=ki.t[:], in0=a.t[:], scalar1=1.0 / (2 * PI), scalar2=None,
                                                  op0=ALU.mult), [a.b], [ki.b])
            P.op("dve", lambda e: e.tensor_copy(out=kf.t[:], in_=ki.t[:]), [ki.b], [kf.b])
            P.op("dve", lambda e: e.scalar_tensor_tensor(out=a.t[:], in0=kf.t[:], scalar=-2 * PI, in1=a.t[:],
                                                         op0=ALU.mult, op1=ALU.add), [kf.b, a.b], [a.b])
            P.op("dve", lambda e: e.tensor_scalar(out=a.t[:], in0=a.t[:], scalar1=PI, scalar2=-PI, op0=ALU.min,
                                                  op1=ALU.max), [a.b], [a.b])
            P.op("act", lambda e, out=out: e.activation(out=out.t[:], in_=a.t[:], func=AF.Sin, scale=-1.0),
                 [a.b], [out.b])
            if nm == "sin":
                P.op("dve", lambda e, out=out, col=col: e.tensor_scalar(out=out.t[:], in0=out.t[:],
                                                                        scalar1=rc.t[:, 2 + col:3 + col], scalar2=None,
                                                                        op0=ALU.mult), [out.b, rc.b], [out.b])
            tabs[(nm, hd)] = out
    return tabs


def build_A():
    cx = Ctx()
    P = cx.P
    xT = cx.dram("xT", [DC, 128, TL], F32)
    pos = cx.dram("pos", [1, TL], I32)
    gnorm = cx.dram("gnorm", [128, DC], F32)
    wF = cx.dram("wF", [NF, 128, DC, 128], F32)
    wV = cx.dram("wV", [8, 128, DC, 256], F32)
    wI = cx.dram("wI", [128, DC, 8], F32)
    gq = cx.dram("gq", [128, NF], F32)
    rcd = cx.dram("rc", [128, 4], F32)
    cmd = cx.dram("cm", [4, 128, 128], F32)
    fT = cx.dram("fT", [NF, 128, TL], BF16, out=True)
    vtok = cx.dram("vtok", [TL, 2048], BF16, out=True)
    iwo = cx.dram("iwo", [TL, 8], F32, out=True)

    gn = cx.sb([128, DC], F32, "gn")
    cx.load(gn.t[:], gnorm, gn.b)
    gqt = cx.sb([128, NF], F32, "gqt")
    cx.load(gqt.t[:], gq, gqt.b)
    rc = cx.sb([128, 4], F32, "rc")
    cx.load(rc.t[:], rcd, rc.b)
    cm = cx.sb([128, 4, 128], F32, "cm")
    for k in range(4):
        cx.load(cm.t[:, k, :], cmd[k], cm.b)
    epsb = cx.sb([128, 1], F32, "epsb")
    P.op("dve", lambda e: e.memset(epsb.t[:], EPS), [], [epsb.b])
    tabs = emit_rope_tables(cx, pos, rc)

    hT = cx.sb([128, DC, TL], BF16, "hT")
    xr = cx.ring(3, [128, TL], F32, "xr")
    sqr = cx.ring(2, [128, TL], F32, "sqr")
    psA = cx.ring(2, [128, 512], F32, "psA", psum=True)
    psM = cx.ring(2, [128, 512], F32, "psM", psum=True)
    psW = cx.ring(2, [128, 512], F32, "psW", psum=True)
    psV = cx.ring(2, [128, 512], F32, "psV", psum=True)
    m0, m1 = psM.next(), psM.next()
    for dc in range(DC):
        x_ = xr.next()
        cx.load(x_.t[:], xT[dc], x_.b)
        s_ = sqr.next()
        P.op("act", lambda e, x_=x_, s_=s_: e.activation(out=s_.t[:], in_=x_.t[:], func=AF.Square), [x_.b], [s_.b])
        for hf, m_ in ((0, m0), (1, m1)):
            P.op("pe", lambda e, m_=m_, s_=s_, hf=hf, dc=dc: e.matmul(m_.t[:], lhsT=cm.t[:, 2, :],
                                                                     rhs=s_.t[:, hf * 512:(hf + 1) * 512],
                                                                     start=(dc == 0), stop=(dc == DC - 1)),
                 [cm.b, s_.b], [m_.b])
    rstd = cx.sb([128, TL], F32, "rstd")
    for hf, m_ in ((0, m0), (1, m1)):
        sl = slice(hf * 512, (hf + 1) * 512)
        P.op("act", lambda e, m_=m_, sl=sl: e.activation(out=rstd.t[:, sl], in_=m_.t[:], func=AF.Sqrt, scale=1.0 / DC,
                                                         bias=epsb.t[:, 0:1]), [m_.b, epsb.b], [rstd.b])
    P.op("dve", lambda e: e.reciprocal(out=rstd.t[:], in_=rstd.t[:]), [rstd.b], [rstd.b])
    for dc in range(DC):
        x_ = xr.next()
        cx.load(x_.t[:], xT[dc], x_.b)
        P.op("dve", lambda e, x_=x_, dc=dc: e.scalar_tensor_tensor(
            out=hT.t[:, dc, :], in0=x_.t[:], scalar=gn.t[:, dc:dc + 1], in1=rstd.t[:], op0=ALU.mult, op1=ALU.mult),
            [x_.b, gn.b, rstd.b], [hT.b])

    wst = cx.ring(3, [128, DC, 128], F32, "wst")
    wbf = cx.ring(2, [128, DC, 128], BF16, "wbf")

    def issue_wF(jj):
        w_ = wst.next()
        cx.load(w_.t[:], wF[jj], w_.b, eng="sp")
        return w_

    pend_w = [issue_wF(jj) for jj in range(3)]
    xs_r = cx.ring(2, [128, 512], F32, "xs")
    sq_r = cx.ring(2, [128, 512], F32, "sq")
    rr_r = cx.ring(2, [128, 512], F32, "rr")
    y_r = cx.ring(2, [128, 512], F32, "y")
    t1_r = cx.ring(2, [128, 512], F32, "t1")
    t2_r = cx.ring(2, [128, 512], F32, "t2")
    ob_r = cx.ring(3, [128, 512], BF16, "ob")
    for j, (nm, c0, kind) in enumerate(FCH):
        ws = pend_w.pop(0)
        wb = wbf.next()
        P.op("dve" if j % 2 else "pool", lambda e, ws=ws, wb=wb: e.tensor_copy(out=wb.t[:], in_=ws.t[:]), [ws.b], [wb.b])
        if j + 3 < NF:
            pend_w.append(issue_wF(j + 3))
        hd = 128 if kind == 0 else 64
        perm_k = 0 if kind == 0 else 1
        mm_k = 2 if kind == 0 else 3
        cosT, sinT = tabs[("cos", hd)], tabs[("sin", hd)]
        for hf in range(2):
            sl = slice(hf * 512, (hf + 1) * 512)
            pa = psA.next()
            for dc in range(DC):
                P.op("pe", lambda e, pa=pa, wb=wb, dc=dc, sl=sl: e.matmul(pa.t[:], lhsT=wb.t[:, dc, :], rhs=hT.t[:, dc, sl],
                                                                         start=(dc == 0), stop=(dc == DC - 1)),
                     [wb.b, hT.b], [pa.b])
            xs = xs_r.next()
            P.op("act", lambda e, xs=xs, pa=pa: e.copy(out=xs.t[:], in_=pa.t[:]), [pa.b], [xs.b])
            if kind != 2:
                sq = sq_r.next()
                P.op("act", lambda e, sq=sq, xs=xs: e.activation(out=sq.t[:], in_=xs.t[:], func=AF.Square), [xs.b], [sq.b])
                pm = psM.next()
                P.op("pe", lambda e, pm=pm, sq=sq, mm_k=mm_k: e.matmul(pm.t[:], lhsT=cm.t[:, mm_k, :], rhs=sq.t[:],
                                                                      start=True, stop=True), [cm.b, sq.b], [pm.b])
                rr = rr_r.next()
                P.op("act", lambda e, rr=rr, pm=pm: e.activation(out=rr.t[:], in_=pm.t[:], func=AF.Sqrt,
                                                                 bias=epsb.t[:, 0:1]), [pm.b, epsb.b], [rr.b])
                P.op("dve", lambda e, rr=rr: e.reciprocal(out=rr.t[:], in_=rr.t[:]), [rr.b], [rr.b])
                y = y_r.next()
                P.op("dve", lambda e, y=y, xs=xs, rr=rr, j=j: e.scalar_tensor_tensor(
                    out=y.t[:], in0=xs.t[:], scalar=gqt.t[:, j:j + 1], in1=rr.t[:], op0=ALU.mult, op1=ALU.mult),
                    [xs.b, gqt.b, rr.b], [y.b])
            else:
                y = xs
            pw = psW.next()
            P.op("pe", lambda e, pw=pw, y=y, perm_k=perm_k: e.matmul(pw.t[:], lhsT=cm.t[:, perm_k, :], rhs=y.t[:],
                                                                    start=True, stop=True), [cm.b, y.b], [pw.b])
            t1 = t1_r.next()
            P.op("pool", lambda e, t1=t1, y=y, sl=sl, cosT=cosT: e.tensor_tensor(out=t1.t[:], in0=y.t[:], in1=cosT.t[:, sl],
                                                                               op=ALU.mult), [y.b, cosT.b], [t1.b])
            t2 = t2_r.next()
            P.op("dve", lambda e, t2=t2, pw=pw, sl=sl, sinT=sinT: e.tensor_tensor(out=t2.t[:], in0=pw.t[:], in1=sinT.t[:, sl],
                                                                                 op=ALU.mult), [pw.b, sinT.b], [t2.b])
            ob = ob_r.next()
            P.op("dve", lambda e, ob=ob, t1=t1, t2=t2: e.tensor_tensor(out=ob.t[:], in0=t1.t[:], in1=t2.t[:], op=ALU.add),
                 [t1.b, t2.b], [ob.b])
            cx.store(fT[j, :, sl], ob.t[:], ob.b)

    vst_r = cx.ring(2, [128, DC, 256], F32, "vst")
    vbf = cx.ring(2, [128, DC, 256], BF16, "vbf")
    vo_r = cx.ring(3, [128, 256], BF16, "vo")

    def issue_wV(cc):
        v_ = vst_r.next()
        cx.load(v_.t[:], wV[cc], v_.b, eng="sp")
        return v_

    pend_v = [issue_wV(0), issue_wV(1)]
    for cb in range(8):
        vst = pend_v.pop(0)
        vb = vbf.next()
        P.op("dve" if cb % 2 else "pool", lambda e, vb=vb, vst=vst: e.tensor_copy(out=vb.t[:], in_=vst.t[:]), [vst.b], [vb.b])
        if cb + 2 < 8:
            pend_v.append(issue_wV(cb + 2))
        for tt in range(8):
            pv = psV.next()
            for dc in range(DC):
                P.op("pe", lambda e, pv=pv, vb=vb, dc=dc, tt=tt: e.matmul(pv.t[:, 0:256], lhsT=hT.t[:, dc, tt * 128:(tt + 1) * 128],
                                                                         rhs=vb.t[:, dc, :], start=(dc == 0), stop=(dc == DC - 1)),
                     [hT.b, vb.b], [pv.b])
            vo = vo_r.next()
            P.op("act", lambda e, vo=vo, pv=pv: e.copy(out=vo.t[:], in_=pv.t[:, 0:256]), [pv.b], [vo.b])
            cx.store(vtok[tt * 128:(tt + 1) * 128, cb * 256:(cb + 1) * 256], vo.t[:], vo.b)
    wis = cx.sb([128, DC, 8], F32, "wis")
    cx.load(wis.t[:], wI, wis.b)
    wib = cx.sb([128, DC, 8], BF16, "wib")
    P.op("dve", lambda e: e.tensor_copy(out=wib.t[:], in_=wis.t[:]), [wis.b], [wib.b])
    io_r = cx.ring(2, [128, 8], F32, "io")
    for tt in range(8):
        pv = psV.next()
        for dc in range(DC):
            P.op("pe", lambda e, pv=pv, dc=dc, tt=tt: e.matmul(pv.t[:, 0:8], lhsT=hT.t[:, dc, tt * 128:(tt + 1) * 128],
                                                              rhs=wib.t[:, dc, :], start=(dc == 0), stop=(dc == DC - 1)),
                 [hT.b, wib.b], [pv.b])
        io = io_r.next()
        P.op("act", lambda e, io=io, pv=pv: e.activation(out=io.t[:], in_=pv.t[:, 0:8], func=AF.Copy, scale=8 ** -0.5),
             [pv.b], [io.b])
        cx.store(iwo[tt * 128:(tt + 1) * 128, :], io.t[:], io.b)
    return cx.finish()


def tok_index(c):
    i = np.arange(8)[:, None]
    r = np.arange(128)[None, :]
    return ((8 * i + c) * 128 + r).reshape(-1)


def fm(a):
    T, F = a.shape
    return np.ascontiguousarray(a.T.reshape(F // 128, 128, T))


def wtile(w):
    return np.ascontiguousarray(w.reshape(DC, 128, w.shape[1]).transpose(1, 0, 2))


def rope_consts():
    rc = np.zeros((128, 4), np.float32)
    p = np.arange(128)
    inv128 = np.float32(10000.0) ** (-(np.arange(64, dtype=np.float32) * np.float32(2.0) / np.float32(128)))
    inv64 = np.float32(10000.0) ** (-(np.arange(32, dtype=np.float32) * np.float32(2.0) / np.float32(64)))
    rc[:, 0] = inv128[p % 64]
    rc[:, 1] = inv64[p % 32]
    rc[:, 2] = np.where(p < 64, -1.0, 1.0)
    rc[:, 3] = np.where((p % 64) < 32, -1.0, 1.0)
    cm = np.zeros((4, 128, 128), np.float32)
    m = np.arange(128)
    cm[0, (m + 64) % 128, m] = 1.0
    cm[1, 64 * (m // 64) + ((m % 64) + 32) % 64, m] = 1.0
    cm[2] = 1.0 / 128
    cm[3, :64, :64] = 1.0 / 64
    cm[3, 64:, 64:] = 1.0 / 64
    return rc, cm


def prep_A_weights(inp, l):
    w_in = inp["w_in"][l]
    wF = np.zeros((NF, 128, DC, 128), np.float32)
    gq = np.ones((128, NF), np.float32)
    gains = {"mq": inp["moba_qk_gain"][l, 0], "mk": inp["moba_qk_gain"][l, 1],
             "dq": np.tile(inp["diff_qk_gain"][l, 0], 2), "dk": np.tile(inp["diff_qk_gain"][l, 1], 2),
             "sq": inp["dsa_qk_gain"][l, 0], "sk": inp["dsa_qk_gain"][l, 1]}
    for j, (nm, c0, kind) in enumerate(FCH):
        ncol = 64 if nm == "ik" else 128
        wF[j, :, :, :ncol] = wtile(w_in[:, c0:c0 + ncol])
        if nm in gains:
            gq[:, j] = gains[nm]
    wv = np.concatenate([w_in[:, OFF["mv"]:OFF["mv"] + 768], w_in[:, OFF["dv"]:OFF["dv"] + 512],
                         w_in[:, OFF["sv"]:OFF["sv"] + 768]], axis=1)
    wV = np.stack([wtile(wv[:, 256 * k:256 * (k + 1)]) for k in range(8)])
    wI = wtile(w_in[:, OFF["iw"]:OFF["iw"] + 8])
    rc, cm = rope_consts()
    return dict(wF=wF, wV=wV, wI=wI, gq=gq, rc=rc, cm=cm,
                gnorm=np.ascontiguousarray(inp["attn_norm"][l].reshape(DC, 128).T))


TLH = 8 * 130
CGRP = ((0, 3), (3, 3), (6, 2))


def build_C(dbg=False):
    cx = Ctx()
    P = cx.P
    x2h = cx.dram("x2h", [DC, 128, TLH], F32)
    gnorm = cx.dram("gnorm", [128, DC], F32)
    wU = cx.dram("wU", [2 * FC, 128, DC, 128], F32)
    cwd = cx.dram("cw", [128, 2 * FC, 4], F32)
    wD = cx.dram("wD", [DC, 3, 128, 16, 128], F32)
    cmd = cx.dram("cm", [4, 128, 128], F32)
    x3T = cx.dram("x3T", [DC, 128, TL], F32, out=True)

    gn = cx.sb([128, DC], F32, "gn")
    cx.load(gn.t[:], gnorm, gn.b)
    cw = cx.sb([128, 2 * FC, 4], F32, "cw")
    cx.load(cw.t[:], cwd, cw.b)
    cm = cx.sb([128, 128], F32, "cm")
    cx.load(cm.t[:], cmd[2], cm.b)
    epsb = cx.sb([128, 1], F32, "epsb")
    P.op("dve", lambda e: e.memset(epsb.t[:], EPS), [], [epsb.b])

    psU = cx.ring(6, [128, 512], F32, "psU", psum=True)
    psD = cx.ring(2, [128, 512], F32, "psD", psum=True)
    hT = cx.sb([128, DC, TLH], BF16, "hT")
    actT = cx.sb([128, FC, TL], BF16, "actT")
    xr = cx.ring(3, [128, TLH], F32, "xr")
    sqr = cx.ring(2, [128, TLH], F32, "sqr")
    ms = [psU.next() for _ in range(3)]
    for dc in range(DC):
        x_ = xr.next()
        cx.load(x_.t[:], x2h[dc], x_.b)
        s_ = sqr.next()
        P.op("act", lambda e, x_=x_, s_=s_: e.activation(out=s_.t[:], in_=x_.t[:], func=AF.Square), [x_.b], [s_.b])
        for (t0, nt), m_ in zip(CGRP, ms):
            P.op("pe", lambda e, m_=m_, s_=s_, t0=t0, nt=nt, dc=dc: e.matmul(
                m_.t[:, 0:nt * 130], lhsT=cm.t[:], rhs=s_.t[:, t0 * 130:(t0 + nt) * 130], start=(dc == 0),
                stop=(dc == DC - 1)), [cm.b, s_.b], [m_.b])
    rstd = cx.sb([128, TLH], F32, "rstd")
    for (t0, nt), m_ in zip(CGRP, ms):
        P.op("act", lambda e, m_=m_, t0=t0, nt=nt: e.activation(out=rstd.t[:, t0 * 130:(t0 + nt) * 130], in_=m_.t[:, 0:nt * 130],
                                                               func=AF.Sqrt, scale=1.0 / DC, bias=epsb.t[:, 0:1]),
             [m_.b, epsb.b], [rstd.b])
    P.op("dve", lambda e: e.reciprocal(out=rstd.t[:], in_=rstd.t[:]), [rstd.b], [rstd.b])
    for dc in range(DC):
        x_ = xr.next()
        cx.load(x_.t[:], x2h[dc], x_.b)
        P.op("dve", lambda e, x_=x_, dc=dc: e.scalar_tensor_tensor(
            out=hT.t[:, dc, :], in0=x_.t[:], scalar=gn.t[:, dc:dc + 1], in1=rstd.t[:], op0=ALU.mult, op1=ALU.mult),
            [x_.b, gn.b, rstd.b], [hT.b])

    wst = cx.ring(3, [128, DC, 128], F32, "wst")
    wbf = cx.ring(2, [128, DC, 128], BF16, "wbf")
    wsrc = []
    for _fc in range(FC):
        wsrc.append(wU[_fc])
        wsrc.append(wU[_fc + FC])
    for _d in range(DC):
        for _g in range(3):
            wsrc.append(wD[_d, _g])
    wpos = [0]

    def issue_w():
        if wpos[0] >= len(wsrc):
            return None
        w_ = wst.next()
        cx.load(w_.t[:], wsrc[wpos[0]], w_.b, eng="sp")
        wpos[0] += 1
        return w_

    pend_w = [issue_w() for _ in range(3)]
    c0_r = cx.ring(2, [128, 3, 128], F32, "c0")
    c1_r = cx.ring(2, [128, 3, 128], F32, "c1")
    gv_r = cx.ring(2, [128, TL], F32, "gv")
    nload = 0
    for fc in range(FC):
        gsil = gv_r.next()
        for which in (0, 1):
            ch = fc + which * FC
            ws = pend_w.pop(0)
            wb = wbf.next()
            P.op("pool" if nload % 2 == 0 else "dve", lambda e, ws=ws, wb=wb: e.tensor_copy(out=wb.t[:], in_=ws.t[:]),
                 [ws.b], [wb.b])
            nload += 1
            pend_w.append(issue_w())
            for (t0, nt) in CGRP:
                pu = psU.next()
                for dc in range(DC):
                    P.op("pe", lambda e, pu=pu, wb=wb, dc=dc, t0=t0, nt=nt: e.matmul(
                        pu.t[:, 0:nt * 130], lhsT=wb.t[:, dc, :], rhs=hT.t[:, dc, t0 * 130:(t0 + nt) * 130],
                        start=(dc == 0), stop=(dc == DC - 1)), [wb.b, hT.b], [pu.b])
                uv = pu.t[:, 0:nt * 130].rearrange("p (t c) -> p t c", c=130)
                c0 = c0_r.next()
                P.op("act", lambda e, c0=c0, uv=uv, nt=nt, ch=ch: e.activation(
                    out=c0.t[:, 0:nt, :], in_=uv[:, :, 2:130], func=AF.Identity, scale=cw.t[:, ch, 2:3],
                    bias=cw.t[:, ch, 3:4]), [pu.b, cw.b], [c0.b])
                c1 = c1_r.next()
                P.op("dve", lambda e, c0=c0, c1=c1, uv=uv, nt=nt, ch=ch: e.scalar_tensor_tensor(
                    out=c1.t[:, 0:nt, :], in0=uv[:, :, 1:129], scalar=cw.t[:, ch, 1:2], in1=c0.t[:, 0:nt, :],
                    op0=ALU.mult, op1=ALU.add), [pu.b, cw.b, c0.b], [c1.b])
                osl = slice(t0 * 128, (t0 + nt) * 128)
                if which == 0:
                    gview = gsil.t[:, osl].rearrange("p (t c) -> p t c", c=128)
                    P.op("dve", lambda e, c1=c1, uv=uv, nt=nt, ch=ch, gview=gview: e.scalar_tensor_tensor(
                        out=gview, in0=uv[:, :, 0:128], scalar=cw.t[:, ch, 0:1], in1=c1.t[:, 0:nt, :],
                        op0=ALU.mult, op1=ALU.add), [pu.b, cw.b, c1.b], [gsil.b])
                    P.op("act", lambda e, osl=osl, gsil=gsil: e.activation(out=gsil.t[:, osl], in_=gsil.t[:, osl], func=AF.Silu),
                         [gsil.b], [gsil.b])
                else:
                    P.op("dve", lambda e, c1=c1, c0=c0, uv=uv, nt=nt, ch=ch: e.scalar_tensor_tensor(
                        out=c0.t[:, 0:nt, :], in0=uv[:, :, 0:128], scalar=cw.t[:, ch, 0:1], in1=c1.t[:, 0:nt, :],
                        op0=ALU.mult, op1=ALU.add), [pu.b, cw.b, c1.b], [c0.b])
                    aview = actT.t[:, fc, osl].rearrange("p (t c) -> p t c", c=128)
                    gview = gsil.t[:, osl].rearrange("p (t c) -> p t c", c=128)
                    P.op("pool", lambda e, c0=c0, nt=nt, aview=aview, gview=gview: e.tensor_tensor(
                        out=aview, in0=c0.t[:, 0:nt, :], in1=gview, op=ALU.mult), [c0.b, gsil.b], [actT.b])

    if dbg:
        dA = cx.dram('dbgA', [128, FC, TL], BF16, out=True)
        cx.store(dA, actT.t[:], actT.b)
        dH = cx.dram('dbgH', [128, DC, TLH], BF16, out=True)
        cx.store(dH, hT.t[:], hT.b)
    xo_r = cx.ring(2, [128, TL], F32, "xo")
    xres_r = cx.ring(2, [128, 8, 128], F32, "xres")
    for dcc in range(DC):
        xres = xres_r.next()
        cx.load(xres.t[:], x2h[dcc].rearrange("p (t c) -> p t c", c=130)[:, :, 2:130], xres.b)
        pd = [psD.next(), psD.next()]
        for gi in range(3):
            nk = 16 if gi < 2 else FC - 32
            ws = pend_w.pop(0)
            wb = wbf.next()
            P.op("pool" if nload % 2 == 0 else "dve", lambda e, ws=ws, wb=wb: e.tensor_copy(out=wb.t[:], in_=ws.t[:]),
                 [ws.b], [wb.b])
            nload += 1
            pend_w.append(issue_w())
            for k in range(nk):
                fc = gi * 16 + k
                for hf in range(2):
                    P.op("pe", lambda e, hf=hf, wb=wb, k=k, fc=fc, pd=pd: e.matmul(
                        pd[hf].t[:], lhsT=wb.t[:, k, :], rhs=actT.t[:, fc, hf * 512:(hf + 1) * 512],
                        start=(fc == 0), stop=(fc == FC - 1)), [wb.b, actT.b], [pd[hf].b])
        xo = xo_r.next()
        for hf in range(2):
            P.op("dve", lambda e, hf=hf, xo=xo, xres=xres, pd=pd: e.tensor_tensor(
                out=xo.t[:, hf * 512:(hf + 1) * 512], in0=pd[hf].t[:],
                in1=xres.t[:, hf * 4:(hf + 1) * 4, :].rearrange("p t c -> p (t c)"), op=ALU.add), [pd[hf].b, xres.b], [xo.b])
        cx.store(x3T[dcc], xo.t[:], xo.b)
    return cx.finish()


def prep_C_weights(inp, l):
    wup = inp["ffn_w_up"][l]
    wU = np.ascontiguousarray(wup.reshape(DC, 128, 2 * FC, 128).transpose(2, 1, 0, 3))
    cwv = np.concatenate([inp["ffn_conv_w"][l], inp["ffn_conv_b"][l][None]], 0)
    cw = np.ascontiguousarray(cwv.reshape(4, 2 * FC, 128).transpose(2, 1, 0))
    wd = inp["ffn_w_down"][l]
    wdp = np.zeros((48 * 128, D_MODEL), np.float32)
    wdp[:D_FF] = wd
    wD = np.ascontiguousarray(wdp.reshape(3, 16, 128, DC, 128).transpose(3, 0, 2, 1, 4))
    rc, cm = rope_consts()
    return dict(wU=wU, cw=cw, wD=wD, cm=cm, gnorm=np.ascontiguousarray(inp["ffn_norm"][l].reshape(DC, 128).T))


def halo_cols(xfull_T, c):
    out = np.zeros((D_MODEL, TLH), np.float32)
    for i in range(8):
        t0 = (8 * i + c) * 128
        lo = max(t0 - 2, 0)
        out[:, i * 130 + (2 - (t0 - lo)):i * 130 + 130] = xfull_T[:, lo:t0 + 128]
    return np.ascontiguousarray(out.reshape(DC, 128, TLH))


def _mm(cx, out, lhsT, rhs, start, stop, reads, writes, skip=False):
    if skip:
        return cx.P.op("pe", lambda e: e.matmul(out, lhsT=lhsT, rhs=rhs, start=start, stop=stop, skip_group_check=True),
                       reads, writes)
    return cx.P.op("pe", lambda e: e.matmul(out, lhsT=lhsT, rhs=rhs, start=start, stop=stop), reads, writes)


def _tr(cx, out, in_, ident, reads, writes):
    return cx.P.op("pe", lambda e: e.transpose(out=out, in_=in_, identity=ident), reads, writes)


def _act(cx, out, in_, func, reads, writes, **kw):
    return cx.P.op("act", lambda e: e.activation(out=out, in_=in_, func=func, **kw), reads, writes)


def _tt(cx, eng, out, in0, in1, op, reads, writes):
    return cx.P.op(eng, lambda e: e.tensor_tensor(out=out, in0=in0, in1=in1, op=op), reads, writes)


def _ts(cx, eng, out, in0, s1, s2, op0, op1, reads, writes, accum_out=None):
    if op1 is None:
        return cx.P.op(eng, lambda e: e.tensor_scalar(out=out, in0=in0, scalar1=s1, scalar2=None, op0=op0), reads, writes)
    if accum_out is not None:
        return cx.P.op(eng, lambda e: e.tensor_scalar(out=out, in0=in0, scalar1=s1, scalar2=s2, op0=op0, op1=op1,
                                                      accum_out=accum_out), reads, writes)
    return cx.P.op(eng, lambda e: e.tensor_scalar(out=out, in0=in0, scalar1=s1, scalar2=s2, op0=op0, op1=op1), reads, writes)


def _stt(cx, out, in0, scalar, in1, op0, op1, reads, writes):
    return cx.P.op("dve", lambda e: e.scalar_tensor_tensor(out=out, in0=in0, scalar=scalar, in1=in1, op0=op0, op1=op1),
                   reads, writes)


def _cp(cx, eng, out, in_, reads, writes):
    if eng == "act":
        return cx.P.op("act", lambda e: e.copy(out=out, in_=in_), reads, writes)
    return cx.P.op(eng, lambda e: e.tensor_copy(out=out, in_=in_), reads, writes)


def _recip(cx, out, in_, reads, writes):
    return cx.P.op("dve", lambda e: e.reciprocal(out=out, in_=in_), reads, writes)


def _memset(cx, eng, out, val, writes):
    return cx.P.op(eng, lambda e: e.memset(out, val), [], writes)


class Scope:
    def __init__(self, cx):
        self.cx = cx

    def __enter__(self):
        self.saved = self.cx.st
        self.cx.st = ExitStack()
        return self

    def __exit__(self, *a):
        cx = self.cx
        cx.st.close()
        cx.st = self.saved
        last = {}
        dmas = []
        for o in cx.P.ops:
            if o.dma:
                dmas.append(o)
            else:
                last[o.eng] = o
        cx.fence = list(last.values()) + dmas[-Prog.NDS:]
        return False


def _fenced_tb(cx, tb):
    f = getattr(cx, "fence", None)
    if f:
        tb.b.rs = list(f)
    return tb


_orig_sb = Ctx.sb
_orig_ps = Ctx.ps
Ctx.sb = lambda self, shape, dt, name=None: _fenced_tb(self, _orig_sb(self, shape, dt, name))
Ctx.ps = lambda self, shape, dt=F32, name=None: _fenced_tb(self, _orig_ps(self, shape, dt, name))


NKT = 64
OFFK = [0]
for _kt in range(NKT):
    OFFK.append(OFFK[-1] + 8 - _kt // 8)
NSLOT = OFFK[-1]
QCH = {"mq": 0, "dq": 6, "sq": 10, "iq": 16}
KCH = {"mk": 0, "dk": 6, "sk": 10, "ik": 16}
NQC = 20
NKC = 17
BIG = 1.0e30
BIGB = 30000.0
NBIS = 18


def attn_pass(cx, env, KT, krows, QT, qrows, Vt, scale, key_tiles, mode, O, R, maskT=None, biasT=None, selrows=None,
              suffix=True):
    pss, ptr = env["pss"], env["ptr"]
    ones, Mj = env["ones"], env["Mj"]
    nkt = len(key_tiles)

    def stage1(n, kt):
        imin = (kt // 8) if suffix else 0
        c0 = imin * 128
        groups = []
        a = c0
        while a < TL:
            b = min(TL, (a // 512 + 1) * 512)
            groups.append((a, b))
            a = b
        pt = ptr.next()
        for (a, b) in groups:
            ps = pss.next()
            w = b - a
            last = (mode != "moba")
            _mm(cx, ps.t[:, 0:w], KT.t[krows, kt * 128:(kt + 1) * 128], QT.t[qrows, a:b], True, last,
                [KT.b, QT.b], [ps.b])
            if mode == "moba":
                nb = kt // 2
                _mm(cx, ps.t[:, 0:w], selrows.t[0:32, nb:nb + 1].to_broadcast([32, 128]), biasT.t[:, a:b], False, True,
                    [selrows.b, biasT.b], [ps.b])
            _act(cx, pt.t[:, a:b], ps.t[:, 0:w], AF.Exp, [ps.b], [pt.b], scale=scale)
        if mode in ("causal", "moba"):
            j = kt % 8
            _tt(cx, "pool", pt.t[:, c0:c0 + 128], pt.t[:, c0:c0 + 128], Mj.t[:, j, :], ALU.mult, [pt.b, Mj.b], [pt.b])
        elif mode == "dsa":
            nq = TL - c0
            mv = maskT.t[:, OFFK[kt]:OFFK[kt] + nq // 128, :].rearrange("p s q -> p (s q)")
            _tt(cx, "dve", pt.t[:, c0:TL], pt.t[:, c0:TL], mv, ALU.mult, [pt.b, maskT.b], [pt.b])
        return pt, groups

    def stage2(n, pt, groups):
        for (a, b) in groups:
            bank = a // 512
            o0 = a - bank * 512
            w = b - a
            _mm(cx, O[bank].t[:, o0:o0 + w], Vt.t[:, n, :], pt.t[:, a:b], n == 0, n == nkt - 1, [Vt.b, pt.b], [O[bank].b],
                skip=True)
            _mm(cx, R[bank].t[:, o0:o0 + w], ones.t[:], pt.t[:, a:b], n == 0, n == nkt - 1, [ones.b, pt.b], [R[bank].b],
                skip=True)

    prev = None
    for n, kt in enumerate(key_tiles):
        pt, groups = stage1(n, kt)
        if prev is not None:
            stage2(*prev)
        prev = (n, pt, groups)
    stage2(*prev)


def build_B(dbg=False):
    cx = Ctx()
    P = cx.P
    xT = cx.dram("xT", [DC, 128, TL], F32)
    qT = cx.dram("qT", [NQC, 128, TL], BF16)
    kT = cx.dram("kT", [NKC, 128, 8192], BF16)
    vG = cx.dram("vG", [16, 128, NKT, 128], BF16)
    iwd = cx.dram("iw", [TL, 8], F32)
    pcd = cx.dram("pc", [128, 4], F32)
    wOd = cx.dram("wO", [DC, 128, DC, 128], F32)
    dld = cx.dram("dl", [1, 256], F32)
    sld = cx.dram("sl", [128, 1], F32)
    cmd = cx.dram("cm", [4, 128, 128], F32)
    gcd = cx.dram("gc", [128, 2 * DC + 2], F32)
    memd = cx.dram("memT", [DC, 128, 256], F32)
    wqd = cx.dram("wq", [4, 128, DC, 128], F32)
    wkd = cx.dram("wk", [4, 128, DC, 128], F32)
    wvd = cx.dram("wv", [4, 128, DC, 128], F32)
    wcod = cx.dram("wco", [DC, 128, 4, 128], F32)
    x2T = cx.dram("x2T", [DC, 128, TL], F32, out=True)

    pc = cx.sb([128, 4], F32, "pc")
    cx.load(pc.t[:], pcd, pc.b)
    cm = cx.sb([128, 4, 128], F32, "cm")
    for k in range(4):
        cx.load(cm.t[:, k, :], cmd[k], cm.b)
    epsb = cx.sb([128, 1], F32, "epsb")
    _memset(cx, "dve", epsb.t[:], EPS, [epsb.b])
    dkq = cx.sb([128, 128], F32, "dkq")
    P.op("pool", lambda e: e.iota(dkq.t[:], [[-1, 128]], base=0, channel_multiplier=1,
                                  allow_small_or_imprecise_dtypes=True), [], [dkq.b])
    ident = cx.sb([128, 128], BF16, "ident")
    _ts(cx, "dve", ident.t[:], dkq.t[:], 0.0, None, ALU.is_equal, None, [dkq.b], [ident.b])
    identf = cx.sb([128, 128], F32, "identf")
    _ts(cx, "dve", identf.t[:], dkq.t[:], 0.0, None, ALU.is_equal, None, [dkq.b], [identf.b])
    ones = cx.sb([128, 128], BF16, "ones")
    _memset(cx, "dve", ones.t[:], 1.0, [ones.b])
    thrj = cx.sb([128, 8], F32, "thrj")
    for j in range(8):
        _ts(cx, "dve", thrj.t[:, j:j + 1], pc.t[:, 0:1], 128.0, -128.0 * j, ALU.mult, ALU.add, [pc.b], [thrj.b])
    Mj = cx.sb([128, 8, 128], BF16, "Mj")
    for j in range(8):
        _ts(cx, "dve", Mj.t[:, j, :], dkq.t[:], thrj.t[:, j:j + 1], None, ALU.is_le, None, [dkq.b, thrj.b], [Mj.b])
    iwt = cx.sb([128, 8, 8], F32, "iwt")
    cx.load(iwt.t[:], iwd.rearrange("(i p) h -> p i h", p=128), iwt.b)
    dl = cx.sb([128, 256], F32, "dl")
    cx.load(dl.t[:], dld.partition_broadcast(128), dl.b)
    lam = cx.sb([128, 4], F32, "lam")
    dlp = cx.sb([128, 2, 64], F32, "dlp")
    _tt(cx, "dve", dlp.t[:, 0, :], dl.t[:, 0:64], dl.t[:, 64:128], ALU.mult, [dl.b], [dlp.b])
    _tt(cx, "dve", dlp.t[:, 1, :], dl.t[:, 128:192], dl.t[:, 192:256], ALU.mult, [dl.b], [dlp.b])
    P.op("dve", lambda e: e.reduce_sum(out=lam.t[:, 0:2], in_=dlp.t[:], axis=AX.X), [dlp.b], [lam.b])
    _act(cx, lam.t[:, 0:2], lam.t[:, 0:2], AF.Exp, [lam.b], [lam.b])
    _tt(cx, "dve", lam.t[:, 2:3], lam.t[:, 0:1], lam.t[:, 1:2], ALU.subtract, [lam.b], [lam.b])
    _ts(cx, "dve", lam.t[:, 3:4], lam.t[:, 2:3], pc.t[:, 2:3], -1.0, ALU.add, ALU.mult, [lam.b, pc.b], [lam.b])
    sl = cx.sb([128, 1], F32, "sl")
    cx.load(sl.t[:], sld, sl.b)
    slg = cx.sb([128, 1], F32, "slg")
    _tt(cx, "dve", slg.t[:], sl.t[:], pc.t[:, 3:4], ALU.mult, [sl.b, pc.b], [slg.b])
    gc = cx.sb([128, 2 * DC + 2], F32, "gc")
    cx.load(gc.t[:], gcd, gc.b)

    mixedT = cx.sb([128, DC, TL], BF16, "mixedT")
    outer = Scope(cx)
    outer.__enter__()
    maskT = cx.sb([128, NSLOT, 128], BF16, "maskT")

    with Scope(cx):
        pss = cx.ring(4, [128, 512], F32, "pss", psum=True)
        pst = cx.ring(2, [128, 4, 128], BF16, "pst", psum=True)
        MTj = cx.sb([128, 8, 128], F32, "MTj")
        NEGj = cx.sb([128, 8, 128], F32, "NEGj")
        POSj = cx.sb([128, 8, 128], F32, "POSj")
        for j in range(8):
            _ts(cx, "dve", MTj.t[:, j, :], dkq.t[:], -1.0, thrj.t[:, j:j + 1], ALU.mult, ALU.is_le, [dkq.b, thrj.b], [MTj.b])
        _ts(cx, "dve", NEGj.t[:], MTj.t[:], -1.0, BIG, ALU.add, ALU.mult, [MTj.b], [NEGj.b])
        _ts(cx, "dve", POSj.t[:], NEGj.t[:], -1.0, None, ALU.mult, None, [NEGj.b], [POSj.b])
        kdup = cx.sb([128, 8192], BF16, "kdup")
        cx.load(kdup.t[0:64, :], kT[KCH["ik"], 0:64, :], kdup.b)
        cx.load(kdup.t[64:128, :], kT[KCH["ik"], 0:64, :], kdup.b, eng="pool")
        qiT = cx.sb([128, 4, TL], BF16, "qiT")
        for k in range(4):
            cx.load(qiT.t[:, k, :], qT[QCH["iq"] + k], qiT.b)
        Isc = cx.sb([128, 8192], F32, "Isc")
        msk = cx.sb([128, 8192], BF16, "msk")
        rl_r = cx.ring(3, [128, 512], F32, "rl")
        st_r = cx.ring(2, [128, 8], F32, "bst")
        dg_r = cx.ring(1, [128, 1024], F32, "dg")
        cnt_r = cx.ring(2, [128, NBIS], F32, "cnt")
        for i in range(8):
            nk = 8 * i + 8
            nkeys = nk * 128
            for cg in range(nk // 4):
                ks = slice(cg * 512, (cg + 1) * 512)
                for h in range(8):
                    rows = slice(64 * (h % 2), 64 * (h % 2) + 64)
                    ps = pss.next()
                    _mm(cx, ps.t[:], qiT.t[rows, h // 2, i * 128:(i + 1) * 128], kdup.t[rows, ks], True, True,
                        [qiT.b, kdup.b], [ps.b])
                    rl = rl_r.next()
                    _act(cx, rl.t[:], ps.t[:], AF.Relu, [ps.b], [rl.b])
                    if h == 0:
                        _ts(cx, "dve", Isc.t[:, ks], rl.t[:], iwt.t[:, i, 0:1], None, ALU.mult, None, [rl.b, iwt.b], [Isc.b])
                    else:
                        _stt(cx, Isc.t[:, ks], rl.t[:], iwt.t[:, i, h:h + 1], Isc.t[:, ks], ALU.mult, ALU.add,
                             [rl.b, iwt.b, Isc.b], [Isc.b])
            dsl = slice(nkeys - 1024, nkeys)
            dg = dg_r.next()
            _tt(cx, "dve", dg.t[:], Isc.t[:, dsl], MTj.t[:].rearrange("p j k -> p (j k)"), ALU.mult, [Isc.b, MTj.b], [dg.b])
            _tt(cx, "dve", Isc.t[:, dsl], dg.t[:], NEGj.t[:].rearrange("p j k -> p (j k)"), ALU.add, [dg.b, NEGj.b], [Isc.b])
            _tt(cx, "dve", dg.t[:], dg.t[:], POSj.t[:].rearrange("p j k -> p (j k)"), ALU.add, [dg.b, POSj.b], [dg.b])
            bs = st_r.next()
            P.op("dve", lambda e, bs=bs, nkeys=nkeys: e.tensor_reduce(out=bs.t[:, 1:2], in_=Isc.t[:, 0:nkeys], axis=AX.X,
                                                                     op=ALU.max), [Isc.b], [bs.b])
            P.op("dve", lambda e, bs=bs, dg=dg: e.tensor_reduce(out=bs.t[:, 0:1], in_=dg.t[:], axis=AX.X, op=ALU.min),
                 [dg.b], [bs.b])
            if i > 0:
                P.op("dve", lambda e, bs=bs, nkeys=nkeys: e.tensor_reduce(out=bs.t[:, 5:6], in_=Isc.t[:, 0:nkeys - 1024],
                                                                         axis=AX.X, op=ALU.min), [Isc.b], [bs.b])
                _tt(cx, "dve", bs.t[:, 0:1], bs.t[:, 0:1], bs.t[:, 5:6], ALU.min, [bs.b], [bs.b])
            _ts(cx, "dve", bs.t[:, 1:2], bs.t[:, 1:2], 1.0, None, ALU.add, None, [bs.b], [bs.b])
            _tt(cx, "dve", bs.t[:, 1:2], bs.t[:, 1:2], bs.t[:, 0:1], ALU.subtract, [bs.b], [bs.b])
            cn = cnt_r.next()
            _memset(cx, "dve", cn.t[:], 0.0, [cn.b])
            for it in range(NBIS):
                _ts(cx, "dve", bs.t[:, 1:2], bs.t[:, 1:2], 0.5, None, ALU.mult, None, [bs.b], [bs.b])
                _tt(cx, "dve", bs.t[:, 2:3], bs.t[:, 0:1], bs.t[:, 1:2], ALU.add, [bs.b], [bs.b])
                _ts(cx, "dve", msk.t[:, 0:nkeys], Isc.t[:, 0:nkeys], bs.t[:, 2:3], 0.0, ALU.is_ge, ALU.add, [Isc.b, bs.b, cn.b],
                    [msk.b, cn.b], accum_out=cn.t[:, it:it + 1])
                _stt(cx, bs.t[:, 5:6], cn.t[:, it:it + 1], 256.0, bs.t[:, 1:2], ALU.is_ge, ALU.mult, [cn.b, bs.b], [bs.b])
                _tt(cx, "dve", bs.t[:, 0:1], bs.t[:, 0:1], bs.t[:, 5:6], ALU.add, [bs.b], [bs.b])
            _ts(cx, "dve", msk.t[:, 0:nkeys], Isc.t[:, 0:nkeys], bs.t[:, 0:1], None, ALU.is_ge, None, [Isc.b, bs.b], [msk.b])
            for g in range(nk // 4):
                pT = pst.next()
                for jj in range(4):
                    kt = g * 4 + jj
                    _tr(cx, pT.t[:, jj, :], msk.t[:, kt * 128:(kt + 1) * 128], ident.t[:], [msk.b, ident.b], [pT.b])
                for jj in range(4):
                    kt = g * 4 + jj
                    slot = OFFK[kt] + (i - kt // 8)
                    _cp(cx, "act" if jj % 2 else "dve", maskT.t[:, slot, :], pT.t[:, jj, :], [pT.b], [maskT.b])

    with Scope(cx):
        env = dict(pss=cx.ring(4, [128, 512], F32, "pss", psum=True), ptr=cx.ring(3, [128, TL], BF16, "pt"),
                   ones=ones, Mj=Mj)
        O = [cx.ps([128, 512], F32, "O0"), cx.ps([128, 512], F32, "O1")]
        R = [cx.ps([128, 512], F32, "R0"), cx.ps([128, 512], F32, "R1")]
        KT_r = cx.ring(2, [128, 8192], BF16, "KT")
        V_r = cx.ring(2, [128, NKT, 128], BF16, "V")
        QT_r = cx.ring(2, [128, TL], BF16, "QT")
        rinv_r = cx.ring(1, [128, TL], F32, "rinv")
        o1 = cx.sb([128, TL], F32, "o1")
        o2 = cx.sb([128, TL], F32, "o2")
        sqd = rinv_r.items[0]
        nidx = cx.sb([128, 32], F32, "nidx")
        P.op("pool", lambda e: e.iota(nidx.t[:], [[1, 32]], base=0, channel_multiplier=0,
                                      allow_small_or_imprecise_dtypes=True), [], [nidx.b])
        curv = cx.sb([128, 8], F32, "curv")
        for i in range(8):
            _ts(cx, "dve", curv.t[:, i:i + 1], pc.t[:, 1:2], 4.0 * i, None, ALU.add, None, [pc.b], [curv.b])
        selrows = ident
        biasT = cx.sb([32, TL], BF16, "biasT")
        kb = cx.sb([128, 32], F32, "kb")
        qf_r = cx.ring(2, [128, 128], F32, "qf")
        gt = cx.sb([128, 6, 32], F32, "gt")
        m8 = cx.sb([128, 8], F32, "m8")
        all_kt = list(range(NKT))

        def load_head(kc, krow_all, qc, vh):
            KT = KT_r.next()
            cx.load(KT.t[:], kT[kc], KT.b, eng="sp")
            Vt = V_r.next()
            cx.load(Vt.t[:], vG[vh], Vt.b, eng="sp")
            QT = QT_r.next()
            cx.load(QT.t[:], qT[qc], QT.b, eng="sp")
            return KT, Vt, QT

        def finalize(dst_ap, dst_b, to_f32_tb=None):
            ri = rinv_r.next()
            for bk in range(2):
                sl_ = slice(bk * 512, (bk + 1) * 512)
                _recip(cx, ri.t[:, sl_], R[bk].t[:], [R[bk].b], [ri.b])
                if to_f32_tb is None:
                    _tt(cx, "dve", dst_ap[:, sl_], O[bk].t[:], ri.t[:, sl_], ALU.mult, [O[bk].b, ri.b], [dst_b])
                else:
                    _tt(cx, "dve", to_f32_tb.t[:, sl_], O[bk].t[:], ri.t[:, sl_], ALU.mult, [O[bk].b, ri.b], [to_f32_tb.b])

        for h in range(6):
            KT, Vt, QT = load_head(KCH["sk"] + h, None, QCH["sq"] + h, 10 + h)
            attn_pass(cx, env, KT, slice(0, 128), QT, slice(0, 128), Vt, 128 ** -0.5, all_kt, "dsa", O, R, maskT=maskT)
            finalize(mixedT.t[:, 10 + h, :], mixedT.b)
        for h in range(6):
            KT, Vt, QT = load_head(KCH["mk"] + h, None, QCH["mq"] + h, h)
            P.op("dve", lambda e, KT=KT: e.tensor_reduce(out=kb.t[:], in_=KT.t[:].rearrange("p (n k) -> p n k", k=256),
                                                         axis=AX.X, op=ALU.add), [KT.b], [kb.b])
            for i in range(8):
                qf = qf_r.next()
                _cp(cx, "pool", qf.t[:], QT.t[:, i * 128:(i + 1) * 128], [QT.b], [qf.b])
                ps = env["pss"].next()
                _mm(cx, ps.t[:, 0:32], qf.t[:], kb.t[:], True, True, [qf.b, kb.b], [ps.b])
                _ts(cx, "dve", gt.t[:, 0, :], nidx.t[:], curv.t[:, i:i + 1], None, ALU.is_lt, None, [nidx.b, curv.b], [gt.b])
                _ts(cx, "dve", gt.t[:, 1, :], nidx.t[:], curv.t[:, i:i + 1], None, ALU.is_equal, None, [nidx.b, curv.b], [gt.b])
                _tt(cx, "dve", gt.t[:, 2, :], ps.t[:, 0:32], gt.t[:, 0, :], ALU.mult, [ps.b, gt.b], [gt.b])
                _ts(cx, "dve", gt.t[:, 3, :], gt.t[:, 0, :], -1.0, BIG, ALU.add, ALU.mult, [gt.b], [gt.b])
                _tt(cx, "dve", gt.t[:, 2, :], gt.t[:, 2, :], gt.t[:, 3, :], ALU.add, [gt.b], [gt.b])
                P.op("dve", lambda e: e.max(out=m8.t[:], in_=gt.t[:, 2, :]), [gt.b], [m8.b])
                _ts(cx, "dve", gt.t[:, 4, :], gt.t[:, 2, :], m8.t[:, 2:3], None, ALU.is_ge, None, [gt.b, m8.b], [gt.b])
                _tt(cx, "dve", gt.t[:, 4, :], gt.t[:, 4, :], gt.t[:, 0, :], ALU.mult, [gt.b], [gt.b])
                _tt(cx, "dve", gt.t[:, 4, :], gt.t[:, 4, :], gt.t[:, 1, :], ALU.add, [gt.b], [gt.b])
                _ts(cx, "dve", gt.t[:, 5, :], gt.t[:, 4, :], -1.0, BIGB, ALU.add, ALU.mult, [gt.b], [gt.b])
                ps2 = env["pss"].next()
                _mm(cx, ps2.t[0:32, 0:128], gt.t[:, 5, :], identf.t[:], True, True, [gt.b, identf.b], [ps2.b])
                _cp(cx, "act", biasT.t[:, i * 128:(i + 1) * 128], ps2.t[0:32, 0:128], [ps2.b], [biasT.b])
            attn_pass(cx, env, KT, slice(0, 128), QT, slice(0, 128), Vt, 128 ** -0.5, all_kt, "moba", O, R, biasT=biasT,
                      selrows=selrows)
            finalize(mixedT.t[:, h, :], mixedT.b)
        for h in range(4):
            KT, Vt, QT = load_head(KCH["dk"] + h, None, QCH["dq"] + h, 6 + h)
            attn_pass(cx, env, KT, slice(0, 64), QT, slice(0, 64), Vt, 64 ** -0.5, all_kt, "causal", O, R)
            finalize(None, None, to_f32_tb=o1)
            attn_pass(cx, env, KT, slice(64, 128), QT, slice(64, 128), Vt, 64 ** -0.5, all_kt, "causal", O, R)
            finalize(None, None, to_f32_tb=o2)
            _stt(cx, o1.t[:], o2.t[:], lam.t[:, 3:4], o1.t[:], ALU.mult, ALU.add, [o2.b, lam.b, o1.b], [o1.b])
            _act(cx, sqd.t[:], o1.t[:], AF.Square, [o1.b], [sqd.b])
            for bk in range(2):
                sl_ = slice(bk * 512, (bk + 1) * 512)
                ps = env["pss"].next()
                _mm(cx, ps.t[:], cm.t[:, 2, :], sqd.t[:, sl_], True, True, [cm.b, sqd.b], [ps.b])
                _act(cx, o2.t[:, sl_], ps.t[:], AF.Sqrt, [ps.b, epsb.b], [o2.b], bias=epsb.t[:, 0:1])
            _recip(cx, o2.t[:], o2.t[:], [o2.b], [o2.b])
            _stt(cx, mixedT.t[:, 6 + h, :], o1.t[:], slg.t[:, 0:1], o2.t[:], ALU.mult, ALU.mult, [o1.b, slg.b, o2.b], [mixedT.b])

    outer.__exit__(None, None, None)
    if dbg:
        dM = cx.dram("dbgM", [128, DC, TL], BF16, out=True)
        cx.store(dM, mixedT.t[:], mixedT.b)

    with Scope(cx):
        pss = cx.ring(4, [128, 512], F32, "pss", psum=True)
        env = dict(pss=pss, ptr=cx.ring(2, [128, TL], BF16, "pt"), ones=ones, Mj=Mj)
        O = [cx.ps([128, 512], F32, "O0"), cx.ps([128, 512], F32, "O1")]
        R = [cx.ps([128, 512], F32, "R0"), cx.ps([128, 512], F32, "R1")]
        x1T = cx.sb([128, DC, TL], F32, "x1T")
        h2T = cx.sb([128, DC, TL], BF16, "h2T")
        wst = cx.ring(1, [128, DC, 128], F32, "wst")
        wbf = cx.ring(2, [128, DC, 128], BF16, "wbf")
        xr = cx.ring(2, [128, TL], F32, "xr")
        sq_r = cx.ring(1, [128, TL], F32, "sq")
        nld = 0
        for dmc in range(DC):
            ws = wst.next()
            cx.load(ws.t[:], wOd[dmc], ws.b, eng="sp" if nld % 2 == 0 else "pool")
            wb = wbf.next()
            _cp(cx, "pool" if nld % 2 == 0 else "dve", wb.t[:], ws.t[:], [ws.b], [wb.b])
            nld += 1
            x_ = xr.next()
            cx.load(x_.t[:], xT[dmc], x_.b)
            for hf in range(2):
                sl_ = slice(hf * 512, (hf + 1) * 512)
                ps = pss.next()
                for hc in range(DC):
                    _mm(cx, ps.t[:], wb.t[:, hc, :], mixedT.t[:, hc, sl_], hc == 0, hc == DC - 1, [wb.b, mixedT.b], [ps.b])
                _tt(cx, "dve", x1T.t[:, dmc, sl_], ps.t[:], x_.t[:, sl_], ALU.add, [ps.b, x_.b], [x1T.b])
            sq = sq_r.next()
            _act(cx, sq.t[:], x1T.t[:, dmc, :], AF.Square, [x1T.b], [sq.b])
            for hf in range(2):
                _mm(cx, R[hf].t[:], cm.t[:, 2, :], sq.t[:, hf * 512:(hf + 1) * 512], dmc == 0, dmc == DC - 1, [cm.b, sq.b],
                    [R[hf].b])
        rstd = cx.sb([128, TL], F32, "rstd")
        for hf in range(2):
            _act(cx, rstd.t[:, hf * 512:(hf + 1) * 512], R[hf].t[:], AF.Sqrt, [R[hf].b, epsb.b], [rstd.b], scale=1.0 / DC,
                 bias=epsb.t[:, 0:1])
        _recip(cx, rstd.t[:], rstd.t[:], [rstd.b], [rstd.b])
        for dc in range(DC):
            _stt(cx, h2T.t[:, dc, :], x1T.t[:, dc, :], gc.t[:, dc:dc + 1], rstd.t[:], ALU.mult, ALU.mult,
                 [x1T.b, gc.b, rstd.b], [h2T.b])
        memr = cx.ring(2, [128, 256], F32, "memr")
        msqr = cx.ring(2, [128, 256], F32, "msqr")
        psm = pss.next()
        for dc in range(DC):
            mf = memr.next()
            cx.load(mf.t[:], memd[dc], mf.b)
            mq_ = msqr.next()
            _act(cx, mq_.t[:], mf.t[:], AF.Square, [mf.b], [mq_.b])
            _mm(cx, psm.t[:, 0:256], cm.t[:, 2, :], mq_.t[:], dc == 0, dc == DC - 1, [cm.b, mq_.b], [psm.b])
        rm = cx.sb([128, 256], F32, "rm")
        _act(cx, rm.t[:], psm.t[:, 0:256], AF.Sqrt, [psm.b, epsb.b], [rm.b], scale=1.0 / DC, bias=epsb.t[:, 0:1])
        _recip(cx, rm.t[:], rm.t[:], [rm.b], [rm.b])
        mT = cx.sb([128, DC, 256], BF16, "mT")
        for dc in range(DC):
            mf = memr.next()
            cx.load(mf.t[:], memd[dc], mf.b)
            _stt(cx, mT.t[:, dc, :], mf.t[:], gc.t[:, DC + dc:DC + dc + 1], rm.t[:], ALU.mult, ALU.mult,
                 [mf.b, gc.b, rm.b], [mT.b])
        ckT = cx.sb([128, 4, 256], BF16, "ckT")
        cv = cx.sb([128, 4, 2, 128], BF16, "cv")
        cqT = TB(mixedT.t[:, 0:4, :], "cqT")
        coT = TB(mixedT.t[:, 4:8, :], "coT")
        for _tb in (cqT, coT):
            _tb.b.rs = list(mixedT.b.rs) + ([mixedT.b.w] if mixedT.b.w is not None else [])
        tmpf = cx.ring(1, [128, 512], F32, "tmpf")
        tmps = cx.ring(1, [128, 512], F32, "tmps")
        tmpr = cx.ring(1, [128, 512], F32, "tmpr")

        def headnorm(ps, w, gcol, dst_ap, dst_b):
            xf = tmpf.next()
            _cp(cx, "act", xf.t[:, 0:w], ps.t[:, 0:w], [ps.b], [xf.b])
            s2 = tmps.next()
            _act(cx, s2.t[:, 0:w], xf.t[:, 0:w], AF.Square, [xf.b], [s2.b])
            pm = pss.next()
            _mm(cx, pm.t[:, 0:w], cm.t[:, 2, :], s2.t[:, 0:w], True, True, [cm.b, s2.b], [pm.b])
            rr = tmpr.next()
            _act(cx, rr.t[:, 0:w], pm.t[:, 0:w], AF.Sqrt, [pm.b, epsb.b], [rr.b], bias=epsb.t[:, 0:1])
            _recip(cx, rr.t[:, 0:w], rr.t[:, 0:w], [rr.b], [rr.b])
            _stt(cx, dst_ap, xf.t[:, 0:w], gc.t[:, gcol:gcol + 1], rr.t[:, 0:w], ALU.mult, ALU.mult, [xf.b, gc.b, rr.b], [dst_b])

        for h in range(4):
            ws = wst.next()
            cx.load(ws.t[:], wkd[h], ws.b)
            wb = wbf.next()
            _cp(cx, "pool", wb.t[:], ws.t[:], [ws.b], [wb.b])
            ps = pss.next()
            for dc in range(DC):
                _mm(cx, ps.t[:, 0:256], wb.t[:, dc, :], mT.t[:, dc, :], dc == 0, dc == DC - 1, [wb.b, mT.b], [ps.b])
            headnorm(ps, 256, 2 * DC + 1, ckT.t[:, h, :], ckT.b)
            ws = wst.next()
            cx.load(ws.t[:], wvd[h], ws.b)
            wb = wbf.next()
            _cp(cx, "dve", wb.t[:], ws.t[:], [ws.b], [wb.b])
            for mt in range(2):
                ps = pss.next()
                for dc in range(DC):
                    _mm(cx, ps.t[:, 0:128], mT.t[:, dc, mt * 128:(mt + 1) * 128], wb.t[:, dc, :], dc == 0, dc == DC - 1,
                        [wb.b, mT.b], [ps.b])
                _cp(cx, "act", cv.t[:, h, mt, :], ps.t[:, 0:128], [ps.b], [cv.b])
            ws = wst.next()
            cx.load(ws.t[:], wqd[h], ws.b)
            wb = wbf.next()
            _cp(cx, "pool", wb.t[:], ws.t[:], [ws.b], [wb.b])
            for hf in range(2):
                sl_ = slice(hf * 512, (hf + 1) * 512)
                ps = pss.next()
                for dc in range(DC):
                    _mm(cx, ps.t[:], wb.t[:, dc, :], h2T.t[:, dc, sl_], dc == 0, dc == DC - 1, [wb.b, h2T.b], [ps.b])
                headnorm(ps, 512, 2 * DC, cqT.t[:, h, sl_], cqT.b)
        for h in range(4):
            KTv = TB(ckT.t[:, h, :])
            KTv.b = ckT.b
            QTv = TB(cqT.t[:, h, :])
            QTv.b = cqT.b
            Vv = TB(cv.t[:, h, :, :])
            Vv.b = cv.b
            attn_pass(cx, env, KTv, slice(0, 128), QTv, slice(0, 128), Vv, 128 ** -0.5, [0, 1], "none", O, R, suffix=False)
            ri = tmpf.next()
            ri2 = tmps.next()
            for bk, rt in ((0, ri), (1, ri2)):
                sl_ = slice(bk * 512, (bk + 1) * 512)
                _recip(cx, rt.t[:], R[bk].t[:], [R[bk].b], [rt.b])
                _tt(cx, "dve", coT.t[:, h, sl_], O[bk].t[:], rt.t[:], ALU.mult, [O[bk].b, rt.b], [coT.b])
        wcs = cx.ring(2, [128, 4, 128], F32, "wcs")
        wcb = cx.ring(2, [128, 4, 128], BF16, "wcb")
        xo_r = xr
        for dmc in range(DC):
            ws = wcs.next()
            cx.load(ws.t[:], wcod[dmc], ws.b)
            wb = wcb.next()
            _cp(cx, "pool", wb.t[:], ws.t[:], [ws.b], [wb.b])
            xo = xo_r.next()
            for hf in range(2):
                sl_ = slice(hf * 512, (hf + 1) * 512)
                ps = pss.next()
                for hc in range(4):
                    _mm(cx, ps.t[:], wb.t[:, hc, :], coT.t[:, hc, sl_], hc == 0, hc == 3, [wb.b, coT.b], [ps.b])
                _tt(cx, "dve", xo.t[:, sl_], ps.t[:], x1T.t[:, dmc, sl_], ALU.add, [ps.b, x1T.b], [xo.b])
            cx.store(x2T[dmc], xo.t[:], xo.b)
    return cx.finish()


def prep_B_weights(inp, l):
    import math
    wO = np.stack([wtile(inp["w_out"][l][:, 128 * k:128 * (k + 1)]) for k in range(DC)])
    gc = np.zeros((128, 2 * DC + 2), np.float32)
    gc[:, 0:DC] = inp["cross_norm"][l].reshape(DC, 128).T
    gc[:, DC:2 * DC] = inp["mem_norm"][l].reshape(DC, 128).T
    gc[:, 2 * DC] = inp["cross_qk_gain"][l, 0]
    gc[:, 2 * DC + 1] = inp["cross_qk_gain"][l, 1]
    wq = np.stack([wtile(inp["cross_wq"][l][:, 128 * h:128 * (h + 1)]) for h in range(4)])
    wk = np.stack([wtile(inp["cross_wkv"][l][:, 128 * h:128 * (h + 1)]) for h in range(4)])
    wv = np.stack([wtile(inp["cross_wkv"][l][:, 512 + 128 * h:512 + 128 * (h + 1)]) for h in range(4)])
    wo = inp["cross_wo"][l]
    wco = np.ascontiguousarray(wo.reshape(4, 128, DC, 128).transpose(2, 1, 0, 3))
    rc, cm = rope_consts()
    lam_init = 0.8 - 0.6 * math.exp(-0.3 * l)
    return dict(wO=wO, gc=gc, wq=wq, wk=wk, wv=wv, wco=wco, cm=cm, memT=fm(inp["mem"][0]),
                dl=np.ascontiguousarray(inp["diff_lambda"][l].reshape(1, 256)),
                sl=np.ascontiguousarray(inp["diff_subln"][l].reshape(128, 1))), lam_init


def glue_A_to_B(resA):
    kT = np.zeros((NKC, 128, 8192), ml_dtypes.bfloat16)
    vfull = np.zeros((8192, 2048), ml_dtypes.bfloat16)
    qTs = []
    ksrc = ([FIDX["mk"] + k for k in range(6)] + [FIDX["dk"] + k for k in range(4)] + [FIDX["sk"] + k for k in range(6)]
            + [FIDX["ik"]])
    qsrc = ([FIDX["mq"] + k for k in range(6)] + [FIDX["dq"] + k for k in range(4)] + [FIDX["sq"] + k for k in range(6)]
            + [FIDX["iq"] + k for k in range(4)])
    for c in range(NCORES):
        ti = tok_index(c)
        fT = np.asarray(resA[c]["fT"])
        kT[:, :, ti] = fT[ksrc]
        vfull[ti] = np.asarray(resA[c]["vtok"])
        qTs.append(np.ascontiguousarray(fT[qsrc]))
    vG = np.ascontiguousarray(vfull.reshape(NKT, 128, 16, 128).transpose(2, 1, 0, 3))
    return qTs, kT, vG


_PROGS = {}


def _prog(name):
    if name not in _PROGS:
        _PROGS[name] = {"A": build_A, "B": build_B, "C": build_C}[name]()
    return _PROGS[name]


def kernel(**inputs):
    import math
    inp = {k: np.asarray(v) for k, v in inputs.items()}
    cores = list(range(NCORES))
    toks = [tok_index(c) for c in cores]
    x0 = inp["x"][0]
    xT_loc = [fm(x0[toks[c]]) for c in cores]
    pos_loc = [np.ascontiguousarray(inp["positions"][0][toks[c]].reshape(1, TL)).astype(np.int32) for c in cores]
    for l in range(2):
        wa = prep_A_weights(inp, l)
        maps = []
        for c in cores:
            m = dict(wa)
            m["xT"] = xT_loc[c]
            m["pos"] = pos_loc[c]
            maps.append(m)
        resA = run_bass_kernel_spmd(_prog("A"), maps, core_ids=cores).results
        del maps, wa
        qTs, kT, vG = glue_A_to_B(resA)
        wb, lam_init = prep_B_weights(inp, l)
        maps = []
        for c in cores:
            m = dict(wb)
            m["xT"] = xT_loc[c]
            m["qT"] = qTs[c]
            m["kT"] = kT
            m["vG"] = vG
            m["iw"] = np.asarray(resA[c]["iwo"])
            pc = np.zeros((128, 4), np.float32)
            pc[:, 0] = c
            pc[:, 1] = c // 2
            pc[:, 2] = lam_init
            pc[:, 3] = 1.0 - lam_init
            m["pc"] = pc
            maps.append(m)
        resB = run_bass_kernel_spmd(_prog("B"), maps, core_ids=cores).results
        del maps, wb, qTs, kT, vG, resA
        xfull = np.zeros((D_MODEL, 8192), np.float32)
        for c in cores:
            xfull[:, toks[c]] = np.asarray(resB[c]["x2T"]).reshape(D_MODEL, TL)
        wc = prep_C_weights(inp, l)
        maps = []
        for c in cores:
            m = dict(wc)
            m["x2h"] = halo_cols(xfull, c)
            maps.append(m)
        resC = run_bass_kernel_spmd(_prog("C"), maps, core_ids=cores).results
        del maps, wc, resB
        xT_loc = [np.ascontiguousarray(np.asarray(resC[c]["x3T"])) for c in cores]
    out = np.zeros((1, 8192, D_MODEL), np.float32)
    for c in cores:
        out[0, toks[c]] = xT_loc[c].reshape(D_MODEL, TL).T
    return out
```
